# Optimizing a Trainium2 kernel written in Bass

```python
import math
import jax, jax.numpy as jnp
from jax import lax
import numpy as np

D_MODEL = 2048
BATCH = 8
SEQ = 4096
DEPTH = 1
DEC_BATCH = 16
DEC_SEQ = 32
PAST_LEN = 4096

CHUNK = 64
MIX_WIDTH = D_MODEL
FOX_WIDTH = MIX_WIDTH // 2
FOX_HEADS = 8
FOX_HEAD_DIM = FOX_WIDTH // FOX_HEADS
GMLP_WIDTH = MIX_WIDTH - FOX_WIDTH
GMLP_GROUPS = 4
GMLP_GROUP_DIM = GMLP_WIDTH // GMLP_GROUPS
GMLP_CHUNK = 2 * CHUNK
Q_BLOCK = 128
N_MEM = 256
MEM_HEADS = 4
MEM_HEAD_DIM = D_MODEL // MEM_HEADS
D_FF = 4 * D_MODEL
ALPHA = (2 * DEPTH) ** 0.25
BETA = (8 * DEPTH) ** -0.25
LN_EPS = 1e-5
IN_COLS = 3 * FOX_WIDTH + FOX_HEADS + 2 * GMLP_WIDTH
FOX_SCALE = FOX_HEAD_DIM ** -0.5
MEM_SCALE = MEM_HEAD_DIM ** -0.5

kernel_name = "fox_gmlp_hybrid_stream_step"


def layer_norm(x, g, b):
    xf = x.astype(jnp.float32)
    mu = jnp.mean(xf, axis=-1, keepdims=True)
    var = jnp.mean(jnp.square(xf - mu), axis=-1, keepdims=True)
    return ((xf - mu) * lax.rsqrt(var + LN_EPS) * g.astype(jnp.float32) + b.astype(jnp.float32)).astype(x.dtype)


def project_in(x, w_in, b_f, sgu_g, sgu_b):
    B, T, _ = x.shape
    z = x @ w_in
    o1, o2, o3 = FOX_WIDTH, 2 * FOX_WIDTH, 3 * FOX_WIDTH
    o4 = o3 + FOX_HEADS
    o5 = o4 + GMLP_WIDTH
    q = z[..., :o1].reshape(B, T, FOX_HEADS, FOX_HEAD_DIM)
    k = z[..., o1:o2].reshape(B, T, FOX_HEADS, FOX_HEAD_DIM)
    v = z[..., o2:o3].reshape(B, T, FOX_HEADS, FOX_HEAD_DIM)
    logf = jax.nn.log_sigmoid((z[..., o3:o4] + b_f).astype(jnp.float32))
    u = jax.nn.gelu(z[..., o4:o5])
    g = layer_norm(jax.nn.gelu(z[..., o5:]), sgu_g, sgu_b)
    return q, k, v, logf, u, g


def fox_block(q, k, v, c_q, c_k, q_pos, k_pos):
    s = jnp.einsum('bqhd,bkhd->bhqk', q, k).astype(jnp.float32) * FOX_SCALE
    s = s + (jnp.transpose(c_q, (0, 2, 1))[..., :, None] - jnp.transpose(c_k, (0, 2, 1))[..., None, :])
    mask = k_pos[None, :] <= q_pos[:, None]
    s = jnp.where(mask, s, -jnp.inf)
    p = jax.nn.softmax(s, axis=-1)
    return jnp.einsum('bhqk,bkhd->bqhd', p.astype(v.dtype), v)


def fox_prompt(q, k, v, logf):
    S = q.shape[1]
    c = jnp.cumsum(logf, axis=1)
    outs = []
    for i in range(S // Q_BLOCK):
        lo, hi = i * Q_BLOCK, (i + 1) * Q_BLOCK
        outs.append(fox_block(q[:, lo:hi], k[:, :hi], v[:, :hi], c[:, lo:hi], c[:, :hi],
                              jnp.arange(lo, hi), jnp.arange(hi)))
    return jnp.concatenate(outs, axis=1)


def fox_sample(q, k, v, logf, cache_k, cache_v, cache_logf):
    P, T = cache_k.shape[1], q.shape[1]
    k_all = jnp.concatenate([cache_k, k.astype(cache_k.dtype)], axis=1)
    v_all = jnp.concatenate([cache_v, v.astype(cache_v.dtype)], axis=1)
    c = jnp.cumsum(jnp.concatenate([cache_logf.astype(jnp.float32), logf], axis=1), axis=1)
    return fox_block(q, k_all, v_all, c[:, P:], c, jnp.arange(P, P + T), jnp.arange(P + T))


def sgu_prompt(u, g, w_s, b_s):
    B, S, _ = u.shape
    n = S // GMLP_CHUNK
    ws = w_s * jnp.tril(jnp.ones((GMLP_CHUNK, GMLP_CHUNK), w_s.dtype))
    gr = g.reshape(B, n, GMLP_CHUNK, GMLP_GROUPS, GMLP_GROUP_DIM)
    s = jnp.einsum('gts,bnsgc->bntgc', ws, gr) + jnp.transpose(b_s)[None, None, :, :, None]
    return u * s.reshape(B, S, GMLP_WIDTH)


def sgu_sample(u, g, w_s, b_s):
    B, T, _ = u.shape
    ws = (w_s * jnp.tril(jnp.ones((GMLP_CHUNK, GMLP_CHUNK), w_s.dtype)))[:, :T, :T]
    gr = g.reshape(B, T, GMLP_GROUPS, GMLP_GROUP_DIM)
    s = jnp.einsum('gts,bsgc->btgc', ws, gr) + jnp.transpose(b_s[:, :T])[None, :, :, None]
    return u * s.reshape(B, T, GMLP_WIDTH)


def mem_kv(mem, w_mk, w_mv):
    B, M, _ = mem.shape
    return ((mem @ w_mk).reshape(B, M, MEM_HEADS, MEM_HEAD_DIM),
            (mem @ w_mv).reshape(B, M, MEM_HEADS, MEM_HEAD_DIM))


def mem_attend(x, mk, mv, w_mq, w_mo):
    B, T, _ = x.shape
    q = (x @ w_mq).reshape(B, T, MEM_HEADS, MEM_HEAD_DIM)
    s = jnp.einsum('bqhd,bkhd->bhqk', q, mk.astype(q.dtype)).astype(jnp.float32) * MEM_SCALE
    p = jax.nn.softmax(s, axis=-1)
    o = jnp.einsum('bhqk,bkhd->bqhd', p.astype(q.dtype), mv.astype(q.dtype))
    return o.reshape(B, T, D_MODEL) @ w_mo


def ffn(x, w_up, w_down):
    return jnp.square(jax.nn.relu(x @ w_up)) @ w_down


def post_sublayers(h, mix, mk, mv, l, ln1_g, ln1_b, w_mq, w_mo, ln2_g, ln2_b, w_up, w_down, ln3_g, ln3_b):
    h = layer_norm(ALPHA * h + mix, ln1_g[l], ln1_b[l])
    h = layer_norm(ALPHA * h + mem_attend(h, mk, mv, w_mq[l], w_mo[l]), ln2_g[l], ln2_b[l])
    return layer_norm(ALPHA * h + ffn(h, w_up[l], w_down[l]), ln3_g[l], ln3_b[l])


def setup_inputs(seed: int = 0) -> dict:
    key = jax.random.key(seed)
    ks = iter(jax.random.split(key, 40))
    nrm = lambda shape, s=1.0: jax.random.normal(next(ks), shape, jnp.float32) * s
    L = DEPTH
    return {
        "x_prompt": nrm((BATCH, SEQ, D_MODEL)),
        "x_sample": nrm((DEC_BATCH, DEC_SEQ, D_MODEL)),
        "mem_prompt": nrm((BATCH, N_MEM, D_MODEL)),
        "cache_fox_k": nrm((L, DEC_BATCH, PAST_LEN, FOX_HEADS, FOX_HEAD_DIM)),
        "cache_fox_v": nrm((L, DEC_BATCH, PAST_LEN, FOX_HEADS, FOX_HEAD_DIM)),
        "cache_fox_logf": jax.nn.log_sigmoid(2.0 + nrm((L, DEC_BATCH, PAST_LEN, FOX_HEADS))),
        "cache_mem_k": nrm((L, DEC_BATCH, N_MEM, MEM_HEADS, MEM_HEAD_DIM)),
        "cache_mem_v": nrm((L, DEC_BATCH, N_MEM, MEM_HEADS, MEM_HEAD_DIM)),
        "w_in": nrm((L, D_MODEL, IN_COLS), D_MODEL ** -0.5),
        "b_f": 2.0 + nrm((L, FOX_HEADS), 0.1),
        "sgu_ln_g": 1.0 + nrm((L, GMLP_WIDTH), 0.01),
        "sgu_ln_b": nrm((L, GMLP_WIDTH), 0.01),
        "w_s": nrm((L, GMLP_GROUPS, GMLP_CHUNK, GMLP_CHUNK), GMLP_CHUNK ** -0.5),
        "b_s": 1.0 + nrm((L, GMLP_GROUPS, GMLP_CHUNK), 0.01),
        "w_out": nrm((L, MIX_WIDTH, D_MODEL), BETA * MIX_WIDTH ** -0.5),
        "ln1_g": 1.0 + nrm((L, D_MODEL), 0.01),
        "ln1_b": nrm((L, D_MODEL), 0.01),
        "w_mq": nrm((L, D_MODEL, D_MODEL), D_MODEL ** -0.5),
        "w_mk": nrm((L, D_MODEL, D_MODEL), D_MODEL ** -0.5),
        "w_mv": nrm((L, D_MODEL, D_MODEL), D_MODEL ** -0.5),
        "w_mo": nrm((L, D_MODEL, D_MODEL), BETA * D_MODEL ** -0.5),
        "ln2_g": 1.0 + nrm((L, D_MODEL), 0.01),
        "ln2_b": nrm((L, D_MODEL), 0.01),
        "w_up": nrm((L, D_MODEL, D_FF), D_MODEL ** -0.5),
        "w_down": nrm((L, D_FF, D_MODEL), BETA * D_FF ** -0.5),
        "ln3_g": 1.0 + nrm((L, D_MODEL), 0.01),
        "ln3_b": nrm((L, D_MODEL), 0.01),
    }


def reference(x_prompt, x_sample, mem_prompt, cache_fox_k, cache_fox_v, cache_fox_logf,
              cache_mem_k, cache_mem_v, w_in, b_f, sgu_ln_g, sgu_ln_b, w_s, b_s, w_out,
              ln1_g, ln1_b, w_mq, w_mk, w_mv, w_mo, ln2_g, ln2_b, w_up, w_down, ln3_g, ln3_b):
    hp, hs = x_prompt, x_sample
    fkp, fvp, flp, mkp, mvp = [], [], [], [], []
    fks, fvs, fls, gvs = [], [], [], []
    for l in range(DEPTH):
        B, S, _ = hp.shape
        q, k, v, logf, u, g = project_in(hp, w_in[l], b_f[l], sgu_ln_g[l], sgu_ln_b[l])
        fo = fox_prompt(q, k, v, logf).reshape(B, S, FOX_WIDTH)
        go = sgu_prompt(u, g, w_s[l], b_s[l])
        mix = jnp.concatenate([fo, go], axis=-1) @ w_out[l]
        mk, mv = mem_kv(mem_prompt, w_mk[l], w_mv[l])
        fkp.append(k); fvp.append(v); flp.append(logf); mkp.append(mk); mvp.append(mv)
        hp = post_sublayers(hp, mix, mk, mv, l, ln1_g, ln1_b, w_mq, w_mo, ln2_g, ln2_b,
                            w_up, w_down, ln3_g, ln3_b)
        Bs, T, _ = hs.shape
        q, k, v, logf, u, g = project_in(hs, w_in[l], b_f[l], sgu_ln_g[l], sgu_ln_b[l])
        fo = fox_sample(q, k, v, logf, cache_fox_k[l], cache_fox_v[l], cache_fox_logf[l]).reshape(Bs, T, FOX_WIDTH)
        go = sgu_sample(u, g, w_s[l], b_s[l])
        mix = jnp.concatenate([fo, go], axis=-1) @ w_out[l]
        fks.append(k); fvs.append(v); fls.append(logf); gvs.append(g)
        hs = post_sublayers(hs, mix, cache_mem_k[l], cache_mem_v[l], l, ln1_g, ln1_b, w_mq, w_mo,
                            ln2_g, ln2_b, w_up, w_down, ln3_g, ln3_b)
    return (hp, hs, jnp.stack(fkp), jnp.stack(fvp), jnp.stack(flp), jnp.stack(mkp), jnp.stack(mvp),
            jnp.stack(fks), jnp.stack(fvs), jnp.stack(fls), jnp.stack(gvs))
```

```python
import contextlib
import numpy as np
import concourse.bass as bass
import concourse.mybir as mybir
from concourse.bass_utils import run_bass_kernel_spmd

F32 = mybir.dt.float32
BF16 = mybir.dt.bfloat16
AF = mybir.ActivationFunctionType
ALU = mybir.AluOpType

D = 2048
SEQ = 4096
NTILE = 8
ALPHA = 2.0 ** 0.25
EPS = 1e-5
FOX_SCALE = 128.0 ** -0.5
MEM_SCALE = 512.0 ** -0.5
ENGMAP = {'pe': 'tensor', 'act': 'scalar', 'dve': 'vector', 'pool': 'gpsimd', 'sp': 'sync'}


class Prog:
    def __init__(self, nc, es):
        self.nc = nc
        self.es = es
        self.q = {e: [] for e in ENGMAP}
        self.cnt = {}
        self.sems = {}
        self.lastw = {}
        self.readers = {}
        self.seen = {e: {} for e in ENGMAP}
        self.bank_i = 0
        self.nops = 0

    def sem(self, name):
        if name not in self.sems:
            self.sems[name] = self.es.enter_context(self.nc.semaphore(name))
        return self.sems[name]

    def op(self, eng, fn, reads=(), writes=(), dma_sem=None):
        deps = {}

        def add(ev):
            if ev is None:
                return
            s, v, e = ev
            if s not in deps or deps[s][0] < v:
                deps[s] = (v, e)
        for r in reads:
            add(self.lastw.get(r))
        for w in writes:
            add(self.lastw.get(w))
            for s, (v, e) in self.readers.get(w, {}).items():
                add((s, v, e))
        waits = []
        for s, (v, e) in deps.items():
            if eng == 'pe' and e == 'pe':
                continue
            if self.seen[eng].get(s, 0) >= v:
                continue
            self.seen[eng][s] = v
            waits.append((self.sem(s), v))
        if dma_sem:
            semname, inc, pe = dma_sem, 16, 'dma'
        else:
            semname, inc, pe = 'E_' + eng, 1, eng
        self.cnt[semname] = self.cnt.get(semname, 0) + inc
        val = self.cnt[semname]
        ev = (semname, val, pe)
        for r in reads:
            self.readers.setdefault(r, {})[semname] = (val, pe)
        for w in writes:
            self.lastw[w] = ev
            self.readers[w] = {}
        sh = self.sem(semname)

        def emit(e):
            for s_, v_ in waits:
                e.wait_ge(s_, v_)
            fn(e).then_inc(sh, inc)
        self.q[eng].append(emit)
        self.nops += 1
        return ev

    def mm(self, out, lhsT, rhs, start, stop, reads, writes):
        return self.op('pe', lambda e: e.matmul(out, lhsT=lhsT, rhs=rhs, start=start, stop=stop), reads, writes)

    def tr(self, out, in_, ident, reads, writes):
        return self.op('pe', lambda e: e.transpose(out, in_, ident), reads, writes)

    def act(self, out, in_, func, reads, writes, bias=None, scale=None):
        kw = {}
        if bias is not None:
            kw['bias'] = bias
        if scale is not None:
            kw['scale'] = scale
        return self.op('act', lambda e: e.activation(out=out, in_=in_, func=func, **kw), reads, writes)

    def tt(self, eng, out, in0, in1, op, reads, writes):
        return self.op(eng, lambda e: e.tensor_tensor(out=out, in0=in0, in1=in1, op=op), reads, writes)

    def ts(self, eng, out, in0, s1, s2, op0, op1, reads, writes):
        if op1 is None:
            return self.op(eng, lambda e: e.tensor_scalar(out=out, in0=in0, scalar1=s1, scalar2=None, op0=op0), reads, writes)
        return self.op(eng, lambda e: e.tensor_scalar(out=out, in0=in0, scalar1=s1, scalar2=s2, op0=op0, op1=op1), reads, writes)

    def stt(self, out, in0, scalar, in1, op0, op1, reads, writes):
        return self.op('dve', lambda e: e.scalar_tensor_tensor(out=out, in0=in0, scalar=scalar, in1=in1, op0=op0, op1=op1), reads, writes)

    def copy(self, eng, out, in_, reads, writes):
        if eng == 'act':
            return self.act(out, in_, AF.Copy, reads, writes)
        return self.op(eng, lambda e: e.tensor_copy(out=out, in_=in_), reads, writes)

    def memset(self, eng, ap, val, writes):
        return self.op(eng, lambda e: e.memset(ap, val), (), writes)

    def dma(self, q, semname, out, in_, reads, writes, slow=False):
        if slow:
            return self.op(q, lambda e: e.dma_start(out=out, in_=in_, allow_slow_non_contiguous=True), reads, writes, dma_sem=semname)
        return self.op(q, lambda e: e.dma_start(out=out, in_=in_), reads, writes, dma_sem=semname)

    def bank(self):
        b = self.bank_i
        self.bank_i = (self.bank_i + 1) % 8
        return b


def build():
    nc = bass.Bass("TRN2", target_bir_lowering=False)
    es = contextlib.ExitStack()
    P = Prog(nc, es)

    def din(name, shape):
        return nc.dram_tensor(name, shape, F32, kind="ExternalInput").ap()

    def dout(name, shape):
        return nc.dram_tensor(name, shape, F32, kind="ExternalOutput").ap()

    def dscr(name, shape):
        return nc.dram_tensor(name, shape, BF16, kind="Internal").ap()

    x_p = din("x_p", [SEQ, D]); x_s = din("x_s", [64, D]); mem = din("mem", [256, D])
    ck = din("ck", [2, SEQ, 1024]); cv = din("cv", [2, SEQ, 1024]); cl = din("cl", [2, SEQ, 8])
    cmk = din("cmk", [2, 256, D]); cmv = din("cmv", [2, 256, D])
    w_in = din("w_in", [D, 5128]); b_f = din("b_f", [1, 8])
    sgu_g = din("sgu_g", [1, 1024]); sgu_b = din("sgu_b", [1, 1024])
    w_s = din("w_s", [4, 128, 128]); b_s = din("b_s", [4, 128])
    w_out = din("w_out", [D, D])
    ln_g = [din("ln%d_g" % i, [1, D]) for i in (1, 2, 3)]
    ln_b = [din("ln%d_b" % i, [1, D]) for i in (1, 2, 3)]
    w_mq = din("w_mq", [D, D]); w_mk = din("w_mk", [D, D]); w_mv = din("w_mv", [D, D]); w_mo = din("w_mo", [D, D])
    w_up = din("w_up", [D, 8192]); w_down = din("w_down", [8192, D])

    y_p = dout("y_p", [SEQ, D]); y_s = dout("y_s", [64, D])
    fk_p = dout("fk_p", [SEQ, 1024]); fv_p = dout("fv_p", [SEQ, 1024]); fl_p = dout("fl_p", [SEQ, 8])
    mk_p = dout("mk_p", [256, D]); mv_p = dout("mv_p", [256, D])
    fk_s = dout("fk_s", [64, 1024]); fv_s = dout("fv_s", [64, 1024]); fl_s = dout("fl_s", [64, 8])
    gv_s = dout("gv_s", [64, 1024])

    mkT_s = dscr("mkT_s", [3, 128, 4096]); mv_s = dscr("mv_s", [3, 256, D])
    pscr = dscr("pscr", [108, 128, 4096])
    kTs = dscr("kTs", [16, 128, 2048]); vs = dscr("vs", [16, 128, 2064])

    def sb(name, shape, dt):
        return es.enter_context(nc.sbuf_tensor(name, shape, dt))

    wbuf = [sb("wbuf%d" % i, [128, 4096], BF16) for i in range(3)]
    wf = sb("wf", [128, 16, 8], BF16)
    A_raw = sb("A_raw", [128, 8256], BF16)
    B_raw = sb("B_raw", [128, 8192], BF16)
    A_v = A_raw[:, 0:8192].rearrange("p (c t) -> p c t", c=16)
    B_v = B_raw[:, :].rearrange("p (c t) -> p c t", c=16)
    oacc = A_raw[:, :].bitcast(F32).rearrange("p (i h d) -> p i h d", i=4, h=8)
    resid = sb("resid", [128, 4, D], F32)
    qT = sb("qT", [128, 8, 512], BF16)
    gn = sb("gn", [128, 4, 1024], BF16)
    gn_flat = gn[:, :, :].rearrange("p a b -> p (a b)")
    mkTst = gn_flat.rearrange("p (c m) -> p c m", c=16)
    tokb = [sb("tokb%d" % i, [128, 1024], BF16) for i in range(2)]
    kb = sb("kb", [128, 2, 1024], BF16)
    kT = [sb("kT%d" % i, [128, 8, 256], BF16) for i in range(2)]
    vaug = [sb("vaug%d" % i, [128, 2, 8, 129], BF16) for i in range(2)]
    kTown = sb("kTown", [128, 8, 512], BF16)
    vown = sb("vown", [128, 4, 8, 129], BF16)
    PT = [sb("PT%d" % i, [128, 2, 512], BF16) for i in range(2)]
    PTS = [[sb("PTS%d_%d" % (b, i), [128, 2, 64], BF16) for i in range(2)] for b in range(2)]
    lnG = sb("lnG", [128, D], F32); lnB = sb("lnB", [128, D], F32)
    st = [sb("st%d" % i, [128, 256], F32) for i in range(6)]
    kbs = [sb("kbs%d" % i, [128, 256], BF16) for i in range(2)]
    xb = [sb("xb%d" % i, [128, D], BF16) for i in range(2)]
    scr4 = sb("scr4", [128, 1024], F32)
    gst = scr4
    rl = [scr4[:, 0:512], scr4[:, 512:1024]]
    tmpf = rl
    ident = sb("ident", [128, 128], BF16); maskP = sb("maskP", [128, 128], BF16); maskS = sb("maskS", [64, 64], BF16)
    onesb = sb("onesb", [128, 128], BF16)
    triu = sb("triu", [128, 128], F32); sel127 = sb("sel127", [128, 128], F32)
    selS = [sb("selS%d" % b, [64, 128], F32) for b in range(2)]
    selOwn = sb("selOwn", [64, 64], F32); tri64 = sb("tri64", [64, 64], F32)
    wsraw = sb("wsraw", [128, 4, 128], F32); wsrb = sb("wsrb", [128, 4, 128], BF16)
    wsT = sb("wsT", [128, 4, 128], BF16)
    wsrawS = sb("wsrawS", [64, 4, 64], F32); wsrbS = sb("wsrbS", [64, 4, 64], BF16); wsTS = sb("wsTS", [64, 4, 64], BF16)
    bscol = sb("bscol", [128, 4], F32); bscolS = sb("bscolS", [64, 4], F32)
    bfb = sb("bfb", [128, 8], F32)
    ckall = sb("ckall", [128, 64, 8], F32)
    clsb = sb("clsb", [128, 32, 8], F32)
    Lloc = sb("Lloc", [128, 32, 8], F32); totb = sb("totb", [128, 32, 8], F32); pref = sb("pref", [128, 32, 8], F32)
    carry = sb("carry", [128, 8], F32)
    cendS = [sb("cendS%d" % b, [128, 8], F32) for b in range(2)]
    crefS = [sb("crefS%d" % b, [128, 8], F32) for b in range(2)]
    carryrows = sb("carryrows", [64, 8], F32)
    cref = sb("cref", [128, 4, 8], F32)
    biasb = [sb("biasb%d" % i, [128, 2, 4, 8], F32) for i in range(2)]
    biaso = sb("biaso", [128, 4, 4, 8], F32)
    zf = sb("zf", [128, 8], F32); zf4 = sb("zf4", [128, 4, 8], F32); lfall = sb("lfall", [128, 4, 8], F32)
    stats = sb("stats", [128, 4, 4, 6], F32); mvar = sb("mvar", [128, 4, 2], F32); rstd = sb("rstd", [128, 4, 2], F32)
    rlo = sb("rlo", [128, 8], F32)
    ones1 = sb("ones1", [128, 1], F32)

    ps = [es.enter_context(nc.psum_tensor("ps%d" % i, [128, 512], F32)) for i in range(8)]

    def psb(bk):
        return ps[bk][:, :].bitcast(BF16)

    def PR(bk):
        return 'ps%d' % bk

    def aff(out, in_, pattern, cmp, base, cm, writes):
        return P.op('pool', lambda e: e.affine_select(out=out, in_=in_, pattern=pattern, compare_op=cmp, fill=0.0,
                                                      base=base, channel_multiplier=cm), writes, writes)

    P.memset('pool', ident[:, :], 1.0, ['ident'])
    aff(ident[:, :], ident[:, :], [[-1, 128]], ALU.is_equal, 0, 1, ['ident'])
    P.memset('pool', maskP[:, :], 1.0, ['maskP'])
    aff(maskP[:, :], maskP[:, :], [[1, 128]], ALU.is_ge, 0, -1, ['maskP'])
    P.memset('pool', maskS[:, :], 1.0, ['maskS'])
    aff(maskS[:, :], maskS[:, :], [[1, 64]], ALU.is_ge, 0, -1, ['maskS'])
    P.memset('pool', maskS[0:32, 32:64], 0.0, ['maskS'])
    P.memset('pool', onesb[:, :], 1.0, ['onesb'])
    P.memset('pool', ones1[:, :], 1.0, ['ones1'])
    P.memset('pool', triu[:, :], 1.0, ['triu'])
    aff(triu[:, :], triu[:, :], [[1, 128]], ALU.is_ge, 0, -1, ['triu'])
    P.memset('pool', tri64[:, :], 1.0, ['tri64'])
    aff(tri64[:, :], tri64[:, :], [[1, 64]], ALU.is_ge, 0, -1, ['tri64'])
    P.memset('pool', tri64[0:32, 32:64], 0.0, ['tri64'])
    P.memset('pool', sel127[:, :], 1.0, ['sel127'])
    aff(sel127[:, :], sel127[:, :], [[0, 128]], ALU.is_equal, -127, 1, ['sel127'])
    for b in range(2):
        P.memset('pool', selS[b][:, :], 1.0, ['selS%d' % b])
        aff(selS[b][:, :], selS[b][:, :], [[0, 128]], ALU.is_equal, -(32 * b + 31), 1, ['selS%d' % b])
    P.memset('pool', selOwn[:, :], 1.0, ['selOwn'])
    aff(selOwn[:, 0:32], selOwn[:, 0:32], [[0, 32]], ALU.is_equal, -31, 1, ['selOwn'])
    aff(selOwn[:, 32:64], selOwn[:, 32:64], [[0, 32]], ALU.is_equal, -63, 1, ['selOwn'])
    for i in range(2):
        P.memset('pool', vaug[i][:, :, :, 128:129], 1.0, ['vaug%d_0' % i, 'vaug%d_1' % i])
        for b in range(2):
            P.memset('pool', PTS[b][i][:, :, :], 0.0, ['PTS%d_%d' % (b, i)])
    P.memset('pool', vown[:, :, :, 128:129], 1.0, ['vown'])

    P.dma('sp', 'S_misc1', bfb[:, :], b_f[0, :].partition_broadcast(128), [], ['bfb'])
    P.dma('sp', 'S_misc2', bscol[:, :], b_s.rearrange("g t -> t g"), [], ['bscol'], slow=True)
    P.dma('sp', 'S_misc3', bscolS[0:32, :], b_s[:, 0:32].rearrange("g t -> t g"), [], ['bscolS'], slow=True)
    P.dma('sp', 'S_misc4', bscolS[32:64, :], b_s[:, 0:32].rearrange("g t -> t g"), [], ['bscolS'], slow=True)
    P.dma('sp', 'S_misc5', wsraw[:, :, :], w_s.rearrange("g t s -> t g s"), [], ['wsraw'])
    P.memset('pool', wsrawS[:, :, :], 0.0, ['wsrawS'])
    P.dma('sp', 'S_misc6', wsrawS[0:32, :, 0:32], w_s[:, 0:32, 0:32].rearrange("g t s -> t g s"), [], ['wsrawS'])
    P.dma('sp', 'S_misc7', wsrawS[32:64, :, 32:64], w_s[:, 0:32, 0:32].rearrange("g t s -> t g s"), [], ['wsrawS'])
    aff(wsraw[:, :, :], wsraw[:, :, :], [[0, 4], [-1, 128]], ALU.is_ge, 0, 1, ['wsraw'])
    aff(wsrawS[:, :, :], wsrawS[:, :, :], [[0, 4], [-1, 64]], ALU.is_ge, 0, 1, ['wsrawS'])
    P.copy('dve', wsrb[:, :, :], wsraw[:, :, :], ['wsraw'], ['wsrb'])
    P.copy('dve', wsrbS[:, :, :], wsrawS[:, :, :], ['wsrawS'], ['wsrbS'])
    bk = P.bank()
    for g in range(4):
        P.tr(psb(bk)[:, g * 128:(g + 1) * 128], wsrb[:, g, :], ident[:, :], ['wsrb', 'ident'], [PR(bk)])
    P.copy('dve', wsT[:, :, :], psb(bk)[:, 0:512].rearrange("p (g t) -> p g t", g=4), [PR(bk)], ['wsT'])
    bk = P.bank()
    for g in range(4):
        P.tr(psb(bk)[0:64, g * 128:g * 128 + 64], wsrbS[:, g, :], ident[0:64, 0:64], ['wsrbS', 'ident'], [PR(bk)])
    P.copy('dve', wsTS[:, :, :], psb(bk)[0:64, 0:512].rearrange("p (g t) -> p g t", g=4)[:, :, 0:64], [PR(bk)], ['wsTS'])

    def conv(dst, src, res):
        P.dma('pool', 'S_cv_' + res, dst, src, [], [res])

    xbi = [0]
    mem_xb = []
    for mb in range(2):
        x_ = xb[xbi[0]]; xr = 'xb%d' % xbi[0]; xbi[0] ^= 1
        P.dma('pool', 'S_' + xr, x_[:, :], mem[mb * 128:(mb + 1) * 128, :], [], [xr])
        mem_xb.append((x_, xr))
    P.dma('pool', 'S_wf', wf[:, :, :], w_in[:, 3072:3080].rearrange("(c p) n -> p c n", p=128), [], ['wf'])
    for b in range(2):
        conv(mv_s[1 + b, :, :], cmv[b, :, :], 'mvs%d' % (1 + b))
    wslot = [0]
    SRC = {'win': w_in, 'wout': w_out, 'wmq': w_mq, 'wmo': w_mo, 'wup': w_up, 'wdn': w_down, 'wmk': w_mk, 'wmv': w_mv}
    piece_idx = {}

    def get_piece(name, r0, c0):
        key = (name, r0, c0)
        s = wslot[0]
        wslot[0] = (s + 1) % 3
        v = wbuf[s][:, :].rearrange("p (c n) -> p c n", c=16)
        if name in ('wmk', 'wmv') or key not in piece_idx:
            cc = c0 + 8 if (name == 'win' and c0 >= 3072) else c0
            ap = SRC[name][r0:r0 + 2048, cc:cc + 256].rearrange("(c p) n -> p c n", p=128)
            P.dma('pool', 'S_w%d' % s, v, ap, [], ['w%d' % s])
            if name not in ('wmk', 'wmv'):
                idx = len(piece_idx)
                piece_idx[key] = idx
                P.dma('sp', 'S_wb%d' % s, pscr[idx, :, :], wbuf[s][:, :], ['w%d' % s], ['pc%d' % idx])
        else:
            idx = piece_idx[key]
            P.dma('sp', 'S_w%d' % s, wbuf[s][:, :], pscr[idx, :, :], ['pc%d' % idx], ['w%d' % s])
        return v, 'w%d' % s

    def load_piece(src_ap, view, reads):
        s = wslot[0]
        wslot[0] = (s + 1) % 3
        if view == 'w':
            v = wbuf[s][:, :].rearrange("p (c n) -> p c n", c=16)
        elif view == 'mv':
            v = wbuf[s][:, :].rearrange("p (m n) -> p m n", m=2)
        P.dma('sp', 'S_w%d' % s, v, src_ap, reads, ['w%d' % s])
        return v, 'w%d' % s

    def wsrc(scr, r0, c0):
        return scr[r0:r0 + 2048, c0:c0 + 256].rearrange("(c p) n -> p c n", p=128)

    sti = [0]
    kbi = [0]
    tki = [0]
    evi = [0]

    def ev_eng():
        evi[0] ^= 1
        return 'act' if evi[0] else 'dve'

    def transpose_rows_to(src_tile, src_res, TB, nchunk, dst_v, dst_res, c0, t0):
        for h0 in range(0, nchunk, 8):
            bk = P.bank()
            for c8 in range(8):
                c = h0 + c8
                P.tr(psb(bk)[:, c8 * 128:c8 * 128 + TB], src_tile[0:TB, c * 128:(c + 1) * 128], ident[0:TB, 0:TB],
                     [src_res, 'ident'], [PR(bk)])
            P.copy(ev_eng(), dst_v[:, c0 + h0:c0 + h0 + 8, t0:t0 + TB],
                   psb(bk)[:, :].rearrange("p (c t) -> p c t", c=8)[:, :, 0:TB], [PR(bk)], [dst_res])

    def tokmajor_piece(wv, wres, src_v, src_res, NB, TB, evac):
        for tb in (range(NB) if isinstance(NB, int) else NB):
            bk = P.bank()
            for c in range(16):
                P.mm(ps[bk][0:TB, 0:256], src_v[:, c, tb * TB:(tb + 1) * TB], wv[:, c, :], c == 0, c == 15,
                     [wres, src_res], [PR(bk)])
            evac(tb, bk)

    def featmajor_piece(wv, wres, src_v, src_res, NT, evac, t0=0):
        for lc in range(2):
            bk = P.bank()
            for c in range(16):
                P.mm(ps[bk][:, 0:NT - t0], wv[:, c, lc * 128:(lc + 1) * 128], src_v[:, c, t0:NT], c == 0, c == 15,
                     [wres, src_res], [PR(bk)])
            evac(lc, bk)

    def layer_norm(tb, TB, width, src, res, nstat):
        for i in range(nstat):
            P.op('dve', lambda e, i=i: e.bn_stats(out=stats[0:TB, tb, i, :], in_=src[:, i * 512:(i + 1) * 512]), [res], ['stats%d' % tb])
        P.op('dve', lambda e: e.bn_aggr(out=mvar[0:TB, tb, :], in_=stats[0:TB, tb, 0:nstat, :].rearrange("p a b -> p (a b)")), ['stats%d' % tb], ['mvar%d' % tb])
        P.ts('dve', rstd[0:TB, tb, 0:1], mvar[0:TB, tb, 1:2], EPS, None, ALU.add, None, ['mvar%d' % tb], ['rstd%d' % tb])
        P.act(rstd[0:TB, tb, 0:1], rstd[0:TB, tb, 0:1], AF.Sqrt, ['rstd%d' % tb], ['rstd%d' % tb])
        P.op('dve', lambda e: e.reciprocal(out=rstd[0:TB, tb, 0:1], in_=rstd[0:TB, tb, 0:1]), ['rstd%d' % tb], ['rstd%d' % tb])
        P.ts('dve', rstd[0:TB, tb, 1:2], mvar[0:TB, tb, 0:1], rstd[0:TB, tb, 0:1], -1.0, ALU.mult, ALU.mult, ['mvar%d' % tb, 'rstd%d' % tb], ['nmr%d' % tb])
        P.act(src, src, AF.Identity, [res, 'rstd%d' % tb, 'nmr%d' % tb], [res], bias=rstd[0:TB, tb, 1:2], scale=rstd[0:TB, tb, 0:1])

    out_events = []

    store_defer = []

    def store(dst, src, reads, semname, res_w=()):
        def emit():
            out_events.append(P.dma('act', semname, dst, src, reads, list(res_w)))
        store_defer.append(emit)
        while len(store_defer) > 1:
            store_defer.pop(0)()

    def flush_stores():
        while store_defer:
            store_defer.pop(0)()

    def load_ln(gsrc, bsrc, width):
        P.dma('sp', 'S_lnG', lnG[:, 0:width], gsrc[0, :].partition_broadcast(128), [], ['lnG'])
        P.dma('sp', 'S_lnB', lnB[:, 0:width], bsrc[0, :].partition_broadcast(128), [], ['lnB'])

    def mem_prologue():
        for mb in range(2):
            x_, xr = mem_xb[mb]
            transpose_rows_to(x_, xr, 128, 16, A_v, 'A', 0, mb * 128)
        for p in range(8):
            wv, wres = get_piece('wmk', 0, p * 256)

            def evac(mb, bk, p=p):
                s_ = st[sti[0]]; sr = 'st%d' % sti[0]; sti[0] = (sti[0] + 1) % 6
                P.copy('act', s_[:, :], ps[bk][:, 0:256], [PR(bk)], [sr, PR(bk)])
                store(mk_p[mb * 128:(mb + 1) * 128, p * 256:(p + 1) * 256], s_[:, :], [sr], 'S_' + sr)
                k_ = kbs[kbi[0]]; kr = 'kbs%d' % kbi[0]; kbi[0] ^= 1
                P.copy('dve', k_[:, :], ps[bk][:, 0:256], [PR(bk)], [kr])
                b2 = P.bank()
                for lc in range(2):
                    P.tr(psb(b2)[:, lc * 128:(lc + 1) * 128], k_[:, lc * 128:(lc + 1) * 128], ident[:, :], [kr, 'ident'], [PR(b2)])
                P.copy('dve', mkTst[:, 2 * p:2 * p + 2, mb * 128:(mb + 1) * 128],
                       psb(b2)[:, 0:256].rearrange("p (c t) -> p c t", c=2), [PR(b2)], ['gn'])
            tokmajor_piece(wv, wres, A_v, 'A', 2, 128, evac)
        P.dma('sp', 'S_mkT', mkT_s[0, :, :], gn_flat, ['gn'], ['mkTs0'])
        for p in range(8):
            wv, wres = get_piece('wmv', 0, p * 256)

            def evac(mb, bk, p=p):
                s_ = st[sti[0]]; sr = 'st%d' % sti[0]; sti[0] = (sti[0] + 1) % 6
                P.copy('act', s_[:, :], ps[bk][:, 0:256], [PR(bk)], [sr, PR(bk)])
                store(mv_p[mb * 128:(mb + 1) * 128, p * 256:(p + 1) * 256], s_[:, :], [sr], 'S_' + sr)
                k_ = kbs[kbi[0]]; kr = 'kbs%d' % kbi[0]; kbi[0] ^= 1
                P.copy('dve', k_[:, :], ps[bk][:, 0:256], [PR(bk)], [kr])
                P.dma('sp', 'S_' + kr, mv_s[0, mb * 128:(mb + 1) * 128, p * 256:(p + 1) * 256], k_[:, :], [kr], ['mvs0_%d_%d' % (mb, p)])
            tokmajor_piece(wv, wres, A_v, 'A', 2, 128, evac)
        flush_stores()

    def mem_prologue_sample():
        for b in range(2):
            for mb in range(2):
                x_ = xb[xbi[0]]; xr = 'xb%d' % xbi[0]; xbi[0] ^= 1
                P.dma('pool', 'S_' + xr, x_[:, :], cmk[b, mb * 128:(mb + 1) * 128, :], [], [xr])
                transpose_rows_to(x_, xr, 128, 16, mkTst, 'gn', 0, mb * 128)
            P.dma('sp', 'S_mkT', mkT_s[1 + b, :, :], gn_flat, ['gn'], ['mkTs%d' % (1 + b)])

    mvs0_res = ['mvs0_%d_%d' % (mb, p) for mb in range(2) for p in range(8)]

    pending = []
    pending2 = []
    xpre = []

    def run_tile(ti, sample, nxt=None):
        NB, TB = (1, 64) if sample else (4, 128)
        NT = NB * TB
        xsrc = x_s if sample else x_p[ti * 512:(ti + 1) * 512, :]
        fk_o = fk_s if sample else fk_p[ti * 512:(ti + 1) * 512, :]
        fv_o = fv_s if sample else fv_p[ti * 512:(ti + 1) * 512, :]
        fl_o = fl_s if sample else fl_p[ti * 512:(ti + 1) * 512, :]
        y_o = y_s if sample else y_p[ti * 512:(ti + 1) * 512, :]
        tag = 's' if sample else 'p%d' % ti

        for tb in range(NB):
            if xpre:
                x_, xr = xpre.pop(0)
            else:
                x_ = xb[xbi[0]]; xr = 'xb%d' % xbi[0]; xbi[0] ^= 1
                P.dma('pool', 'S_' + xr, x_[0:TB, :], xsrc[tb * TB:(tb + 1) * TB, :], [], [xr])
            transpose_rows_to(x_, xr, TB, 16, A_v, 'A', 0, tb * TB)

        while pending:
            pending.pop(0)()
        for p in range(4):
            wv, wres = get_piece('win', 0, p * 256)

            def evac(lc, bk, p=p):
                P.copy('act', qT[:, 2 * p + lc, 0:NT], ps[bk][:, 0:NT], [PR(bk)], ['qT'])
            featmajor_piece(wv, wres, A_v, 'A', NT, evac)
        while pending2:
            pending2.pop(0)()
        kdefer = []
        for p in range(4):
            wv, wres = get_piece('win', 0, 1024 + p * 256)

            def evac(tb, bk, p=p):
                s_ = st[sti[0]]; sr = 'st%d' % sti[0]; sti[0] = (sti[0] + 1) % 6
                P.copy('act', s_[0:TB, :], ps[bk][0:TB, 0:256], [PR(bk)], [sr, PR(bk)])
                store(fk_o[tb * TB:(tb + 1) * TB, p * 256:(p + 1) * 256], s_[0:TB, :], [sr], 'S_' + sr,
                      ['fk_%s_%d_%d' % (tag, tb, p)])
                k_ = kbs[kbi[0]]; kr = 'kbs%d' % kbi[0]; kbi[0] ^= 1
                P.copy('pool', k_[0:TB, :], s_[0:TB, :], [sr], [kr])
                if kdefer:
                    kdefer.pop(0)()

                def trs(k_=k_, kr=kr, p=p, tb=tb):
                    b2 = P.bank()
                    for lc in range(2):
                        P.tr(psb(b2)[:, lc * 128:lc * 128 + TB], k_[0:TB, lc * 128:(lc + 1) * 128], ident[0:TB, 0:TB], [kr, 'ident'], [PR(b2)])
                    P.copy('dve', kTown[:, 2 * p:2 * p + 2, tb * TB:(tb + 1) * TB],
                           psb(b2)[:, 0:256].rearrange("p (c t) -> p c t", c=2)[:, :, 0:TB], [PR(b2)], ['kTown'])
                kdefer.append(trs)
            tokmajor_piece(wv, wres, A_v, 'A', NB, TB, evac)
        while kdefer:
            kdefer.pop(0)()
        flush_stores()
        fbanks = []
        for tb in range(NB):
            bk = P.bank()
            for c in range(16):
                P.mm(ps[bk][0:TB, 0:8], A_v[:, c, tb * TB:(tb + 1) * TB], wf[:, c, :], c == 0, c == 15, ['A', 'wf'], [PR(bk)])
            fbanks.append(bk)
        for tb in range(NB):
            bk = fbanks[tb]
            zr = 'zf4_%d' % tb
            P.tt('dve', zf4[0:TB, tb, :], ps[bk][0:TB, 0:8], bfb[0:TB, :], ALU.add, [PR(bk), 'bfb'], [zr])
            P.act(zf4[0:TB, tb, :], zf4[0:TB, tb, :], AF.Exp, [zr], [zr], scale=-1.0)
            P.ts('dve', zf4[0:TB, tb, :], zf4[0:TB, tb, :], 1.0, None, ALU.add, None, [zr], [zr])
            P.act(zf4[0:TB, tb, :], zf4[0:TB, tb, :], AF.Ln, [zr], [zr])
            P.ts('dve', lfall[0:TB, tb, :], zf4[0:TB, tb, :], -1.0, None, ALU.mult, None, [zr], ['lfall%d' % tb])
            store(fl_o[tb * TB:(tb + 1) * TB, :], lfall[0:TB, tb, :], ['lfall%d' % tb], 'S_lf')
        flush_stores()

        def f_cumsum(tb):
            j = ti * 4 + tb
            if j == 0:
                P.memset('dve', carry[:, :], 0.0, ['carry'])
            bk = P.bank()
            P.mm(ps[bk][:, 0:8], triu[:, :], lfall[:, tb, :], True, True, ['triu', 'lfall%d' % tb], [PR(bk)])
            P.tt('dve', ckall[:, j, :], ps[bk][:, 0:8], carry[:, :], ALU.add, [PR(bk), 'carry'], ['ckall'])
            bk = P.bank()
            P.mm(ps[bk][:, 0:8], sel127[:, :], ckall[:, j, :], True, True, ['sel127', 'ckall'], [PR(bk)])
            P.copy('dve', carry[:, :], ps[bk][:, 0:8], [PR(bk)], ['carry'])
            if tb == 1:
                for i_ in range(4):
                    P.copy('dve', cref[:, i_, :], ps[bk][:, 0:8], [PR(bk)], ['cref'])

        def f_cumsum_sample():
            for b in range(2):
                P.dma('sp', 'S_clsb', clsb[:, :, :], cl[b, :, :].rearrange("(j p) h -> p j h", p=128), [], ['clsb'])
                bk = P.bank()
                P.mm(ps[bk][:, 0:256], triu[:, :], clsb[:, :, :].rearrange("p j h -> p (j h)"), True, True, ['triu', 'clsb'], [PR(bk)])
                P.copy('dve', Lloc[:, :, :].rearrange("p j h -> p (j h)"), ps[bk][:, 0:256], [PR(bk)], ['Lloc'])
                bk = P.bank()
                P.mm(ps[bk][:, 0:256], sel127[:, :], Lloc[:, :, :].rearrange("p j h -> p (j h)"), True, True, ['sel127', 'Lloc'], [PR(bk)])
                P.copy('dve', totb[:, :, :].rearrange("p j h -> p (j h)"), ps[bk][:, 0:256], [PR(bk)], ['totb'])
                for h in range(8):
                    P.op('dve', lambda e, h=h: e.tensor_tensor_scan(out=pref[:, :, h], data0=onesb_f[:, 0:32], data1=totb[:, :, h],
                                                                  initial=0.0, op0=ALU.mult, op1=ALU.add), ['totb', 'onesf'], ['pref'])
                P.copy('dve', cendS[b][:, :], pref[:, 31, :], ['pref'], ['cendS%d' % b])
                P.tt('dve', pref[:, :, :], pref[:, :, :], totb[:, :, :], ALU.subtract, ['pref', 'totb'], ['pref'])
                P.tt('dve', ckall[:, b * 32:(b + 1) * 32, :], Lloc[:, :, :], pref[:, :, :], ALU.add, ['Lloc', 'pref'], ['ckall'])
            P.copy('dve', carryrows[0:32, :], cendS[0][0:32, :], ['cendS0'], ['carryrows'])
            P.copy('dve', carryrows[32:64, :], cendS[1][32:64, :], ['cendS1'], ['carryrows'])
            bk = P.bank()
            P.mm(ps[bk][0:64, 0:8], tri64[:, :], lfall[0:64, 0, :], True, True, ['tri64', 'lfall0'], [PR(bk)])
            P.tt('dve', zf[0:64, :], ps[bk][0:64, 0:8], carryrows[:, :], ALU.add, [PR(bk), 'carryrows'], ['zf'])
            bk = P.bank()
            P.mm(ps[bk][0:64, 0:8], selOwn[:, :], zf[0:64, :], True, True, ['selOwn', 'zf'], [PR(bk)])
            P.tt('dve', biaso[0:64, 0, 0, :], ps[bk][0:64, 0:8], zf[0:64, :], ALU.subtract, [PR(bk), 'zf'], ['biaso'])
            for b in range(2):
                bk = P.bank()
                P.mm(ps[bk][:, 0:8], selS[b][:, :], zf[0:64, :], True, True, ['selS%d' % b, 'zf'], [PR(bk)])
                P.copy('dve', crefS[b][:, :], ps[bk][:, 0:8], [PR(bk)], ['crefS%d' % b])

        for p in range(4):
            wv, wres = get_piece('win', 0, 2048 + p * 256)

            def evac(tb, bk, p=p):
                s_ = st[sti[0]]; sr = 'st%d' % sti[0]; sti[0] = (sti[0] + 1) % 6
                P.copy('act', s_[0:TB, :], ps[bk][0:TB, 0:256], [PR(bk)], [sr, PR(bk)])
                store(fv_o[tb * TB:(tb + 1) * TB, p * 256:(p + 1) * 256], s_[0:TB, :], [sr], 'S_' + sr,
                      ['fv_%s_%d_%d' % (tag, tb, p)])
                P.copy('pool', vown[0:TB, tb, 2 * p:2 * p + 2, 0:128],
                       s_[0:TB, :].rearrange("p (h d) -> p h d", h=2), [sr], ['vown'])
            tokmajor_piece(wv, wres, A_v, 'A', NB, TB, evac)
        flush_stores()
        if not sample and ti < NTILE - 1:
            for half in range(2):
                c_ = 2 * ti + half
                P.dma('act', 'S_kTs', kTs[c_, :, :].rearrange("p (h t) -> p h t", h=8), kTown[:, :, half * 256:(half + 1) * 256],
                      ['kTown'], ['kTs%d' % c_])
                P.dma('act', 'S_vs', vs[c_, :, :], vown[:, 2 * half:2 * half + 2, :, :].rearrange("p j h d -> p (j h d)"),
                      ['vown'], ['vs%d' % c_])
        load_ln(sgu_g, sgu_b, 1024)
        for p in range(4):
            wv, wres = get_piece('win', 0, 4096 + p * 256)

            def evac(tb, bk, p=p):
                P.act(resid[0:TB, tb, 1024 + p * 256:1024 + (p + 1) * 256], ps[bk][0:TB, 0:256], AF.Gelu_apprx_tanh, [PR(bk)], ['resid%d' % tb])
            tokmajor_piece(wv, wres, A_v, 'A', NB, TB, evac)
            if not sample:
                f_cumsum(p)
        if sample:
            f_cumsum_sample()
        for tb in range(NB):
            layer_norm(tb, TB, 1024, resid[0:TB, tb, 1024:2048], 'resid%d' % tb, 2)
        for tb in range(NB):
            src = resid[0:TB, tb, 1024:2048]
            P.tt('pool', src, src, lnG[0:TB, 0:1024], ALU.mult, ['resid%d' % tb, 'lnG'], ['resid%d' % tb])
            if sample:
                P.tt('dve', gst[0:TB, :], src, lnB[0:TB, 0:1024], ALU.add, ['resid%d' % tb, 'lnB'], ['rl0', 'rl1'])
                store(gv_s[:, :], gst[0:TB, :], ['rl0', 'rl1'], 'S_gst')
                flush_stores()
                P.copy('act', gn[0:TB, tb, :], gst[0:TB, :], ['rl0', 'rl1'], ['gn'])
            else:
                P.tt('dve', gn[0:TB, tb, :], src, lnB[0:TB, 0:1024], ALU.add, ['resid%d' % tb, 'lnB'], ['gn'])
        for p in range(4):
            wv, wres = get_piece('win', 0, 3072 + p * 256)

            def evac(tb, bk, p=p):
                P.act(resid[0:TB, tb, p * 256:(p + 1) * 256], ps[bk][0:TB, 0:256], AF.Gelu_apprx_tanh, [PR(bk)], ['resid%d' % tb])
            tokmajor_piece(wv, wres, A_v, 'A', NB, TB, evac)
        wsT_ = wsTS if sample else wsT
        bsc = bscolS if sample else bscol
        wres_ = 'wsTS' if sample else 'wsT'
        sdefer = []

        def sgu_tb(tb):
            t_ = tokb[tki[0]]; tr_ = 'tokb%d' % tki[0]; tki[0] ^= 1
            for g in range(4):
                bk = P.bank()
                P.mm(ps[bk][0:TB, 0:256], wsT_[0:TB, g, 0:TB], gn[0:TB, tb, g * 256:(g + 1) * 256], True, True, [wres_, 'gn'], [PR(bk)])
                P.stt(t_[0:TB, g * 256:(g + 1) * 256], ps[bk][0:TB, 0:256], bsc[0:TB, g:g + 1], resid[0:TB, tb, g * 256:(g + 1) * 256],
                      ALU.add, ALU.mult, [PR(bk), 'bscol', 'resid%d' % tb], [tr_])
            if sdefer:
                sdefer.pop(0)()
            sdefer.append(lambda: transpose_rows_to(t_, tr_, TB, 8, B_v, 'B', 8, tb * TB))
        sgu_q = [(lambda tb=tb: sgu_tb(tb)) for tb in range(NB)]
        chunks = []
        if sample:
            for b in range(2):
                for cj in range(16):
                    chunks.append(('cache', b, cj))
            chunks.append(('ownS', 0, 0))
        else:
            for cj in range(2 * ti):
                chunks.append(('prev', 0, cj))
            chunks.append(('own', 0, 0)); chunks.append(('own', 0, 1))
        units = []
        cslot = [0]
        ptc = [0]

        def prep_stream(kind, b, cj, s):
            if kind == 'prev':
                jb = cj * 2
                P.dma('sp', 'S_kT%d' % s, kT[s][:, :, :], kTs[cj, :, :].rearrange("p (h t) -> p h t", h=8), ['kTs%d' % cj], ['kT%d' % s])
                P.dma('sp', 'S_vg%d' % s, vaug[s][:, :, :, :].rearrange("p j h d -> p (j h d)"), vs[cj, :, :], ['vs%d' % cj],
                      ['vaug%d_0' % s, 'vaug%d_1' % s])
                for jj in range(2):
                    P.tt('dve', biasb[s][:, jj, 0, :], cref[:, 0, :], ckall[:, jb + jj, :], ALU.subtract, ['cref', 'ckall'], ['biasb%d' % s])
                return
            ksrc = ck[b, cj * 256:(cj + 1) * 256, :]; vsrc = cv[b, cj * 256:(cj + 1) * 256, :]
            jb = b * 32 + cj * 2
            P.dma('pool', 'S_kb', kb[:, :, :], ksrc.rearrange("(j p) d -> p j d", p=128), [], ['kb'])
            for jj in range(2):
                P.dma('pool', 'S_vaug%d_%d' % (s, jj), vaug[s][:, jj, :, 0:128], vsrc[jj * 128:(jj + 1) * 128, :].rearrange("p (h d) -> p h d", h=8),
                      [], ['vaug%d_%d' % (s, jj)])
            for jj in range(2):
                bk = P.bank()
                for h in range(8):
                    P.tr(psb(bk)[:, h * 128:(h + 1) * 128], kb[:, jj, h * 128:(h + 1) * 128], ident[:, :], ['kb', 'ident'], [PR(bk)])
                P.copy('dve', kT[s][:, :, jj * 128:(jj + 1) * 128], psb(bk)[:, :].rearrange("p (h t) -> p h t", h=8), [PR(bk)], ['kT%d' % s])
                P.tt('dve', biasb[s][:, jj, 0, :], crefS[b][:, :], ckall[:, jb + jj, :], ALU.subtract, ['crefS%d' % b, 'ckall'], ['biasb%d' % s])

        def mk_stream_unit(kind, b, cj, s, h, first, fc):
            st_ = {}

            def A():
                if first:
                    prep_stream(kind, b, cj, s)
                pi = ptc[0]; ptc[0] ^= 1
                if kind == 'prev':
                    pt_ = PT[pi]; ptr = 'PT%d' % pi
                else:
                    pt_ = PTS[b][pi]; ptr = 'PTS%d_%d' % (b, pi)
                st_['pt'] = (pt_, ptr)
                for jj in range(2):
                    bk = P.bank()
                    P.mm(ps[bk][:, 0:NT], kT[s][:, h, jj * 128:(jj + 1) * 128], qT[:, h, 0:NT], True, True, ['kT%d' % s, 'qT'], [PR(bk)])
                    if kind == 'prev':
                        P.act(pt_[:, jj, 0:512], ps[bk][:, 0:512], AF.Exp, [PR(bk), 'biasb%d' % s], [ptr],
                              bias=biasb[s][:, jj, 0, h:h + 1], scale=FOX_SCALE)
                    else:
                        P.act(pt_[:, jj, b * 32:(b + 1) * 32], ps[bk][:, b * 32:(b + 1) * 32], AF.Exp, [PR(bk), 'biasb%d' % s], [ptr],
                              bias=biasb[s][:, jj, 0, h:h + 1], scale=FOX_SCALE)

            def B():
                pt_, ptr = st_['pt']
                for ip in range(0, NB, 2):
                    n = min(2, NB - ip)
                    bk = P.bank()
                    for il in range(n):
                        i = ip + il
                        for jj in range(2):
                            P.mm(ps[bk][0:TB, il * 129:(il + 1) * 129], pt_[:, jj, i * TB:(i + 1) * TB], vaug[s][:, jj, h, :], jj == 0, jj == 1,
                                 [ptr, 'vaug%d_%d' % (s, jj)], [PR(bk)])
                    if fc:
                        P.copy('dve', oacc[0:TB, ip:ip + n, h, :], ps[bk][0:TB, 0:n * 129].rearrange("p (i d) -> p i d", i=n), [PR(bk)], ['A'])
                    else:
                        P.tt('dve', oacc[0:TB, ip:ip + n, h, :], oacc[0:TB, ip:ip + n, h, :], ps[bk][0:TB, 0:n * 129].rearrange("p (i d) -> p i d", i=n),
                             ALU.add, ['A', PR(bk)], ['A'])
            return A, B

        def mk_own_unit(oc, h, first, fc):
            st_ = {}

            def A():
                if first:
                    for jj in range(2):
                        blk = 2 * oc + jj
                        P.tt('dve', biaso[:, blk, 0, :], cref[:, 0, :], ckall[:, ti * 4 + blk, :], ALU.subtract, ['cref', 'ckall'], ['biaso'])
                pi = ptc[0]; ptc[0] ^= 1
                pt_ = PT[pi]; ptr = 'PT%d' % pi
                st_['pt'] = (pt_, ptr)
                for jj in range(2):
                    blk = 2 * oc + jj
                    nq = 4 - blk
                    bk = P.bank()
                    P.mm(ps[bk][:, 0:nq * 128], kTown[:, h, blk * 128:(blk + 1) * 128], qT[:, h, blk * 128:512], True, True, ['kTown', 'qT'], [PR(bk)])
                    P.act(pt_[:, jj, blk * 128:512], ps[bk][:, 0:nq * 128], AF.Exp, [PR(bk), 'biaso'], [ptr],
                          bias=biaso[:, blk, 0, h:h + 1], scale=FOX_SCALE)
                    P.tt('pool', pt_[:, jj, blk * 128:(blk + 1) * 128], pt_[:, jj, blk * 128:(blk + 1) * 128], maskP[:, :], ALU.mult, [ptr, 'maskP'], [ptr])

            def B():
                pt_, ptr = st_['pt']
                for ip in range(2 * oc, 4, 2):
                    bk = P.bank()
                    for il in range(2):
                        i = ip + il
                        jjs = [jj for jj in range(2) if 2 * oc + jj <= i]
                        for n_, jj in enumerate(jjs):
                            P.mm(ps[bk][:, il * 129:(il + 1) * 129], pt_[:, jj, i * 128:(i + 1) * 128], vown[:, 2 * oc + jj, h, :], n_ == 0, n_ == len(jjs) - 1,
                                 [ptr, 'vown'], [PR(bk)])
                    if fc:
                        P.copy('dve', oacc[:, ip:ip + 2, h, :], ps[bk][:, 0:258].rearrange("p (i d) -> p i d", i=2), [PR(bk)], ['A'])
                    else:
                        P.tt('dve', oacc[:, ip:ip + 2, h, :], oacc[:, ip:ip + 2, h, :], ps[bk][:, 0:258].rearrange("p (i d) -> p i d", i=2),
                             ALU.add, ['A', PR(bk)], ['A'])
            return A, B

        def mk_ownS_unit(h, fc):
            st_ = {}

            def A():
                pi = ptc[0]; ptc[0] ^= 1
                pt_ = PT[pi]; ptr = 'PT%d' % pi
                st_['pt'] = (pt_, ptr)
                bk = P.bank()
                P.mm(ps[bk][0:64, 0:64], kTown[:, h, 0:64], qT[:, h, 0:64], True, True, ['kTown', 'qT'], [PR(bk)])
                P.act(pt_[0:64, 0, 0:64], ps[bk][0:64, 0:64], AF.Exp, [PR(bk), 'biaso'], [ptr], bias=biaso[0:64, 0, 0, h:h + 1], scale=FOX_SCALE)
                P.tt('pool', pt_[0:64, 0, 0:64], pt_[0:64, 0, 0:64], maskS[:, :], ALU.mult, [ptr, 'maskS'], [ptr])

            def B():
                pt_, ptr = st_['pt']
                bk = P.bank()
                P.mm(ps[bk][0:64, 0:129], pt_[0:64, 0, 0:64], vown[0:64, 0, h, :], True, True, [ptr, 'vown'], [PR(bk)])
                if fc:
                    P.copy('dve', oacc[0:64, 0, h, :], ps[bk][0:64, 0:129], [PR(bk)], ['A'])
                else:
                    P.tt('dve', oacc[0:64, 0, h, :], oacc[0:64, 0, h, :], ps[bk][0:64, 0:129], ALU.add, ['A', PR(bk)], ['A'])
            return A, B

        for ci, (kind, b, cj) in enumerate(chunks):
            fc = (ci == 0)
            if kind in ('prev', 'cache'):
                s = cslot[0]; cslot[0] ^= 1
                for h in range(8):
                    units.append(mk_stream_unit(kind, b, cj, s, h, h == 0, fc))
            elif kind == 'own':
                for h in range(8):
                    units.append(mk_own_unit(cj, h, h == 0, fc))
            else:
                for h in range(8):
                    units.append(mk_ownS_unit(h, fc))
        for u in range(len(units)):
            units[u][0]()
            if u > 0:
                units[u - 1][1]()
            if u % 2 == 1 and sgu_q:
                sgu_q.pop(0)()
        units[-1][1]()
        while sgu_q:
            sgu_q.pop(0)()
        while sdefer:
            sdefer.pop(0)()
        P.dma('sp', 'S_resid', resid[0:TB, 0:NB, :], xsrc.rearrange("(b p) d -> p b d", p=TB), [],
              ['resid%d' % tb for tb in range(NB)])

        for i in range(NB):
            P.op('dve', lambda e, i=i: e.reciprocal(out=rlo[0:TB, :], in_=oacc[0:TB, i, :, 128]), ['A'], ['rlo'])
            t_ = tokb[tki[0]]; tr_ = 'tokb%d' % tki[0]; tki[0] ^= 1
            P.tt('dve', t_[0:TB, :].rearrange("p (h d) -> p h d", h=8), oacc[0:TB, i, :, 0:128],
                 rlo[0:TB, :].unsqueeze(2).to_broadcast([TB, 8, 128]), ALU.mult, ['A', 'rlo'], [tr_])
            if sdefer:
                sdefer.pop(0)()
            sdefer.append(lambda t_=t_, tr_=tr_, i=i: transpose_rows_to(t_, tr_, TB, 8, B_v, 'B', 0, i * TB))
        while sdefer:
            sdefer.pop(0)()

        def resid_evac(first):
            def evac(tb, bk, p):
                dst = resid[0:TB, tb, p * 256:(p + 1) * 256]
                if first:
                    P.stt(dst, dst, ALPHA, ps[bk][0:TB, 0:256], ALU.mult, ALU.add, ['resid%d' % tb, PR(bk)], ['resid%d' % tb])
                else:
                    P.tt('dve', dst, dst, ps[bk][0:TB, 0:256], ALU.add, ['resid%d' % tb, PR(bk)], ['resid%d' % tb])
            return evac

        halves = [[0]] if sample else [[0, 1], [2, 3]]
        tokh = [(0, NT)] if sample else [(0, 256), (256, 512)]
        xb_of = {}

        def ln_nonpe(hv):
            for tb in hv:
                layer_norm(tb, TB, 2048, resid[0:TB, tb, :], 'resid%d' % tb, 4)
            for tb in hv:
                src = resid[0:TB, tb, :]
                rr = 'resid%d' % tb
                P.tt('pool', src, src, lnG[0:TB, :], ALU.mult, [rr, 'lnG'], [rr])
                P.tt('dve', src, src, lnB[0:TB, :], ALU.add, [rr, 'lnB'], [rr])
                x_ = xb[xbi[0]]; xr = 'xb%d' % xbi[0]; xbi[0] ^= 1
                P.copy('act', x_[0:TB, :], src, [rr], [xr])
                xb_of[tb] = (x_, xr)

        def ln_tr(hv, dst_v, dst_res):
            for tb in hv:
                x_, xr = xb_of[tb]
                transpose_rows_to(x_, xr, TB, 16, dst_v, dst_res, 0, tb * TB)

        def ln3_compute():
            for tb in range(NB):
                layer_norm(tb, TB, 2048, resid[0:TB, tb, :], 'resid%d' % tb, 4)
            for tb in range(NB):
                src = resid[0:TB, tb, :]
                rr = 'resid%d' % tb
                P.tt('pool', src, src, lnG[0:TB, :], ALU.mult, [rr, 'lnG'], [rr])
                P.tt('dve', src, src, lnB[0:TB, :], ALU.add, [rr, 'lnB'], [rr])

        def ln3_store():
            for tb in range(NB):
                store(y_o[tb * TB:(tb + 1) * TB, :], resid[0:TB, tb, :], ['resid%d' % tb], 'S_y%d' % tb)
            flush_stores()

        def tok_pass(name, r0, src_v, src_res, hv, evacf):
            for p in range(8):
                wv, wres = get_piece(name, r0, p * 256)
                tokmajor_piece(wv, wres, src_v, src_res, hv, TB, lambda tb, bk, p=p: evacf(tb, bk, p))

        def feat_pass(name, c0, src_v, src_res, t0, t1, evacf):
            for p in range(8):
                wv, wres = get_piece(name, 0, c0 + p * 256)
                featmajor_piece(wv, wres, src_v, src_res, t1, lambda lc, bk, p=p: evacf(lc, bk, p, t0, t1), t0=t0)

        def evac_qm(lc, bk, p, t0, t1):
            P.copy('act', B_v[:, 2 * p + lc, t0:t1], ps[bk][:, 0:t1 - t0], [PR(bk)], ['B'])

        def evac_up(lc, bk, p, t0, t1):
            t_ = tmpf[lc]; tr_ = 'rl%d' % lc
            P.act(t_[:, 0:t1 - t0], ps[bk][:, 0:t1 - t0], AF.Relu, [PR(bk)], [tr_])
            P.tt('dve', A_v[:, 2 * p + lc, t0:t1], t_[:, 0:t1 - t0], t_[:, 0:t1 - t0], ALU.mult, [tr_], ['A'])

        ev1 = resid_evac(True)
        load_ln(ln_g[0], ln_b[0], 2048)
        for hi, hv in enumerate(halves):
            tok_pass('wout', 0, B_v, 'B', hv, ev1)
            if hi > 0:
                ln_tr(halves[hi - 1], A_v, 'A')
            ln_nonpe(hv)
            if hi > 0:
                feat_pass('wmq', 0, A_v, 'A', tokh[hi - 1][0], tokh[hi - 1][1], evac_qm)
        ln_tr(halves[-1], A_v, 'A')
        feat_pass('wmq', 0, A_v, 'A', tokh[-1][0], tokh[-1][1], evac_qm)
        load_ln(ln_g[1], ln_b[1], 2048)
        msets = [(1, 0, 32), (2, 32, 64)] if sample else [(0, 0, 512)]
        for (ms, c0, c1) in msets:
            kv_, kres = load_piece(mkT_s[ms, :, :].rearrange("p (c m) -> p c m", c=16), 'w', ['mkTs%d' % ms])
            vv_, vres = load_piece(mv_s[ms, :, :].rearrange("(m p) n -> p m n", p=128), 'mv',
                                   (mvs0_res if ms == 0 else ['mvs%d' % ms]))
            for hd in range(4):
                pt_ = PT[ptc[0]]; ptr = 'PT%d' % ptc[0]; ptc[0] ^= 1
                for mb in range(2):
                    bk = P.bank()
                    for cc in range(4):
                        P.mm(ps[bk][:, 0:c1 - c0], kv_[:, hd * 4 + cc, mb * 128:(mb + 1) * 128], B_v[:, hd * 4 + cc, c0:c1], cc == 0, cc == 3, [kres, 'B'], [PR(bk)])
                    P.act(pt_[:, mb, c0:c1], ps[bk][:, 0:c1 - c0], AF.Exp, [PR(bk)], [ptr], scale=MEM_SCALE)
                bk = P.bank()
                for mb in range(2):
                    P.mm(ps[bk][:, 0:c1 - c0], onesb[:, :], pt_[:, mb, c0:c1], mb == 0, mb == 1, ['onesb', ptr], [PR(bk)])
                r_ = rl[hd % 2]; rr_ = 'rl%d' % (hd % 2)
                P.op('dve', lambda e, r_=r_, bk=bk: e.reciprocal(out=r_[:, 0:c1 - c0], in_=ps[bk][:, 0:c1 - c0]), [PR(bk)], [rr_])
                for cc in range(4):
                    bk = P.bank()
                    ch = hd * 4 + cc
                    for mb in range(2):
                        P.mm(ps[bk][:, 0:c1 - c0], vv_[:, mb, ch * 128:(ch + 1) * 128], pt_[:, mb, c0:c1], mb == 0, mb == 1, [vres, ptr], [PR(bk)])
                    P.tt('dve', A_v[:, ch, c0:c1], ps[bk][:, 0:c1 - c0], r_[:, 0:c1 - c0], ALU.mult, [PR(bk), rr_], ['A'])
        for hi, hv in enumerate(halves):
            tok_pass('wmo', 0, A_v, 'A', hv, ev1)
            if hi > 0:
                ln_tr(halves[hi - 1], B_v, 'B')
            ln_nonpe(hv)
            if hi > 0:
                feat_pass('wup', 0, B_v, 'B', tokh[hi - 1][0], tokh[hi - 1][1], evac_up)
        ln_tr(halves[-1], B_v, 'B')
        feat_pass('wup', 0, B_v, 'B', tokh[-1][0], tokh[-1][1], evac_up)
        if False and nxt is not None:
            nsrc, nTB, nNB = nxt
            for tb in range(min(2, nNB)):
                x_ = xb[xbi[0]]; xr = 'xb%d' % xbi[0]; xbi[0] ^= 1
                P.dma('pool', 'S_' + xr, x_[0:nTB, :], nsrc[tb * nTB:(tb + 1) * nTB, :], [], [xr])
                xpre.append((x_, xr))
        load_ln(ln_g[2], ln_b[2], 2048)
        for g in range(4):
            if g > 0:
                feat_pass('wup', g * 2048, B_v, 'B', 0, NT, evac_up)
            evg = resid_evac(g == 0)
            tok_pass('wdn', g * 2048, A_v, 'A', list(range(NB)), evg)
        pending.append(ln3_compute)
        pending2.append(ln3_store)

    onesf = sb("onesf", [128, 32], F32)
    onesb_f = onesf
    P.memset('pool', onesf[:, :], 1.0, ['onesf'])

    mem_prologue()
    for ti in range(NTILE):
        nxt = (x_p[(ti + 1) * 512:(ti + 2) * 512, :], 128, 4) if ti + 1 < NTILE else None
        run_tile(ti, False, nxt)
    mem_prologue_sample()
    run_tile(0, True)
    while pending:
        pending.pop(0)()
    while pending2:
        pending2.pop(0)()

    flush_stores()
    final = {}
    for (s, v, _) in out_events:
        final[s] = max(final.get(s, 0), v)
    for s in ['S_mkT']:
        final[s] = P.cnt[s]
    fw = [(P.sem(s), v) for s, v in final.items()]

    with nc.Block() as block:
        @block.sync
        def _(e):
            for f in P.q['sp']:
                f(e)
            for s_, v_ in fw:
                e.wait_ge(s_, v_)

        @block.tensor
        def _(e):
            for f in P.q['pe']:
                f(e)

        @block.vector
        def _(e):
            for f in P.q['dve']:
                f(e)

        @block.scalar
        def _(e):
            for f in P.q['act']:
                f(e)

        @block.gpsimd
        def _(e):
            for f in P.q['pool']:
                f(e)
    es.close()
    return nc, P


_CACHE = {}


def kernel(**inp):
    f32 = lambda a: np.ascontiguousarray(np.asarray(a, dtype=np.float32))
    if 'nc' not in _CACHE:
        _CACHE['nc'] = build()[0]
    nc = _CACHE['nc']
    shared = {
        "w_in": f32(inp["w_in"][0]), "b_f": f32(inp["b_f"]), "sgu_g": f32(inp["sgu_ln_g"]), "sgu_b": f32(inp["sgu_ln_b"]),
        "w_s": f32(inp["w_s"][0]), "b_s": f32(inp["b_s"][0]), "w_out": f32(inp["w_out"][0]),
        "ln1_g": f32(inp["ln1_g"]), "ln1_b": f32(inp["ln1_b"]), "ln2_g": f32(inp["ln2_g"]), "ln2_b": f32(inp["ln2_b"]),
        "ln3_g": f32(inp["ln3_g"]), "ln3_b": f32(inp["ln3_b"]),
        "w_mq": f32(inp["w_mq"][0]), "w_mk": f32(inp["w_mk"][0]), "w_mv": f32(inp["w_mv"][0]), "w_mo": f32(inp["w_mo"][0]),
        "w_up": f32(inp["w_up"][0]), "w_down": f32(inp["w_down"][0]),
    }
    in_maps = []
    for c in range(8):
        m = dict(shared)
        m["x_p"] = f32(inp["x_prompt"][c])
        m["x_s"] = f32(inp["x_sample"][2 * c:2 * c + 2]).reshape(64, D)
        m["mem"] = f32(inp["mem_prompt"][c])
        m["ck"] = f32(inp["cache_fox_k"][0, 2 * c:2 * c + 2]).reshape(2, SEQ, 1024)
        m["cv"] = f32(inp["cache_fox_v"][0, 2 * c:2 * c + 2]).reshape(2, SEQ, 1024)
        m["cl"] = f32(inp["cache_fox_logf"][0, 2 * c:2 * c + 2])
        m["cmk"] = f32(inp["cache_mem_k"][0, 2 * c:2 * c + 2]).reshape(2, 256, D)
        m["cmv"] = f32(inp["cache_mem_v"][0, 2 * c:2 * c + 2]).reshape(2, 256, D)
        in_maps.append(m)
    res = run_bass_kernel_spmd(nc, in_maps, core_ids=list(range(8)))
    R = res.results
    cat = lambda k: np.stack([np.asarray(R[c][k]) for c in range(8)], axis=0)
    y_prompt = cat("y_p")
    y_sample = cat("y_s").reshape(16, 32, D)
    fkp = cat("fk_p").reshape(1, 8, SEQ, 8, 128)
    fvp = cat("fv_p").reshape(1, 8, SEQ, 8, 128)
    flp = cat("fl_p").reshape(1, 8, SEQ, 8)
    mkp = cat("mk_p").reshape(1, 8, 256, 4, 512)
    mvp = cat("mv_p").reshape(1, 8, 256, 4, 512)
    fks = cat("fk_s").reshape(1, 16, 32, 8, 128)
    fvs = cat("fv_s").reshape(1, 16, 32, 8, 128)
    fls = cat("fl_s").reshape(1, 16, 32, 8)
    gvs = cat("gv_s").reshape(1, 16, 32, 1024)
    return (y_prompt, y_sample, fkp, fvp, flp, mkp, mvp, fks, fvs, fls, gvs)
```

```python
import contextlib
import numpy as np
import concourse.bass as bass
import concourse.mybir as mybir
from concourse.bass_utils import run_bass_kernel_spmd

F32 = mybir.dt.float32
BF16 = mybir.dt.bfloat16
AF = mybir.ActivationFunctionType
ALU = mybir.AluOpType

D = 2048
SEQ = 4096
NTILE = 8
ALPHA = 2.0 ** 0.25
EPS = 1e-5
FOX_SCALE = 128.0 ** -0.5
MEM_SCALE = 512.0 ** -0.5
ENGMAP = {'pe': 'tensor', 'act': 'scalar', 'dve': 'vector', 'pool': 'gpsimd', 'sp': 'sync'}


class Prog:
    def __init__(self, nc, es):
        self.nc = nc
        self.es = es
        self.q = {e: [] for e in ENGMAP}
        self.cnt = {}
        self.sems = {}
        self.lastw = {}
        self.readers = {}
        self.seen = {e: {} for e in ENGMAP}
        self.bank_i = 0
        self.nops = 0

    def sem(self, name):
        if name not in self.sems:
            self.sems[name] = self.es.enter_context(self.nc.semaphore(name))
        return self.sems[name]

    def op(self, eng, fn, reads=(), writes=(), dma_sem=None):
        deps = {}

        def add(ev):
            if ev is None:
                return
            s, v, e = ev
            if s not in deps or deps[s][0] < v:
                deps[s] = (v, e)
        for r in reads:
            add(self.lastw.get(r))
        for w in writes:
            add(self.lastw.get(w))
            for s, (v, e) in self.readers.get(w, {}).items():
                add((s, v, e))
        waits = []
        for s, (v, e) in deps.items():
            if eng == 'pe' and e == 'pe':
                continue
            if self.seen[eng].get(s, 0) >= v:
                continue
            self.seen[eng][s] = v
            waits.append((self.sem(s), v))
        if dma_sem:
            semname, inc, pe = dma_sem, 16, 'dma'
        else:
            semname, inc, pe = 'E_' + eng, 1, eng
        self.cnt[semname] = self.cnt.get(semname, 0) + inc
        val = self.cnt[semname]
        ev = (semname, val, pe)
        for r in reads:
            self.readers.setdefault(r, {})[semname] = (val, pe)
        for w in writes:
            self.lastw[w] = ev
            self.readers[w] = {}
        sh = self.sem(semname)

        def emit(e):
            for s_, v_ in waits:
                e.wait_ge(s_, v_)
            fn(e).then_inc(sh, inc)
        self.q[eng].append(emit)
        self.nops += 1
        return ev

    def mm(self, out, lhsT, rhs, start, stop, reads, writes):
        return self.op('pe', lambda e: e.matmul(out, lhsT=lhsT, rhs=rhs, start=start, stop=stop), reads, writes)

    def tr(self, out, in_, ident, reads, writes):
        return self.op('pe', lambda e: e.transpose(out, in_, ident), reads, writes)

    def act(self, out, in_, func, reads, writes, bias=None, scale=None):
        kw = {}
        if bias is not None:
            kw['bias'] = bias
        if scale is not None:
            kw['scale'] = scale
        return self.op('act', lambda e: e.activation(out=out, in_=in_, func=func, **kw), reads, writes)

    def tt(self, eng, out, in0, in1, op, reads, writes):
        return self.op(eng, lambda e: e.tensor_tensor(out=out, in0=in0, in1=in1, op=op), reads, writes)

    def ts(self, eng, out, in0, s1, s2, op0, op1, reads, writes):
        if op1 is None:
            return self.op(eng, lambda e: e.tensor_scalar(out=out, in0=in0, scalar1=s1, scalar2=None, op0=op0), reads, writes)
        return self.op(eng, lambda e: e.tensor_scalar(out=out, in0=in0, scalar1=s1, scalar2=s2, op0=op0, op1=op1), reads, writes)

    def stt(self, out, in0, scalar, in1, op0, op1, reads, writes):
        return self.op('dve', lambda e: e.scalar_tensor_tensor(out=out, in0=in0, scalar=scalar, in1=in1, op0=op0, op1=op1), reads, writes)

    def copy(self, eng, out, in_, reads, writes):
        if eng == 'act':
            return self.act(out, in_, AF.Copy, reads, writes)
        return self.op(eng, lambda e: e.tensor_copy(out=out, in_=in_), reads, writes)

    def memset(self, eng, ap, val, writes):
        return self.op(eng, lambda e: e.memset(ap, val), (), writes)

    def dma(self, q, semname, out, in_, reads, writes, slow=False):
        if slow:
            return self.op(q, lambda e: e.dma_start(out=out, in_=in_, allow_slow_non_contiguous=True), reads, writes, dma_sem=semname)
        return self.op(q, lambda e: e.dma_start(out=out, in_=in_), reads, writes, dma_sem=semname)

    def bank(self):
        b = self.bank_i
        self.bank_i = (self.bank_i + 1) % 8
        return b


def build():
    nc = bass.Bass("TRN2", target_bir_lowering=False)
    es = contextlib.ExitStack()
    P = Prog(nc, es)

    def din(name, shape):
        return nc.dram_tensor(name, shape, F32, kind="ExternalInput").ap()

    def dout(name, shape):
        return nc.dram_tensor(name, shape, F32, kind="ExternalOutput").ap()

    def dscr(name, shape):
        return nc.dram_tensor(name, shape, BF16, kind="Internal").ap()

    x_p = din("x_p", [SEQ, D]); x_s = din("x_s", [64, D]); mem = din("mem", [256, D])
    ck = din("ck", [2, SEQ, 1024]); cv = din("cv", [2, SEQ, 1024]); cl = din("cl", [2, SEQ, 8])
    cmk = din("cmk", [2, 256, D]); cmv = din("cmv", [2, 256, D])
    w_in = din("w_in", [D, 5128]); b_f = din("b_f", [1, 8])
    sgu_g = din("sgu_g", [1, 1024]); sgu_b = din("sgu_b", [1, 1024])
    w_s = din("w_s", [4, 128, 128]); b_s = din("b_s", [4, 128])
    w_out = din("w_out", [D, D])
    ln_g = [din("ln%d_g" % i, [1, D]) for i in (1, 2, 3)]
    ln_b = [din("ln%d_b" % i, [1, D]) for i in (1, 2, 3)]
    w_mq = din("w_mq", [D, D]); w_mk = din("w_mk", [D, D]); w_mv = din("w_mv", [D, D]); w_mo = din("w_mo", [D, D])
    w_up = din("w_up", [D, 8192]); w_down = din("w_down", [8192, D])

    y_p = dout("y_p", [SEQ, D]); y_s = dout("y_s", [64, D])
    fk_p = dout("fk_p", [SEQ, 1024]); fv_p = dout("fv_p", [SEQ, 1024]); fl_p = dout("fl_p", [SEQ, 8])
    mk_p = dout("mk_p", [256, D]); mv_p = dout("mv_p", [256, D])
    fk_s = dout("fk_s", [64, 1024]); fv_s = dout("fv_s", [64, 1024]); fl_s = dout("fl_s", [64, 8])
    gv_s = dout("gv_s", [64, 1024])

    mkT_s = dscr("mkT_s", [3, 128, 4096]); mv_s = dscr("mv_s", [3, 256, D])
    pscr = dscr("pscr", [108, 128, 4096])
    kTs = dscr("kTs", [16, 128, 2048]); vs = dscr("vs", [16, 128, 2064])

    def sb(name, shape, dt):
        return es.enter_context(nc.sbuf_tensor(name, shape, dt))

    wbuf = [sb("wbuf%d" % i, [128, 4096], BF16) for i in range(3)]
    wf = sb("wf", [128, 16, 8], BF16)
    A_raw = sb("A_raw", [128, 8256], BF16)
    B_raw = sb("B_raw", [128, 8192], BF16)
    A_v = A_raw[:, 0:8192].rearrange("p (c t) -> p c t", c=16)
    B_v = B_raw[:, :].rearrange("p (c t) -> p c t", c=16)
    oacc = A_raw[:, :].bitcast(F32).rearrange("p (i h d) -> p i h d", i=4, h=8)
    resid = sb("resid", [128, 4, D], F32)
    qT = sb("qT", [128, 8, 512], BF16)
    gn = sb("gn", [128, 4, 1024], BF16)
    gn_flat = gn[:, :, :].rearrange("p a b -> p (a b)")
    mkTst = gn_flat.rearrange("p (c m) -> p c m", c=16)
    tokb = [sb("tokb%d" % i, [128, 1024], BF16) for i in range(2)]
    kb = sb("kb", [128, 2, 1024], BF16)
    kT = [sb("kT%d" % i, [128, 8, 256], BF16) for i in range(2)]
    vaug = [sb("vaug%d" % i, [128, 2, 8, 129], BF16) for i in range(2)]
    kTown = sb("kTown", [128, 8, 512], BF16)
    vown = sb("vown", [128, 4, 8, 129], BF16)
    PT = [sb("PT%d" % i, [128, 2, 512], BF16) for i in range(2)]
    PTS = [[sb("PTS%d_%d" % (b, i), [128, 2, 64], BF16) for i in range(2)] for b in range(2)]
    lnG = sb("lnG", [128, D], F32); lnB = sb("lnB", [128, D], F32)
    st = [sb("st%d" % i, [128, 256], F32) for i in range(6)]
    kbs = [sb("kbs%d" % i, [128, 256], BF16) for i in range(2)]
    xb = [sb("xb%d" % i, [128, D], BF16) for i in range(2)]
    scr4 = sb("scr4", [128, 1024], F32)
    gst = scr4
    rl = [scr4[:, 0:512], scr4[:, 512:1024]]
    tmpf = rl
    ident = sb("ident", [128, 128], BF16); maskP = sb("maskP", [128, 128], BF16); maskS = sb("maskS", [64, 64], BF16)
    onesb = sb("onesb", [128, 128], BF16)
    triu = sb("triu", [128, 128], F32); sel127 = sb("sel127", [128, 128], F32)
    selS = [sb("selS%d" % b, [64, 128], F32) for b in range(2)]
    selOwn = sb("selOwn", [64, 64], F32); tri64 = sb("tri64", [64, 64], F32)
    wsraw = sb("wsraw", [128, 4, 128], F32); wsrb = sb("wsrb", [128, 4, 128], BF16)
    wsT = sb("wsT", [128, 4, 128], BF16)
    wsrawS = sb("wsrawS", [64, 4, 64], F32); wsrbS = sb("wsrbS", [64, 4, 64], BF16); wsTS = sb("wsTS", [64, 4, 64], BF16)
    bscol = sb("bscol", [128, 4], F32); bscolS = sb("bscolS", [64, 4], F32)
    bfb = sb("bfb", [128, 8], F32)
    ckall = sb("ckall", [128, 64, 8], F32)
    clsb = sb("clsb", [128, 32, 8], F32)
    Lloc = sb("Lloc", [128, 32, 8], F32); totb = sb("totb", [128, 32, 8], F32); pref = sb("pref", [128, 32, 8], F32)
    carry = sb("carry", [128, 8], F32)
    cendS = [sb("cendS%d" % b, [128, 8], F32) for b in range(2)]
    crefS = [sb("crefS%d" % b, [128, 8], F32) for b in range(2)]
    carryrows = sb("carryrows", [64, 8], F32)
    cref = sb("cref", [128, 4, 8], F32)
    biasb = [sb("biasb%d" % i, [128, 2, 4, 8], F32) for i in range(2)]
    biaso = sb("biaso", [128, 4, 4, 8], F32)
    zf = sb("zf", [128, 8], F32); zf4 = sb("zf4", [128, 4, 8], F32); lfall = sb("lfall", [128, 4, 8], F32)
    stats = sb("stats", [128, 4, 4, 6], F32); mvar = sb("mvar", [128, 4, 2], F32); rstd = sb("rstd", [128, 4, 2], F32)
    rlo = sb("rlo", [128, 8], F32)
    ones1 = sb("ones1", [128, 1], F32)

    ps = [es.enter_context(nc.psum_tensor("ps%d" % i, [128, 512], F32)) for i in range(8)]

    def psb(bk):
        return ps[bk][:, :].bitcast(BF16)

    def PR(bk):
        return 'ps%d' % bk

    def aff(out, in_, pattern, cmp, base, cm, writes):
        return P.op('pool', lambda e: e.affine_select(out=out, in_=in_, pattern=pattern, compare_op=cmp, fill=0.0,
                                                      base=base, channel_multiplier=cm), writes, writes)

    P.memset('pool', ident[:, :], 1.0, ['ident'])
    aff(ident[:, :], ident[:, :], [[-1, 128]], ALU.is_equal, 0, 1, ['ident'])
    P.memset('pool', maskP[:, :], 1.0, ['maskP'])
    aff(maskP[:, :], maskP[:, :], [[1, 128]], ALU.is_ge, 0, -1, ['maskP'])
    P.memset('pool', maskS[:, :], 1.0, ['maskS'])
    aff(maskS[:, :], maskS[:, :], [[1, 64]], ALU.is_ge, 0, -1, ['maskS'])
    P.memset('pool', maskS[0:32, 32:64], 0.0, ['maskS'])
    P.memset('pool', onesb[:, :], 1.0, ['onesb'])
    P.memset('pool', ones1[:, :], 1.0, ['ones1'])
    P.memset('pool', triu[:, :], 1.0, ['triu'])
    aff(triu[:, :], triu[:, :], [[1, 128]], ALU.is_ge, 0, -1, ['triu'])
    P.memset('pool', tri64[:, :], 1.0, ['tri64'])
    aff(tri64[:, :], tri64[:, :], [[1, 64]], ALU.is_ge, 0, -1, ['tri64'])
    P.memset('pool', tri64[0:32, 32:64], 0.0, ['tri64'])
    P.memset('pool', sel127[:, :], 1.0, ['sel127'])
    aff(sel127[:, :], sel127[:, :], [[0, 128]], ALU.is_equal, -127, 1, ['sel127'])
    for b in range(2):
        P.memset('pool', selS[b][:, :], 1.0, ['selS%d' % b])
        aff(selS[b][:, :], selS[b][:, :], [[0, 128]], ALU.is_equal, -(32 * b + 31), 1, ['selS%d' % b])
    P.memset('pool', selOwn[:, :], 1.0, ['selOwn'])
    aff(selOwn[:, 0:32], selOwn[:, 0:32], [[0, 32]], ALU.is_equal, -31, 1, ['selOwn'])
    aff(selOwn[:, 32:64], selOwn[:, 32:64], [[0, 32]], ALU.is_equal, -63, 1, ['selOwn'])
    for i in range(2):
        P.memset('pool', vaug[i][:, :, :, 128:129], 1.0, ['vaug%d_0' % i, 'vaug%d_1' % i])
        for b in range(2):
            P.memset('pool', PTS[b][i][:, :, :], 0.0, ['PTS%d_%d' % (b, i)])
    P.memset('pool', vown[:, :, :, 128:129], 1.0, ['vown'])

    P.dma('sp', 'S_misc1', bfb[:, :], b_f[0, :].partition_broadcast(128), [], ['bfb'])
    P.dma('sp', 'S_misc2', bscol[:, :], b_s.rearrange("g t -> t g"), [], ['bscol'], slow=True)
    P.dma('sp', 'S_misc3', bscolS[0:32, :], b_s[:, 0:32].rearrange("g t -> t g"), [], ['bscolS'], slow=True)
    P.dma('sp', 'S_misc4', bscolS[32:64, :], b_s[:, 0:32].rearrange("g t -> t g"), [], ['bscolS'], slow=True)
    P.dma('sp', 'S_misc5', wsraw[:, :, :], w_s.rearrange("g t s -> t g s"), [], ['wsraw'])
    P.memset('pool', wsrawS[:, :, :], 0.0, ['wsrawS'])
    P.dma('sp', 'S_misc6', wsrawS[0:32, :, 0:32], w_s[:, 0:32, 0:32].rearrange("g t s -> t g s"), [], ['wsrawS'])
    P.dma('sp', 'S_misc7', wsrawS[32:64, :, 32:64], w_s[:, 0:32, 0:32].rearrange("g t s -> t g s"), [], ['wsrawS'])
    aff(wsraw[:, :, :], wsraw[:, :, :], [[0, 4], [-1, 128]], ALU.is_ge, 0, 1, ['wsraw'])
    aff(wsrawS[:, :, :], wsrawS[:, :, :], [[0, 4], [-1, 64]], ALU.is_ge, 0, 1, ['wsrawS'])
    P.copy('dve', wsrb[:, :, :], wsraw[:, :, :], ['wsraw'], ['wsrb'])
    P.copy('dve', wsrbS[:, :, :], wsrawS[:, :, :], ['wsrawS'], ['wsrbS'])
    bk = P.bank()
    for g in range(4):
        P.tr(psb(bk)[:, g * 128:(g + 1) * 128], wsrb[:, g, :], ident[:, :], ['wsrb', 'ident'], [PR(bk)])
    P.copy('dve', wsT[:, :, :], psb(bk)[:, 0:512].rearrange("p (g t) -> p g t", g=4), [PR(bk)], ['wsT'])
    bk = P.bank()
    for g in range(4):
        P.tr(psb(bk)[0:64, g * 128:g * 128 + 64], wsrbS[:, g, :], ident[0:64, 0:64], ['wsrbS', 'ident'], [PR(bk)])
    P.copy('dve', wsTS[:, :, :], psb(bk)[0:64, 0:512].rearrange("p (g t) -> p g t", g=4)[:, :, 0:64], [PR(bk)], ['wsTS'])

    def conv(dst, src, res):
        P.dma('pool', 'S_cv_' + res, dst, src, [], [res])

    xbi = [0]
    mem_xb = []
    for mb in range(2):
        x_ = xb[xbi[0]]; xr = 'xb%d' % xbi[0]; xbi[0] ^= 1
        P.dma('pool', 'S_' + xr, x_[:, :], mem[mb * 128:(mb + 1) * 128, :], [], [xr])
        mem_xb.append((x_, xr))
    P.dma('pool', 'S_wf', wf[:, :, :], w_in[:, 3072:3080].rearrange("(c p) n -> p c n", p=128), [], ['wf'])
    for b in range(2):
        conv(mv_s[1 + b, :, :], cmv[b, :, :], 'mvs%d' % (1 + b))
    wslot = [0]
    SRC = {'win': w_in, 'wout': w_out, 'wmq': w_mq, 'wmo': w_mo, 'wup': w_up, 'wdn': w_down, 'wmk': w_mk, 'wmv': w_mv}
    piece_idx = {}

    def get_piece(name, r0, c0):
        key = (name, r0, c0)
        s = wslot[0]
        wslot[0] = (s + 1) % 3
        v = wbuf[s][:, :].rearrange("p (c n) -> p c n", c=16)
        if name in ('wmk', 'wmv') or key not in piece_idx:
            cc = c0 + 8 if (name == 'win' and c0 >= 3072) else c0
            ap = SRC[name][r0:r0 + 2048, cc:cc + 256].rearrange("(c p) n -> p c n", p=128)
            P.dma('pool', 'S_w%d' % s, v, ap, [], ['w%d' % s])
            if name not in ('wmk', 'wmv'):
                idx = len(piece_idx)
                piece_idx[key] = idx
                P.dma('sp', 'S_wb%d' % s, pscr[idx, :, :], wbuf[s][:, :], ['w%d' % s], ['pc%d' % idx])
        else:
            idx = piece_idx[key]
            P.dma('sp', 'S_w%d' % s, wbuf[s][:, :], pscr[idx, :, :], ['pc%d' % idx], ['w%d' % s])
        return v, 'w%d' % s

    def load_piece(src_ap, view, reads):
        s = wslot[0]
        wslot[0] = (s + 1) % 3
        if view == 'w':
            v = wbuf[s][:, :].rearrange("p (c n) -> p c n", c=16)
        elif view == 'mv':
            v = wbuf[s][:, :].rearrange("p (m n) -> p m n", m=2)
        P.dma('sp', 'S_w%d' % s, v, src_ap, reads, ['w%d' % s])
        return v, 'w%d' % s

    def wsrc(scr, r0, c0):
        return scr[r0:r0 + 2048, c0:c0 + 256].rearrange("(c p) n -> p c n", p=128)

    sti = [0]
    kbi = [0]
    tki = [0]
    evi = [0]

    def ev_eng():
        evi[0] ^= 1
        return 'act' if evi[0] else 'dve'

    def transpose_rows_to(src_tile, src_res, TB, nchunk, dst_v, dst_res, c0, t0):
        for h0 in range(0, nchunk, 8):
            bk = P.bank()
            for c8 in range(8):
                c = h0 + c8
                P.tr(psb(bk)[:, c8 * 128:c8 * 128 + TB], src_tile[0:TB, c * 128:(c + 1) * 128], ident[0:TB, 0:TB],
                     [src_res, 'ident'], [PR(bk)])
            P.copy(ev_eng(), dst_v[:, c0 + h0:c0 + h0 + 8, t0:t0 + TB],
                   psb(bk)[:, :].rearrange("p (c t) -> p c t", c=8)[:, :, 0:TB], [PR(bk)], [dst_res])

    def tokmajor_piece(wv, wres, src_v, src_res, NB, TB, evac):
        for tb in (range(NB) if isinstance(NB, int) else NB):
            bk = P.bank()
            for c in range(16):
                P.mm(ps[bk][0:TB, 0:256], src_v[:, c, tb * TB:(tb + 1) * TB], wv[:, c, :], c == 0, c == 15,
                     [wres, src_res], [PR(bk)])
            evac(tb, bk)

    def featmajor_piece(wv, wres, src_v, src_res, NT, evac, t0=0):
        for lc in range(2):
            bk = P.bank()
            for c in range(16):
                P.mm(ps[bk][:, 0:NT - t0], wv[:, c, lc * 128:(lc + 1) * 128], src_v[:, c, t0:NT], c == 0, c == 15,
                     [wres, src_res], [PR(bk)])
            evac(lc, bk)

    def layer_norm(tb, TB, width, src, res, nstat):
        for i in range(nstat):
            P.op('dve', lambda e, i=i: e.bn_stats(out=stats[0:TB, tb, i, :], in_=src[:, i * 512:(i + 1) * 512]), [res], ['stats%d' % tb])
        P.op('dve', lambda e: e.bn_aggr(out=mvar[0:TB, tb, :], in_=stats[0:TB, tb, 0:nstat, :].rearrange("p a b -> p (a b)")), ['stats%d' % tb], ['mvar%d' % tb])
        P.ts('dve', rstd[0:TB, tb, 0:1], mvar[0:TB, tb, 1:2], EPS, None, ALU.add, None, ['mvar%d' % tb], ['rstd%d' % tb])
        P.act(rstd[0:TB, tb, 0:1], rstd[0:TB, tb, 0:1], AF.Sqrt, ['rstd%d' % tb], ['rstd%d' % tb])
        P.op('dve', lambda e: e.reciprocal(out=rstd[0:TB, tb, 0:1], in_=rstd[0:TB, tb, 0:1]), ['rstd%d' % tb], ['rstd%d' % tb])
        P.ts('dve', rstd[0:TB, tb, 1:2], mvar[0:TB, tb, 0:1], rstd[0:TB, tb, 0:1], -1.0, ALU.mult, ALU.mult, ['mvar%d' % tb, 'rstd%d' % tb], ['nmr%d' % tb])
        P.act(src, src, AF.Identity, [res, 'rstd%d' % tb, 'nmr%d' % tb], [res], bias=rstd[0:TB, tb, 1:2], scale=rstd[0:TB, tb, 0:1])

    out_events = []

    store_defer = []

    def store(dst, src, reads, semname, res_w=()):
        def emit():
            out_events.append(P.dma('act', semname, dst, src, reads, list(res_w)))
        store_defer.append(emit)
        while len(store_defer) > 1:
            store_defer.pop(0)()

    def flush_stores():
        while store_defer:
            store_defer.pop(0)()

    def load_ln(gsrc, bsrc, width):
        P.dma('sp', 'S_lnG', lnG[:, 0:width], gsrc[0, :].partition_broadcast(128), [], ['lnG'])
        P.dma('sp', 'S_lnB', lnB[:, 0:width], bsrc[0, :].partition_broadcast(128), [], ['lnB'])

    def mem_prologue():
        for mb in range(2):
            x_, xr = mem_xb[mb]
            transpose_rows_to(x_, xr, 128, 16, A_v, 'A', 0, mb * 128)
        for p in range(8):
            wv, wres = get_piece('wmk', 0, p * 256)

            def evac(mb, bk, p=p):
                s_ = st[sti[0]]; sr = 'st%d' % sti[0]; sti[0] = (sti[0] + 1) % 6
                P.copy('act', s_[:, :], ps[bk][:, 0:256], [PR(bk)], [sr, PR(bk)])
                store(mk_p[mb * 128:(mb + 1) * 128, p * 256:(p + 1) * 256], s_[:, :], [sr], 'S_' + sr)
                k_ = kbs[kbi[0]]; kr = 'kbs%d' % kbi[0]; kbi[0] ^= 1
                P.copy('dve', k_[:, :], ps[bk][:, 0:256], [PR(bk)], [kr])
                b2 = P.bank()
                for lc in range(2):
                    P.tr(psb(b2)[:, lc * 128:(lc + 1) * 128], k_[:, lc * 128:(lc + 1) * 128], ident[:, :], [kr, 'ident'], [PR(b2)])
                P.copy('dve', mkTst[:, 2 * p:2 * p + 2, mb * 128:(mb + 1) * 128],
                       psb(b2)[:, 0:256].rearrange("p (c t) -> p c t", c=2), [PR(b2)], ['gn'])
            tokmajor_piece(wv, wres, A_v, 'A', 2, 128, evac)
        P.dma('sp', 'S_mkT', mkT_s[0, :, :], gn_flat, ['gn'], ['mkTs0'])
        for p in range(8):
            wv, wres = get_piece('wmv', 0, p * 256)

            def evac(mb, bk, p=p):
                s_ = st[sti[0]]; sr = 'st%d' % sti[0]; sti[0] = (sti[0] + 1) % 6
                P.copy('act', s_[:, :], ps[bk][:, 0:256], [PR(bk)], [sr, PR(bk)])
                store(mv_p[mb * 128:(mb + 1) * 128, p * 256:(p + 1) * 256], s_[:, :], [sr], 'S_' + sr)
                k_ = kbs[kbi[0]]; kr = 'kbs%d' % kbi[0]; kbi[0] ^= 1
                P.copy('dve', k_[:, :], ps[bk][:, 0:256], [PR(bk)], [kr])
                P.dma('sp', 'S_' + kr, mv_s[0, mb * 128:(mb + 1) * 128, p * 256:(p + 1) * 256], k_[:, :], [kr], ['mvs0_%d_%d' % (mb, p)])
            tokmajor_piece(wv, wres, A_v, 'A', 2, 128, evac)
        flush_stores()

    def mem_prologue_sample():
        for b in range(2):
            for mb in range(2):
                x_ = xb[xbi[0]]; xr = 'xb%d' % xbi[0]; xbi[0] ^= 1
                P.dma('pool', 'S_' + xr, x_[:, :], cmk[b, mb * 128:(mb + 1) * 128, :], [], [xr])
                transpose_rows_to(x_, xr, 128, 16, mkTst, 'gn', 0, mb * 128)
            P.dma('sp', 'S_mkT', mkT_s[1 + b, :, :], gn_flat, ['gn'], ['mkTs%d' % (1 + b)])

    mvs0_res = ['mvs0_%d_%d' % (mb, p) for mb in range(2) for p in range(8)]

    pending = []
    pending2 = []
    xpre = []

    def run_tile(ti, sample, nxt=None):
        NB, TB = (1, 64) if sample else (4, 128)
        NT = NB * TB
        xsrc = x_s if sample else x_p[ti * 512:(ti + 1) * 512, :]
        fk_o = fk_s if sample else fk_p[ti * 512:(ti + 1) * 512, :]
        fv_o = fv_s if sample else fv_p[ti * 512:(ti + 1) * 512, :]
        fl_o = fl_s if sample else fl_p[ti * 512:(ti + 1) * 512, :]
        y_o = y_s if sample else y_p[ti * 512:(ti + 1) * 512, :]
        tag = 's' if sample else 'p%d' % ti

        for tb in range(NB):
            if xpre:
                x_, xr = xpre.pop(0)
            else:
                x_ = xb[xbi[0]]; xr = 'xb%d' % xbi[0]; xbi[0] ^= 1
                P.dma('pool', 'S_' + xr, x_[0:TB, :], xsrc[tb * TB:(tb + 1) * TB, :], [], [xr])
            transpose_rows_to(x_, xr, TB, 16, A_v, 'A', 0, tb * TB)

        while pending:
            pending.pop(0)()
        for p in range(4):
            wv, wres = get_piece('win', 0, p * 256)

            def evac(lc, bk, p=p):
                P.copy('act', qT[:, 2 * p + lc, 0:NT], ps[bk][:, 0:NT], [PR(bk)], ['qT'])
            featmajor_piece(wv, wres, A_v, 'A', NT, evac)
        while pending2:
            pending2.pop(0)()
        kdefer = []
        for p in range(4):
            wv, wres = get_piece('win', 0, 1024 + p * 256)

            def evac(tb, bk, p=p):
                s_ = st[sti[0]]; sr = 'st%d' % sti[0]; sti[0] = (sti[0] + 1) % 6
                P.copy('act', s_[0:TB, :], ps[bk][0:TB, 0:256], [PR(bk)], [sr, PR(bk)])
                store(fk_o[tb * TB:(tb + 1) * TB, p * 256:(p + 1) * 256], s_[0:TB, :], [sr], 'S_' + sr,
                      ['fk_%s_%d_%d' % (tag, tb, p)])
                k_ = kbs[kbi[0]]; kr = 'kbs%d' % kbi[0]; kbi[0] ^= 1
                P.copy('pool', k_[0:TB, :], s_[0:TB, :], [sr], [kr])
                if kdefer:
                    kdefer.pop(0)()

                def trs(k_=k_, kr=kr, p=p, tb=tb):
                    b2 = P.bank()
                    for lc in range(2):
                        P.tr(psb(b2)[:, lc * 128:lc * 128 + TB], k_[0:TB, lc * 128:(lc + 1) * 128], ident[0:TB, 0:TB], [kr, 'ident'], [PR(b2)])
                    P.copy('dve', kTown[:, 2 * p:2 * p + 2, tb * TB:(tb + 1) * TB],
                           psb(b2)[:, 0:256].rearrange("p (c t) -> p c t", c=2)[:, :, 0:TB], [PR(b2)], ['kTown'])
                kdefer.append(trs)
            tokmajor_piece(wv, wres, A_v, 'A', NB, TB, evac)
        while kdefer:
            kdefer.pop(0)()
        flush_stores()
        fbanks = []
        for tb in range(NB):
            bk = P.bank()
            for c in range(16):
                P.mm(ps[bk][0:TB, 0:8], A_v[:, c, tb * TB:(tb + 1) * TB], wf[:, c, :], c == 0, c == 15, ['A', 'wf'], [PR(bk)])
            fbanks.append(bk)
        for tb in range(NB):
            bk = fbanks[tb]
            zr = 'zf4_%d' % tb
            P.tt('dve', zf4[0:TB, tb, :], ps[bk][0:TB, 0:8], bfb[0:TB, :], ALU.add, [PR(bk), 'bfb'], [zr])
            P.act(zf4[0:TB, tb, :], zf4[0:TB, tb, :], AF.Exp, [zr], [zr], scale=-1.0)
            P.ts('dve', zf4[0:TB, tb, :], zf4[0:TB, tb, :], 1.0, None, ALU.add, None, [zr], [zr])
            P.act(zf4[0:TB, tb, :], zf4[0:TB, tb, :], AF.Ln, [zr], [zr])
            P.ts('dve', lfall[0:TB, tb, :], zf4[0:TB, tb, :], -1.0, None, ALU.mult, None, [zr], ['lfall%d' % tb])
            store(fl_o[tb * TB:(tb + 1) * TB, :], lfall[0:TB, tb, :], ['lfall%d' % tb], 'S_lf')
        flush_stores()

        def f_cumsum(tb):
            j = ti * 4 + tb
            if j == 0:
                P.memset('dve', carry[:, :], 0.0, ['carry'])
            bk = P.bank()
            P.mm(ps[bk][:, 0:8], triu[:, :], lfall[:, tb, :], True, True, ['triu', 'lfall%d' % tb], [PR(bk)])
            P.tt('dve', ckall[:, j, :], ps[bk][:, 0:8], carry[:, :], ALU.add, [PR(bk), 'carry'], ['ckall'])
            bk = P.bank()
            P.mm(ps[bk][:, 0:8], sel127[:, :], ckall[:, j, :], True, True, ['sel127', 'ckall'], [PR(bk)])
            P.copy('dve', carry[:, :], ps[bk][:, 0:8], [PR(bk)], ['carry'])
            if tb == 1:
                for i_ in range(4):
                    P.copy('dve', cref[:, i_, :], ps[bk][:, 0:8], [PR(bk)], ['cref'])

        def f_cumsum_sample():
            for b in range(2):
                P.dma('sp', 'S_clsb', clsb[:, :, :], cl[b, :, :].rearrange("(j p) h -> p j h", p=128), [], ['clsb'])
                bk = P.bank()
                P.mm(ps[bk][:, 0:256], triu[:, :], clsb[:, :, :].rearrange("p j h -> p (j h)"), True, True, ['triu', 'clsb'], [PR(bk)])
                P.copy('dve', Lloc[:, :, :].rearrange("p j h -> p (j h)"), ps[bk][:, 0:256], [PR(bk)], ['Lloc'])
                bk = P.bank()
                P.mm(ps[bk][:, 0:256], sel127[:, :], Lloc[:, :, :].rearrange("p j h -> p (j h)"), True, True, ['sel127', 'Lloc'], [PR(bk)])
                P.copy('dve', totb[:, :, :].rearrange("p j h -> p (j h)"), ps[bk][:, 0:256], [PR(bk)], ['totb'])
                for h in range(8):
                    P.op('dve', lambda e, h=h: e.tensor_tensor_scan(out=pref[:, :, h], data0=onesb_f[:, 0:32], data1=totb[:, :, h],
                                                                  initial=0.0, op0=ALU.mult, op1=ALU.add), ['totb', 'onesf'], ['pref'])
                P.copy('dve', cendS[b][:, :], pref[:, 31, :], ['pref'], ['cendS%d' % b])
                P.tt('dve', pref[:, :, :], pref[:, :, :], totb[:, :, :], ALU.subtract, ['pref', 'totb'], ['pref'])
                P.tt('dve', ckall[:, b * 32:(b + 1) * 32, :], Lloc[:, :, :], pref[:, :, :], ALU.add, ['Lloc', 'pref'], ['ckall'])
            P.copy('dve', carryrows[0:32, :], cendS[0][0:32, :], ['cendS0'], ['carryrows'])
            P.copy('dve', carryrows[32:64, :], cendS[1][32:64, :], ['cendS1'], ['carryrows'])
            bk = P.bank()
            P.mm(ps[bk][0:64, 0:8], tri64[:, :], lfall[0:64, 0, :], True, True, ['tri64', 'lfall0'], [PR(bk)])
            P.tt('dve', zf[0:64, :], ps[bk][0:64, 0:8], carryrows[:, :], ALU.add, [PR(bk), 'carryrows'], ['zf'])
            bk = P.bank()
            P.mm(ps[bk][0:64, 0:8], selOwn[:, :], zf[0:64, :], True, True, ['selOwn', 'zf'], [PR(bk)])
            P.tt('dve', biaso[0:64, 0, 0, :], ps[bk][0:64, 0:8], zf[0:64, :], ALU.subtract, [PR(bk), 'zf'], ['biaso'])
            for b in range(2):
                bk = P.bank()
                P.mm(ps[bk][:, 0:8], selS[b][:, :], zf[0:64, :], True, True, ['selS%d' % b, 'zf'], [PR(bk)])
                P.copy('dve', crefS[b][:, :], ps[bk][:, 0:8], [PR(bk)], ['crefS%d' % b])

        for p in range(4):
            wv, wres = get_piece('win', 0, 2048 + p * 256)

            def evac(tb, bk, p=p):
                s_ = st[sti[0]]; sr = 'st%d' % sti[0]; sti[0] = (sti[0] + 1) % 6
                P.copy('act', s_[0:TB, :], ps[bk][0:TB, 0:256], [PR(bk)], [sr, PR(bk)])
                store(fv_o[tb * TB:(tb + 1) * TB, p * 256:(p + 1) * 256], s_[0:TB, :], [sr], 'S_' + sr,
                      ['fv_%s_%d_%d' % (tag, tb, p)])
                P.copy('pool', vown[0:TB, tb, 2 * p:2 * p + 2, 0:128],
                       s_[0:TB, :].rearrange("p (h d) -> p h d", h=2), [sr], ['vown'])
            tokmajor_piece(wv, wres, A_v, 'A', NB, TB, evac)
        flush_stores()
        if not sample and ti < NTILE - 1:
            for half in range(2):
                c_ = 2 * ti + half
                P.dma('act', 'S_kTs', kTs[c_, :, :].rearrange("p (h t) -> p h t", h=8), kTown[:, :, half * 256:(half + 1) * 256],
                      ['kTown'], ['kTs%d' % c_])
                P.dma('act', 'S_vs', vs[c_, :, :], vown[:, 2 * half:2 * half + 2, :, :].rearrange("p j h d -> p (j h d)"),
                      ['vown'], ['vs%d' % c_])
        load_ln(sgu_g, sgu_b, 1024)
        for p in range(4):
            wv, wres = get_piece('win', 0, 4096 + p * 256)

            def evac(tb, bk, p=p):
                P.act(resid[0:TB, tb, 1024 + p * 256:1024 + (p + 1) * 256], ps[bk][0:TB, 0:256], AF.Gelu_apprx_tanh, [PR(bk)], ['resid%d' % tb])
            tokmajor_piece(wv, wres, A_v, 'A', NB, TB, evac)
            if not sample:
                f_cumsum(p)
        if sample:
            f_cumsum_sample()
        for tb in range(NB):
            layer_norm(tb, TB, 1024, resid[0:TB, tb, 1024:2048], 'resid%d' % tb, 2)
        for tb in range(NB):
            src = resid[0:TB, tb, 1024:2048]
            P.tt('pool', src, src, lnG[0:TB, 0:1024], ALU.mult, ['resid%d' % tb, 'lnG'], ['resid%d' % tb])
            if sample:
                P.tt('dve', gst[0:TB, :], src, lnB[0:TB, 0:1024], ALU.add, ['resid%d' % tb, 'lnB'], ['rl0', 'rl1'])
                store(gv_s[:, :], gst[0:TB, :], ['rl0', 'rl1'], 'S_gst')
                flush_stores()
                P.copy('act', gn[0:TB, tb, :], gst[0:TB, :], ['rl0', 'rl1'], ['gn'])
            else:
                P.tt('dve', gn[0:TB, tb, :], src, lnB[0:TB, 0:1024], ALU.add, ['resid%d' % tb, 'lnB'], ['gn'])
        for p in range(4):
            wv, wres = get_piece('win', 0, 3072 + p * 256)

            def evac(tb, bk, p=p):
                P.act(resid[0:TB, tb, p * 256:(p + 1) * 256], ps[bk][0:TB, 0:256], AF.Gelu_apprx_tanh, [PR(bk)], ['resid%d' % tb])
            tokmajor_piece(wv, wres, A_v, 'A', NB, TB, evac)
        wsT_ = wsTS if sample else wsT
        bsc = bscolS if sample else bscol
        wres_ = 'wsTS' if sample else 'wsT'
        sdefer = []

        def sgu_tb(tb):
            t_ = tokb[tki[0]]; tr_ = 'tokb%d' % tki[0]; tki[0] ^= 1
            for g in range(4):
                bk = P.bank()
                P.mm(ps[bk][0:TB, 0:256], wsT_[0:TB, g, 0:TB], gn[0:TB, tb, g * 256:(g + 1) * 256], True, True, [wres_, 'gn'], [PR(bk)])
                P.stt(t_[0:TB, g * 256:(g + 1) * 256], ps[bk][0:TB, 0:256], bsc[0:TB, g:g + 1], resid[0:TB, tb, g * 256:(g + 1) * 256],
                      ALU.add, ALU.mult, [PR(bk), 'bscol', 'resid%d' % tb], [tr_])
            if sdefer:
                sdefer.pop(0)()
            sdefer.append(lambda: transpose_rows_to(t_, tr_, TB, 8, B_v, 'B', 8, tb * TB))
        sgu_q = [(lambda tb=tb: sgu_tb(tb)) for tb in range(NB)]
        chunks = []
        if sample:
            for b in range(2):
                for cj in range(16):
                    chunks.append(('cache', b, cj))
            chunks.append(('ownS', 0, 0))
        else:
            for cj in range(2 * ti):
                chunks.append(('prev', 0, cj))
            chunks.append(('own', 0, 0)); chunks.append(('own', 0, 1))
        units = []
        cslot = [0]
        ptc = [0]

        vflat = [vaug[i][:, :, :, :].rearrange("p j h d -> p (j h d)")[:, 0:2048].rearrange("p (j n) -> p j n", j=2) for i in range(2)]

        def prep_stream(kind, b, cj, s):
            if kind == 'prev':
                jb = cj * 2
                P.dma('sp', 'S_kT%d' % s, kT[s][:, :, :], kTs[cj, :, :].rearrange("p (h t) -> p h t", h=8), ['kTs%d' % cj], ['kT%d' % s])
                P.dma('sp', 'S_vg%d' % s, vaug[s][:, :, :, :].rearrange("p j h d -> p (j h d)"), vs[cj, :, :], ['vs%d' % cj],
                      ['vaug%d_0' % s, 'vaug%d_1' % s])
                for jj in range(2):
                    P.tt('dve', biasb[s][:, jj, 0, :], cref[:, 0, :], ckall[:, jb + jj, :], ALU.subtract, ['cref', 'ckall'], ['biasb%d' % s])
                return
            ksrc = ck[b, cj * 256:(cj + 1) * 256, :]; vsrc = cv[b, cj * 256:(cj + 1) * 256, :]
            jb = b * 32 + cj * 2
            P.dma('pool', 'S_kb', kb[:, :, :], ksrc.rearrange("(j p) d -> p j d", p=128), [], ['kb'])
            P.dma('pool', 'S_vaug%d_0' % s, vflat[s], vsrc.rearrange("(j p) d -> p j d", p=128), [], ['vaug%d_0' % s, 'vaug%d_1' % s])
            for jj in range(2):
                bk = P.bank()
                for h in range(8):
                    P.tr(psb(bk)[:, h * 128:(h + 1) * 128], kb[:, jj, h * 128:(h + 1) * 128], ident[:, :], ['kb', 'ident'], [PR(bk)])
                P.copy('dve', kT[s][:, :, jj * 128:(jj + 1) * 128], psb(bk)[:, :].rearrange("p (h t) -> p h t", h=8), [PR(bk)], ['kT%d' % s])
                P.tt('dve', biasb[s][:, jj, 0, :], crefS[b][:, :], ckall[:, jb + jj, :], ALU.subtract, ['crefS%d' % b, 'ckall'], ['biasb%d' % s])

        def mk_stream_unit(kind, b, cj, s, h, first, fc):
            st_ = {}

            def A():
                if first:
                    prep_stream(kind, b, cj, s)
                pi = ptc[0]; ptc[0] ^= 1
                if kind == 'prev':
                    pt_ = PT[pi]; ptr = 'PT%d' % pi
                else:
                    pt_ = PTS[b][pi]; ptr = 'PTS%d_%d' % (b, pi)
                st_['pt'] = (pt_, ptr)
                for jj in range(2):
                    bk = P.bank()
                    P.mm(ps[bk][:, 0:NT], kT[s][:, h, jj * 128:(jj + 1) * 128], qT[:, h, 0:NT], True, True, ['kT%d' % s, 'qT'], [PR(bk)])
                    if kind == 'prev':
                        P.act(pt_[:, jj, 0:512], ps[bk][:, 0:512], AF.Exp, [PR(bk), 'biasb%d' % s], [ptr],
                              bias=biasb[s][:, jj, 0, h:h + 1], scale=FOX_SCALE)
                    else:
                        P.act(pt_[:, jj, b * 32:(b + 1) * 32], ps[bk][:, b * 32:(b + 1) * 32], AF.Exp, [PR(bk), 'biasb%d' % s], [ptr],
                              bias=biasb[s][:, jj, 0, h:h + 1], scale=FOX_SCALE)

            def B():
                pt_, ptr = st_['pt']
                for ip in range(0, NB, 2):
                    n = min(2, NB - ip)
                    bk = P.bank()
                    if kind == 'cache':
                        for jj in range(2):
                            P.mm(ps[bk][0:TB, 0:128], pt_[:, jj, 0:TB], vflat[s][:, jj, h * 128:(h + 1) * 128], jj == 0, jj == 1,
                                 [ptr, 'vaug%d_%d' % (s, jj)], [PR(bk)])
                        for jj in range(2):
                            P.mm(ps[bk][0:TB, 256:257], pt_[:, jj, 0:TB], onesb[:, 0:1], jj == 0, jj == 1, [ptr, 'onesb'], [PR(bk)])
                        if fc:
                            P.copy('dve', oacc[0:TB, 0, h, 0:128], ps[bk][0:TB, 0:128], [PR(bk)], ['A'])
                            P.copy('dve', oacc[0:TB, 0, h, 128:129], ps[bk][0:TB, 256:257], [PR(bk)], ['A'])
                        else:
                            P.tt('dve', oacc[0:TB, 0, h, 0:128], oacc[0:TB, 0, h, 0:128], ps[bk][0:TB, 0:128], ALU.add, ['A', PR(bk)], ['A'])
                            P.tt('dve', oacc[0:TB, 0, h, 128:129], oacc[0:TB, 0, h, 128:129], ps[bk][0:TB, 256:257], ALU.add, ['A', PR(bk)], ['A'])
                        continue
                    for il in range(n):
                        i = ip + il
                        for jj in range(2):
                            P.mm(ps[bk][0:TB, il * 129:(il + 1) * 129], pt_[:, jj, i * TB:(i + 1) * TB], vaug[s][:, jj, h, :], jj == 0, jj == 1,
                                 [ptr, 'vaug%d_%d' % (s, jj)], [PR(bk)])
                    if fc:
                        P.copy('dve', oacc[0:TB, ip:ip + n, h, :], ps[bk][0:TB, 0:n * 129].rearrange("p (i d) -> p i d", i=n), [PR(bk)], ['A'])
                    else:
                        P.tt('dve', oacc[0:TB, ip:ip + n, h, :], oacc[0:TB, ip:ip + n, h, :], ps[bk][0:TB, 0:n * 129].rearrange("p (i d) -> p i d", i=n),
                             ALU.add, ['A', PR(bk)], ['A'])
            return A, B

        def mk_own_unit(oc, h, first, fc):
            st_ = {}

            def A():
                if first:
                    for jj in range(2):
                        blk = 2 * oc + jj
                        P.tt('dve', biaso[:, blk, 0, :], cref[:, 0, :], ckall[:, ti * 4 + blk, :], ALU.subtract, ['cref', 'ckall'], ['biaso'])
                pi = ptc[0]; ptc[0] ^= 1
                pt_ = PT[pi]; ptr = 'PT%d' % pi
                st_['pt'] = (pt_, ptr)
                for jj in range(2):
                    blk = 2 * oc + jj
                    nq = 4 - blk
                    bk = P.bank()
                    P.mm(ps[bk][:, 0:nq * 128], kTown[:, h, blk * 128:(blk + 1) * 128], qT[:, h, blk * 128:512], True, True, ['kTown', 'qT'], [PR(bk)])
                    P.act(pt_[:, jj, blk * 128:512], ps[bk][:, 0:nq * 128], AF.Exp, [PR(bk), 'biaso'], [ptr],
                          bias=biaso[:, blk, 0, h:h + 1], scale=FOX_SCALE)
                    P.tt('pool', pt_[:, jj, blk * 128:(blk + 1) * 128], pt_[:, jj, blk * 128:(blk + 1) * 128], maskP[:, :], ALU.mult, [ptr, 'maskP'], [ptr])

            def B():
                pt_, ptr = st_['pt']
                for ip in range(2 * oc, 4, 2):
                    bk = P.bank()
                    for il in range(2):
                        i = ip + il
                        jjs = [jj for jj in range(2) if 2 * oc + jj <= i]
                        for n_, jj in enumerate(jjs):
                            P.mm(ps[bk][:, il * 129:(il + 1) * 129], pt_[:, jj, i * 128:(i + 1) * 128], vown[:, 2 * oc + jj, h, :], n_ == 0, n_ == len(jjs) - 1,
                                 [ptr, 'vown'], [PR(bk)])
                    if fc:
                        P.copy('dve', oacc[:, ip:ip + 2, h, :], ps[bk][:, 0:258].rearrange("p (i d) -> p i d", i=2), [PR(bk)], ['A'])
                    else:
                        P.tt('dve', oacc[:, ip:ip + 2, h, :], oacc[:, ip:ip + 2, h, :], ps[bk][:, 0:258].rearrange("p (i d) -> p i d", i=2),
                             ALU.add, ['A', PR(bk)], ['A'])
            return A, B

        def mk_ownS_unit(h, fc):
            st_ = {}

            def A():
                pi = ptc[0]; ptc[0] ^= 1
                pt_ = PT[pi]; ptr = 'PT%d' % pi
                st_['pt'] = (pt_, ptr)
                bk = P.bank()
                P.mm(ps[bk][0:64, 0:64], kTown[:, h, 0:64], qT[:, h, 0:64], True, True, ['kTown', 'qT'], [PR(bk)])
                P.act(pt_[0:64, 0, 0:64], ps[bk][0:64, 0:64], AF.Exp, [PR(bk), 'biaso'], [ptr], bias=biaso[0:64, 0, 0, h:h + 1], scale=FOX_SCALE)
                P.tt('pool', pt_[0:64, 0, 0:64], pt_[0:64, 0, 0:64], maskS[:, :], ALU.mult, [ptr, 'maskS'], [ptr])

            def B():
                pt_, ptr = st_['pt']
                bk = P.bank()
                P.mm(ps[bk][0:64, 0:129], pt_[0:64, 0, 0:64], vown[0:64, 0, h, :], True, True, [ptr, 'vown'], [PR(bk)])
                if fc:
                    P.copy('dve', oacc[0:64, 0, h, :], ps[bk][0:64, 0:129], [PR(bk)], ['A'])
                else:
                    P.tt('dve', oacc[0:64, 0, h, :], oacc[0:64, 0, h, :], ps[bk][0:64, 0:129], ALU.add, ['A', PR(bk)], ['A'])
            return A, B

        for ci, (kind, b, cj) in enumerate(chunks):
            fc = (ci == 0)
            if kind in ('prev', 'cache'):
                s = cslot[0]; cslot[0] ^= 1
                for h in range(8):
                    units.append(mk_stream_unit(kind, b, cj, s, h, h == 0, fc))
            elif kind == 'own':
                for h in range(8):
                    units.append(mk_own_unit(cj, h, h == 0, fc))
            else:
                for h in range(8):
                    units.append(mk_ownS_unit(h, fc))
        for u in range(len(units)):
            units[u][0]()
            if u > 0:
                units[u - 1][1]()
            if u % 2 == 1 and sgu_q:
                sgu_q.pop(0)()
        units[-1][1]()
        while sgu_q:
            sgu_q.pop(0)()
        while sdefer:
            sdefer.pop(0)()
        P.dma('sp', 'S_resid', resid[0:TB, 0:NB, :], xsrc.rearrange("(b p) d -> p b d", p=TB), [],
              ['resid%d' % tb for tb in range(NB)])

        for i in range(NB):
            P.op('dve', lambda e, i=i: e.reciprocal(out=rlo[0:TB, :], in_=oacc[0:TB, i, :, 128]), ['A'], ['rlo'])
            t_ = tokb[tki[0]]; tr_ = 'tokb%d' % tki[0]; tki[0] ^= 1
            P.tt('dve', t_[0:TB, :].rearrange("p (h d) -> p h d", h=8), oacc[0:TB, i, :, 0:128],
                 rlo[0:TB, :].unsqueeze(2).to_broadcast([TB, 8, 128]), ALU.mult, ['A', 'rlo'], [tr_])
            if sdefer:
                sdefer.pop(0)()
            sdefer.append(lambda t_=t_, tr_=tr_, i=i: transpose_rows_to(t_, tr_, TB, 8, B_v, 'B', 0, i * TB))
        while sdefer:
            sdefer.pop(0)()

        def resid_evac(first):
            def evac(tb, bk, p):
                dst = resid[0:TB, tb, p * 256:(p + 1) * 256]
                if first:
                    P.stt(dst, dst, ALPHA, ps[bk][0:TB, 0:256], ALU.mult, ALU.add, ['resid%d' % tb, PR(bk)], ['resid%d' % tb])
                else:
                    P.tt('dve', dst, dst, ps[bk][0:TB, 0:256], ALU.add, ['resid%d' % tb, PR(bk)], ['resid%d' % tb])
            return evac

        halves = [[0]] if sample else [[0, 1], [2, 3]]
        tokh = [(0, NT)] if sample else [(0, 256), (256, 512)]
        xb_of = {}

        def ln_nonpe(hv):
            for tb in hv:
                layer_norm(tb, TB, 2048, resid[0:TB, tb, :], 'resid%d' % tb, 4)
            for tb in hv:
                src = resid[0:TB, tb, :]
                rr = 'resid%d' % tb
                P.tt('pool', src, src, lnG[0:TB, :], ALU.mult, [rr, 'lnG'], [rr])
                P.tt('dve', src, src, lnB[0:TB, :], ALU.add, [rr, 'lnB'], [rr])
                x_ = xb[xbi[0]]; xr = 'xb%d' % xbi[0]; xbi[0] ^= 1
                P.copy('act', x_[0:TB, :], src, [rr], [xr])
                xb_of[tb] = (x_, xr)

        def ln_tr(hv, dst_v, dst_res):
            for tb in hv:
                x_, xr = xb_of[tb]
                transpose_rows_to(x_, xr, TB, 16, dst_v, dst_res, 0, tb * TB)

        def ln3_compute():
            for tb in range(NB):
                layer_norm(tb, TB, 2048, resid[0:TB, tb, :], 'resid%d' % tb, 4)
            for tb in range(NB):
                src = resid[0:TB, tb, :]
                rr = 'resid%d' % tb
                P.tt('pool', src, src, lnG[0:TB, :], ALU.mult, [rr, 'lnG'], [rr])
                P.tt('dve', src, src, lnB[0:TB, :], ALU.add, [rr, 'lnB'], [rr])

        def ln3_store():
            for tb in range(NB):
                store(y_o[tb * TB:(tb + 1) * TB, :], resid[0:TB, tb, :], ['resid%d' % tb], 'S_y%d' % tb)
            flush_stores()

        def tok_pass(name, r0, src_v, src_res, hv, evacf):
            for p in range(8):
                wv, wres = get_piece(name, r0, p * 256)
                tokmajor_piece(wv, wres, src_v, src_res, hv, TB, lambda tb, bk, p=p: evacf(tb, bk, p))

        def feat_pass(name, c0, src_v, src_res, t0, t1, evacf):
            for p in range(8):
                wv, wres = get_piece(name, 0, c0 + p * 256)
                featmajor_piece(wv, wres, src_v, src_res, t1, lambda lc, bk, p=p: evacf(lc, bk, p, t0, t1), t0=t0)

        def evac_qm(lc, bk, p, t0, t1):
            P.copy('act', B_v[:, 2 * p + lc, t0:t1], ps[bk][:, 0:t1 - t0], [PR(bk)], ['B'])

        def evac_up(lc, bk, p, t0, t1):
            t_ = tmpf[lc]; tr_ = 'rl%d' % lc
            P.act(t_[:, 0:t1 - t0], ps[bk][:, 0:t1 - t0], AF.Relu, [PR(bk)], [tr_])
            P.tt('dve', A_v[:, 2 * p + lc, t0:t1], t_[:, 0:t1 - t0], t_[:, 0:t1 - t0], ALU.mult, [tr_], ['A'])

        ev1 = resid_evac(True)
        load_ln(ln_g[0], ln_b[0], 2048)
        for hi, hv in enumerate(halves):
            tok_pass('wout', 0, B_v, 'B', hv, ev1)
            if hi > 0:
                ln_tr(halves[hi - 1], A_v, 'A')
            ln_nonpe(hv)
            if hi > 0:
                feat_pass('wmq', 0, A_v, 'A', tokh[hi - 1][0], tokh[hi - 1][1], evac_qm)
        ln_tr(halves[-1], A_v, 'A')
        feat_pass('wmq', 0, A_v, 'A', tokh[-1][0], tokh[-1][1], evac_qm)
        load_ln(ln_g[1], ln_b[1], 2048)
        msets = [(1, 0, 32), (2, 32, 64)] if sample else [(0, 0, 512)]
        for (ms, c0, c1) in msets:
            kv_, kres = load_piece(mkT_s[ms, :, :].rearrange("p (c m) -> p c m", c=16), 'w', ['mkTs%d' % ms])
            vv_, vres = load_piece(mv_s[ms, :, :].rearrange("(m p) n -> p m n", p=128), 'mv',
                                   (mvs0_res if ms == 0 else ['mvs%d' % ms]))
            for hd in range(4):
                pt_ = PT[ptc[0]]; ptr = 'PT%d' % ptc[0]; ptc[0] ^= 1
                for mb in range(2):
                    bk = P.bank()
                    for cc in range(4):
                        P.mm(ps[bk][:, 0:c1 - c0], kv_[:, hd * 4 + cc, mb * 128:(mb + 1) * 128], B_v[:, hd * 4 + cc, c0:c1], cc == 0, cc == 3, [kres, 'B'], [PR(bk)])
                    P.act(pt_[:, mb, c0:c1], ps[bk][:, 0:c1 - c0], AF.Exp, [PR(bk)], [ptr], scale=MEM_SCALE)
                bk = P.bank()
                for mb in range(2):
                    P.mm(ps[bk][:, 0:c1 - c0], onesb[:, :], pt_[:, mb, c0:c1], mb == 0, mb == 1, ['onesb', ptr], [PR(bk)])
                r_ = rl[hd % 2]; rr_ = 'rl%d' % (hd % 2)
                P.op('dve', lambda e, r_=r_, bk=bk: e.reciprocal(out=r_[:, 0:c1 - c0], in_=ps[bk][:, 0:c1 - c0]), [PR(bk)], [rr_])
                for cc in range(4):
                    bk = P.bank()
                    ch = hd * 4 + cc
                    for mb in range(2):
                        P.mm(ps[bk][:, 0:c1 - c0], vv_[:, mb, ch * 128:(ch + 1) * 128], pt_[:, mb, c0:c1], mb == 0, mb == 1, [vres, ptr], [PR(bk)])
                    P.tt('dve', A_v[:, ch, c0:c1], ps[bk][:, 0:c1 - c0], r_[:, 0:c1 - c0], ALU.mult, [PR(bk), rr_], ['A'])
        for hi, hv in enumerate(halves):
            tok_pass('wmo', 0, A_v, 'A', hv, ev1)
            if hi > 0:
                ln_tr(halves[hi - 1], B_v, 'B')
            ln_nonpe(hv)
            if hi > 0:
                feat_pass('wup', 0, B_v, 'B', tokh[hi - 1][0], tokh[hi - 1][1], evac_up)
        ln_tr(halves[-1], B_v, 'B')
        feat_pass('wup', 0, B_v, 'B', tokh[-1][0], tokh[-1][1], evac_up)
        if False and nxt is not None:
            nsrc, nTB, nNB = nxt
            for tb in range(min(2, nNB)):
                x_ = xb[xbi[0]]; xr = 'xb%d' % xbi[0]; xbi[0] ^= 1
                P.dma('pool', 'S_' + xr, x_[0:nTB, :], nsrc[tb * nTB:(tb + 1) * nTB, :], [], [xr])
                xpre.append((x_, xr))
        load_ln(ln_g[2], ln_b[2], 2048)
        for g in range(4):
            if g > 0:
                feat_pass('wup', g * 2048, B_v, 'B', 0, NT, evac_up)
            evg = resid_evac(g == 0)
            tok_pass('wdn', g * 2048, A_v, 'A', list(range(NB)), evg)
        pending.append(ln3_compute)
        pending2.append(ln3_store)

    onesf = sb("onesf", [128, 32], F32)
    onesb_f = onesf
    P.memset('pool', onesf[:, :], 1.0, ['onesf'])

    mem_prologue()
    for ti in range(NTILE):
        nxt = (x_p[(ti + 1) * 512:(ti + 2) * 512, :], 128, 4) if ti + 1 < NTILE else None
        run_tile(ti, False, nxt)
    mem_prologue_sample()
    run_tile(0, True)
    while pending:
        pending.pop(0)()
    while pending2:
        pending2.pop(0)()

    flush_stores()
    final = {}
    for (s, v, _) in out_events:
        final[s] = max(final.get(s, 0), v)
    for s in ['S_mkT']:
        final[s] = P.cnt[s]
    fw = [(P.sem(s), v) for s, v in final.items()]

    with nc.Block() as block:
        @block.sync
        def _(e):
            for f in P.q['sp']:
                f(e)
            for s_, v_ in fw:
                e.wait_ge(s_, v_)

        @block.tensor
        def _(e):
            for f in P.q['pe']:
                f(e)

        @block.vector
        def _(e):
            for f in P.q['dve']:
                f(e)

        @block.scalar
        def _(e):
            for f in P.q['act']:
                f(e)

        @block.gpsimd
        def _(e):
            for f in P.q['pool']:
                f(e)
    es.close()
    return nc, P


_CACHE = {}


def kernel(**inp):
    f32 = lambda a: np.ascontiguousarray(np.asarray(a, dtype=np.float32))
    if 'nc' not in _CACHE:
        _CACHE['nc'] = build()[0]
    nc = _CACHE['nc']
    shared = {
        "w_in": f32(inp["w_in"][0]), "b_f": f32(inp["b_f"]), "sgu_g": f32(inp["sgu_ln_g"]), "sgu_b": f32(inp["sgu_ln_b"]),
        "w_s": f32(inp["w_s"][0]), "b_s": f32(inp["b_s"][0]), "w_out": f32(inp["w_out"][0]),
        "ln1_g": f32(inp["ln1_g"]), "ln1_b": f32(inp["ln1_b"]), "ln2_g": f32(inp["ln2_g"]), "ln2_b": f32(inp["ln2_b"]),
        "ln3_g": f32(inp["ln3_g"]), "ln3_b": f32(inp["ln3_b"]),
        "w_mq": f32(inp["w_mq"][0]), "w_mk": f32(inp["w_mk"][0]), "w_mv": f32(inp["w_mv"][0]), "w_mo": f32(inp["w_mo"][0]),
        "w_up": f32(inp["w_up"][0]), "w_down": f32(inp["w_down"][0]),
    }
    in_maps = []
    for c in range(8):
        m = dict(shared)
        m["x_p"] = f32(inp["x_prompt"][c])
        m["x_s"] = f32(inp["x_sample"][2 * c:2 * c + 2]).reshape(64, D)
        m["mem"] = f32(inp["mem_prompt"][c])
        m["ck"] = f32(inp["cache_fox_k"][0, 2 * c:2 * c + 2]).reshape(2, SEQ, 1024)
        m["cv"] = f32(inp["cache_fox_v"][0, 2 * c:2 * c + 2]).reshape(2, SEQ, 1024)
        m["cl"] = f32(inp["cache_fox_logf"][0, 2 * c:2 * c + 2])
        m["cmk"] = f32(inp["cache_mem_k"][0, 2 * c:2 * c + 2]).reshape(2, 256, D)
        m["cmv"] = f32(inp["cache_mem_v"][0, 2 * c:2 * c + 2]).reshape(2, 256, D)
        in_maps.append(m)
    res = run_bass_kernel_spmd(nc, in_maps, core_ids=list(range(8)))
    R = res.results
    cat = lambda k: np.stack([np.asarray(R[c][k]) for c in range(8)], axis=0)
    y_prompt = cat("y_p")
    y_sample = cat("y_s").reshape(16, 32, D)
    fkp = cat("fk_p").reshape(1, 8, SEQ, 8, 128)
    fvp = cat("fv_p").reshape(1, 8, SEQ, 8, 128)
    flp = cat("fl_p").reshape(1, 8, SEQ, 8)
    mkp = cat("mk_p").reshape(1, 8, 256, 4, 512)
    mvp = cat("mv_p").reshape(1, 8, 256, 4, 512)
    fks = cat("fk_s").reshape(1, 16, 32, 8, 128)
    fvs = cat("fv_s").reshape(1, 16, 32, 8, 128)
    fls = cat("fl_s").reshape(1, 16, 32, 8)
    gvs = cat("gv_s").reshape(1, 16, 32, 1024)
    return (y_prompt, y_sample, fkp, fvp, flp, mkp, mvp, fks, fvs, fls, gvs)
```

```python
import contextlib
import numpy as np
import concourse.bass as bass
import concourse.mybir as mybir
from concourse.bass_utils import run_bass_kernel_spmd

F32 = mybir.dt.float32
BF16 = mybir.dt.bfloat16
AF = mybir.ActivationFunctionType
ALU = mybir.AluOpType

D = 2048
SEQ = 4096
NTILE = 8
ALPHA = 2.0 ** 0.25
EPS = 1e-5
FOX_SCALE = 128.0 ** -0.5
MEM_SCALE = 512.0 ** -0.5
ENGMAP = {'pe': 'tensor', 'act': 'scalar', 'dve': 'vector', 'pool': 'gpsimd', 'sp': 'sync'}


class Prog:
    def __init__(self, nc, es):
        self.nc = nc
        self.es = es
        self.q = {e: [] for e in ENGMAP}
        self.cnt = {}
        self.sems = {}
        self.lastw = {}
        self.readers = {}
        self.seen = {e: {} for e in ENGMAP}
        self.bank_i = 0
        self.nops = 0

    def sem(self, name):
        if name not in self.sems:
            self.sems[name] = self.es.enter_context(self.nc.semaphore(name))
        return self.sems[name]

    def op(self, eng, fn, reads=(), writes=(), dma_sem=None):
        deps = {}

        def add(ev):
            if ev is None:
                return
            s, v, e = ev
            if s not in deps or deps[s][0] < v:
                deps[s] = (v, e)
        for r in reads:
            add(self.lastw.get(r))
        for w in writes:
            add(self.lastw.get(w))
            for s, (v, e) in self.readers.get(w, {}).items():
                add((s, v, e))
        waits = []
        for s, (v, e) in deps.items():
            if eng == 'pe' and e == 'pe':
                continue
            if self.seen[eng].get(s, 0) >= v:
                continue
            self.seen[eng][s] = v
            waits.append((self.sem(s), v))
        if dma_sem:
            semname, inc, pe = dma_sem, 16, 'dma'
        else:
            semname, inc, pe = 'E_' + eng, 1, eng
        self.cnt[semname] = self.cnt.get(semname, 0) + inc
        val = self.cnt[semname]
        ev = (semname, val, pe)
        for r in reads:
            self.readers.setdefault(r, {})[semname] = (val, pe)
        for w in writes:
            self.lastw[w] = ev
            self.readers[w] = {}
        sh = self.sem(semname)

        def emit(e):
            for s_, v_ in waits:
                e.wait_ge(s_, v_)
            fn(e).then_inc(sh, inc)
        self.q[eng].append(emit)
        self.nops += 1
        return ev

    def mm(self, out, lhsT, rhs, start, stop, reads, writes):
        return self.op('pe', lambda e: e.matmul(out, lhsT=lhsT, rhs=rhs, start=start, stop=stop), reads, writes)

    def tr(self, out, in_, ident, reads, writes):
        return self.op('pe', lambda e: e.transpose(out, in_, ident), reads, writes)

    def act(self, out, in_, func, reads, writes, bias=None, scale=None):
        kw = {}
        if bias is not None:
            kw['bias'] = bias
        if scale is not None:
            kw['scale'] = scale
        return self.op('act', lambda e: e.activation(out=out, in_=in_, func=func, **kw), reads, writes)

    def tt(self, eng, out, in0, in1, op, reads, writes):
        return self.op(eng, lambda e: e.tensor_tensor(out=out, in0=in0, in1=in1, op=op), reads, writes)

    def ts(self, eng, out, in0, s1, s2, op0, op1, reads, writes):
        if op1 is None:
            return self.op(eng, lambda e: e.tensor_scalar(out=out, in0=in0, scalar1=s1, scalar2=None, op0=op0), reads, writes)
        return self.op(eng, lambda e: e.tensor_scalar(out=out, in0=in0, scalar1=s1, scalar2=s2, op0=op0, op1=op1), reads, writes)

    def stt(self, out, in0, scalar, in1, op0, op1, reads, writes):
        return self.op('dve', lambda e: e.scalar_tensor_tensor(out=out, in0=in0, scalar=scalar, in1=in1, op0=op0, op1=op1), reads, writes)

    def copy(self, eng, out, in_, reads, writes):
        if eng == 'act':
            return self.act(out, in_, AF.Copy, reads, writes)
        return self.op(eng, lambda e: e.tensor_copy(out=out, in_=in_), reads, writes)

    def memset(self, eng, ap, val, writes):
        return self.op(eng, lambda e: e.memset(ap, val), (), writes)

    def dma(self, q, semname, out, in_, reads, writes, slow=False):
        if slow:
            return self.op(q, lambda e: e.dma_start(out=out, in_=in_, allow_slow_non_contiguous=True), reads, writes, dma_sem=semname)
        return self.op(q, lambda e: e.dma_start(out=out, in_=in_), reads, writes, dma_sem=semname)

    def bank(self):
        b = self.bank_i
        self.bank_i = (self.bank_i + 1) % 8
        return b


def build():
    nc = bass.Bass("TRN2", target_bir_lowering=False)
    es = contextlib.ExitStack()
    P = Prog(nc, es)

    def din(name, shape):
        return nc.dram_tensor(name, shape, F32, kind="ExternalInput").ap()

    def dout(name, shape):
        return nc.dram_tensor(name, shape, F32, kind="ExternalOutput").ap()

    def dscr(name, shape):
        return nc.dram_tensor(name, shape, BF16, kind="Internal").ap()

    x_p = din("x_p", [SEQ, D]); x_s = din("x_s", [64, D]); mem = din("mem", [256, D])
    ck = din("ck", [2, SEQ, 1024]); cv = din("cv", [2, SEQ, 1024]); cl = din("cl", [2, SEQ, 8])
    cmk = din("cmk", [2, 256, D]); cmv = din("cmv", [2, 256, D])
    w_in = din("w_in", [D, 5128]); b_f = din("b_f", [1, 8])
    sgu_g = din("sgu_g", [1, 1024]); sgu_b = din("sgu_b", [1, 1024])
    w_s = din("w_s", [4, 128, 128]); b_s = din("b_s", [4, 128])
    w_out = din("w_out", [D, D])
    ln_g = [din("ln%d_g" % i, [1, D]) for i in (1, 2, 3)]
    ln_b = [din("ln%d_b" % i, [1, D]) for i in (1, 2, 3)]
    w_mq = din("w_mq", [D, D]); w_mk = din("w_mk", [D, D]); w_mv = din("w_mv", [D, D]); w_mo = din("w_mo", [D, D])
    w_up = din("w_up", [D, 8192]); w_down = din("w_down", [8192, D])

    y_p = dout("y_p", [SEQ, D]); y_s = dout("y_s", [64, D])
    fk_p = dout("fk_p", [SEQ, 1024]); fv_p = dout("fv_p", [SEQ, 1024]); fl_p = dout("fl_p", [SEQ, 8])
    mk_p = dout("mk_p", [256, D]); mv_p = dout("mv_p", [256, D])
    fk_s = dout("fk_s", [64, 1024]); fv_s = dout("fv_s", [64, 1024]); fl_s = dout("fl_s", [64, 8])
    gv_s = dout("gv_s", [64, 1024])

    mkT_s = dscr("mkT_s", [3, 128, 4096]); mv_s = dscr("mv_s", [3, 256, D])
    pscr = dscr("pscr", [108, 128, 4096])
    kTs = dscr("kTs", [16, 128, 2048]); vs = dscr("vs", [16, 128, 2064])

    def sb(name, shape, dt):
        return es.enter_context(nc.sbuf_tensor(name, shape, dt))

    wbuf = [sb("wbuf%d" % i, [128, 4096], BF16) for i in range(3)]
    wf = sb("wf", [128, 16, 8], BF16)
    A_raw = sb("A_raw", [128, 8256], BF16)
    B_raw = sb("B_raw", [128, 8192], BF16)
    A_v = A_raw[:, 0:8192].rearrange("p (c t) -> p c t", c=16)
    B_v = B_raw[:, :].rearrange("p (c t) -> p c t", c=16)
    oacc = A_raw[:, :].bitcast(F32).rearrange("p (i h d) -> p i h d", i=4, h=8)
    resid = sb("resid", [128, 4, D], F32)
    qT = sb("qT", [128, 8, 512], BF16)
    gn = sb("gn", [128, 4, 1024], BF16)
    gn_flat = gn[:, :, :].rearrange("p a b -> p (a b)")
    mkTst = gn_flat.rearrange("p (c m) -> p c m", c=16)
    tokb = [sb("tokb%d" % i, [128, 1024], BF16) for i in range(2)]
    kb = sb("kb", [128, 2, 1024], BF16)
    kT = [sb("kT%d" % i, [128, 8, 256], BF16) for i in range(2)]
    vaug = [sb("vaug%d" % i, [128, 2, 8, 129], BF16) for i in range(2)]
    kTown = sb("kTown", [128, 8, 512], BF16)
    vown = sb("vown", [128, 4, 8, 129], BF16)
    PT = [sb("PT%d" % i, [128, 2, 512], BF16) for i in range(2)]
    PTS = [[sb("PTS%d_%d" % (b, i), [128, 2, 64], BF16) for i in range(2)] for b in range(2)]
    lnG = sb("lnG", [128, D], F32); lnB = sb("lnB", [128, D], F32)
    st = [sb("st%d" % i, [128, 256], F32) for i in range(6)]
    kbs = [sb("kbs%d" % i, [128, 256], BF16) for i in range(2)]
    xb = [sb("xb%d" % i, [128, D], BF16) for i in range(2)]
    scr4 = sb("scr4", [128, 1024], F32)
    gst = scr4
    rl = [scr4[:, 0:512], scr4[:, 512:1024]]
    tmpf = rl
    ident = sb("ident", [128, 128], BF16); maskP = sb("maskP", [128, 128], BF16); maskS = sb("maskS", [64, 64], BF16)
    onesb = sb("onesb", [128, 128], BF16)
    triu = sb("triu", [128, 128], F32); sel127 = sb("sel127", [128, 128], F32)
    selS = [sb("selS%d" % b, [64, 128], F32) for b in range(2)]
    selOwn = sb("selOwn", [64, 64], F32); tri64 = sb("tri64", [64, 64], F32)
    wsraw = sb("wsraw", [128, 4, 128], F32); wsrb = sb("wsrb", [128, 4, 128], BF16)
    wsT = sb("wsT", [128, 4, 128], BF16)
    wsrawS = sb("wsrawS", [64, 4, 64], F32); wsrbS = sb("wsrbS", [64, 4, 64], BF16); wsTS = sb("wsTS", [64, 4, 64], BF16)
    bscol = sb("bscol", [128, 4], F32); bscolS = sb("bscolS", [64, 4], F32)
    bfb = sb("bfb", [128, 8], F32)
    ckall = sb("ckall", [128, 64, 8], F32)
    clsb = sb("clsb", [128, 32, 8], F32)
    Lloc = sb("Lloc", [128, 32, 8], F32); totb = sb("totb", [128, 32, 8], F32); pref = sb("pref", [128, 32, 8], F32)
    carry = sb("carry", [128, 8], F32)
    cendS = [sb("cendS%d" % b, [128, 8], F32) for b in range(2)]
    crefS = [sb("crefS%d" % b, [128, 8], F32) for b in range(2)]
    carryrows = sb("carryrows", [64, 8], F32)
    cref = sb("cref", [128, 4, 8], F32)
    biasb = [sb("biasb%d" % i, [128, 2, 4, 8], F32) for i in range(2)]
    biaso = sb("biaso", [128, 4, 4, 8], F32)
    zf = sb("zf", [128, 8], F32); zf4 = sb("zf4", [128, 4, 8], F32); lfall = sb("lfall", [128, 4, 8], F32)
    stats = sb("stats", [128, 4, 4, 6], F32); mvar = sb("mvar", [128, 4, 2], F32); rstd = sb("rstd", [128, 4, 2], F32)
    rlo = sb("rlo", [128, 8], F32)
    ones1 = sb("ones1", [128, 1], F32)

    ps = [es.enter_context(nc.psum_tensor("ps%d" % i, [128, 512], F32)) for i in range(8)]

    def psb(bk):
        return ps[bk][:, :].bitcast(BF16)

    def PR(bk):
        return 'ps%d' % bk

    def aff(out, in_, pattern, cmp, base, cm, writes):
        return P.op('pool', lambda e: e.affine_select(out=out, in_=in_, pattern=pattern, compare_op=cmp, fill=0.0,
                                                      base=base, channel_multiplier=cm), writes, writes)

    P.memset('pool', ident[:, :], 1.0, ['ident'])
    aff(ident[:, :], ident[:, :], [[-1, 128]], ALU.is_equal, 0, 1, ['ident'])
    P.memset('pool', maskP[:, :], 1.0, ['maskP'])
    aff(maskP[:, :], maskP[:, :], [[1, 128]], ALU.is_ge, 0, -1, ['maskP'])
    P.memset('pool', maskS[:, :], 1.0, ['maskS'])
    aff(maskS[:, :], maskS[:, :], [[1, 64]], ALU.is_ge, 0, -1, ['maskS'])
    P.memset('pool', maskS[0:32, 32:64], 0.0, ['maskS'])
    P.memset('pool', onesb[:, :], 1.0, ['onesb'])
    P.memset('pool', ones1[:, :], 1.0, ['ones1'])
    P.memset('pool', triu[:, :], 1.0, ['triu'])
    aff(triu[:, :], triu[:, :], [[1, 128]], ALU.is_ge, 0, -1, ['triu'])
    P.memset('pool', tri64[:, :], 1.0, ['tri64'])
    aff(tri64[:, :], tri64[:, :], [[1, 64]], ALU.is_ge, 0, -1, ['tri64'])
    P.memset('pool', tri64[0:32, 32:64], 0.0, ['tri64'])
    P.memset('pool', sel127[:, :], 1.0, ['sel127'])
    aff(sel127[:, :], sel127[:, :], [[0, 128]], ALU.is_equal, -127, 1, ['sel127'])
    for b in range(2):
        P.memset('pool', selS[b][:, :], 1.0, ['selS%d' % b])
        aff(selS[b][:, :], selS[b][:, :], [[0, 128]], ALU.is_equal, -(32 * b + 31), 1, ['selS%d' % b])
    P.memset('pool', selOwn[:, :], 1.0, ['selOwn'])
    aff(selOwn[:, 0:32], selOwn[:, 0:32], [[0, 32]], ALU.is_equal, -31, 1, ['selOwn'])
    aff(selOwn[:, 32:64], selOwn[:, 32:64], [[0, 32]], ALU.is_equal, -63, 1, ['selOwn'])
    for i in range(2):
        P.memset('pool', vaug[i][:, :, :, 128:129], 1.0, ['vaug%d_0' % i, 'vaug%d_1' % i])
        for b in range(2):
            P.memset('pool', PTS[b][i][:, :, :], 0.0, ['PTS%d_%d' % (b, i)])
    P.memset('pool', vown[:, :, :, 128:129], 1.0, ['vown'])

    P.dma('sp', 'S_misc1', bfb[:, :], b_f[0, :].partition_broadcast(128), [], ['bfb'])
    P.dma('sp', 'S_misc2', bscol[:, :], b_s.rearrange("g t -> t g"), [], ['bscol'], slow=True)
    P.dma('sp', 'S_misc3', bscolS[0:32, :], b_s[:, 0:32].rearrange("g t -> t g"), [], ['bscolS'], slow=True)
    P.dma('sp', 'S_misc4', bscolS[32:64, :], b_s[:, 0:32].rearrange("g t -> t g"), [], ['bscolS'], slow=True)
    P.dma('sp', 'S_misc5', wsraw[:, :, :], w_s.rearrange("g t s -> t g s"), [], ['wsraw'])
    P.memset('pool', wsrawS[:, :, :], 0.0, ['wsrawS'])
    P.dma('sp', 'S_misc6', wsrawS[0:32, :, 0:32], w_s[:, 0:32, 0:32].rearrange("g t s -> t g s"), [], ['wsrawS'])
    P.dma('sp', 'S_misc7', wsrawS[32:64, :, 32:64], w_s[:, 0:32, 0:32].rearrange("g t s -> t g s"), [], ['wsrawS'])
    aff(wsraw[:, :, :], wsraw[:, :, :], [[0, 4], [-1, 128]], ALU.is_ge, 0, 1, ['wsraw'])
    aff(wsrawS[:, :, :], wsrawS[:, :, :], [[0, 4], [-1, 64]], ALU.is_ge, 0, 1, ['wsrawS'])
    P.copy('dve', wsrb[:, :, :], wsraw[:, :, :], ['wsraw'], ['wsrb'])
    P.copy('dve', wsrbS[:, :, :], wsrawS[:, :, :], ['wsrawS'], ['wsrbS'])
    bk = P.bank()
    for g in range(4):
        P.tr(psb(bk)[:, g * 128:(g + 1) * 128], wsrb[:, g, :], ident[:, :], ['wsrb', 'ident'], [PR(bk)])
    P.copy('dve', wsT[:, :, :], psb(bk)[:, 0:512].rearrange("p (g t) -> p g t", g=4), [PR(bk)], ['wsT'])
    bk = P.bank()
    for g in range(4):
        P.tr(psb(bk)[0:64, g * 128:g * 128 + 64], wsrbS[:, g, :], ident[0:64, 0:64], ['wsrbS', 'ident'], [PR(bk)])
    P.copy('dve', wsTS[:, :, :], psb(bk)[0:64, 0:512].rearrange("p (g t) -> p g t", g=4)[:, :, 0:64], [PR(bk)], ['wsTS'])

    def conv(dst, src, res):
        P.dma('pool', 'S_cv_' + res, dst, src, [], [res])

    xbi = [0]
    mem_xb = []
    for mb in range(2):
        x_ = xb[xbi[0]]; xr = 'xb%d' % xbi[0]; xbi[0] ^= 1
        P.dma('pool', 'S_' + xr, x_[:, :], mem[mb * 128:(mb + 1) * 128, :], [], [xr])
        mem_xb.append((x_, xr))
    P.dma('pool', 'S_wf', wf[:, :, :], w_in[:, 3072:3080].rearrange("(c p) n -> p c n", p=128), [], ['wf'])
    for b in range(2):
        conv(mv_s[1 + b, :, :], cmv[b, :, :], 'mvs%d' % (1 + b))
    wslot = [0]
    SRC = {'win': w_in, 'wout': w_out, 'wmq': w_mq, 'wmo': w_mo, 'wup': w_up, 'wdn': w_down, 'wmk': w_mk, 'wmv': w_mv}
    piece_idx = {}

    def get_piece(name, r0, c0):
        key = (name, r0, c0)
        s = wslot[0]
        wslot[0] = (s + 1) % 3
        v = wbuf[s][:, :].rearrange("p (c n) -> p c n", c=16)
        if name in ('wmk', 'wmv') or key not in piece_idx:
            cc = c0 + 8 if (name == 'win' and c0 >= 3072) else c0
            ap = SRC[name][r0:r0 + 2048, cc:cc + 256].rearrange("(c p) n -> p c n", p=128)
            P.dma('pool', 'S_w%d' % s, v, ap, [], ['w%d' % s])
            if name not in ('wmk', 'wmv'):
                idx = len(piece_idx)
                piece_idx[key] = idx
                P.dma('sp', 'S_wb%d' % s, pscr[idx, :, :], wbuf[s][:, :], ['w%d' % s], ['pc%d' % idx])
        else:
            idx = piece_idx[key]
            P.dma('sp', 'S_w%d' % s, wbuf[s][:, :], pscr[idx, :, :], ['pc%d' % idx], ['w%d' % s])
        return v, 'w%d' % s

    def load_piece(src_ap, view, reads):
        s = wslot[0]
        wslot[0] = (s + 1) % 3
        if view == 'w':
            v = wbuf[s][:, :].rearrange("p (c n) -> p c n", c=16)
        elif view == 'mv':
            v = wbuf[s][:, :].rearrange("p (m n) -> p m n", m=2)
        P.dma('sp', 'S_w%d' % s, v, src_ap, reads, ['w%d' % s])
        return v, 'w%d' % s

    def wsrc(scr, r0, c0):
        return scr[r0:r0 + 2048, c0:c0 + 256].rearrange("(c p) n -> p c n", p=128)

    sti = [0]
    kbi = [0]
    tki = [0]
    evi = [0]

    def ev_eng():
        evi[0] ^= 1
        return 'act' if evi[0] else 'dve'

    def transpose_rows_to(src_tile, src_res, TB, nchunk, dst_v, dst_res, c0, t0):
        for h0 in range(0, nchunk, 8):
            bk = P.bank()
            for c8 in range(8):
                c = h0 + c8
                P.tr(psb(bk)[:, c8 * 128:c8 * 128 + TB], src_tile[0:TB, c * 128:(c + 1) * 128], ident[0:TB, 0:TB],
                     [src_res, 'ident'], [PR(bk)])
            P.copy(ev_eng(), dst_v[:, c0 + h0:c0 + h0 + 8, t0:t0 + TB],
                   psb(bk)[:, :].rearrange("p (c t) -> p c t", c=8)[:, :, 0:TB], [PR(bk)], [dst_res])

    def tokmajor_piece(wv, wres, src_v, src_res, NB, TB, evac):
        for tb in (range(NB) if isinstance(NB, int) else NB):
            bk = P.bank()
            for c in range(16):
                P.mm(ps[bk][0:TB, 0:256], src_v[:, c, tb * TB:(tb + 1) * TB], wv[:, c, :], c == 0, c == 15,
                     [wres, src_res], [PR(bk)])
            evac(tb, bk)

    def featmajor_piece(wv, wres, src_v, src_res, NT, evac, t0=0):
        for lc in range(2):
            bk = P.bank()
            for c in range(16):
                P.mm(ps[bk][:, 0:NT - t0], wv[:, c, lc * 128:(lc + 1) * 128], src_v[:, c, t0:NT], c == 0, c == 15,
                     [wres, src_res], [PR(bk)])
            evac(lc, bk)

    def layer_norm(tb, TB, width, src, res, nstat):
        for i in range(nstat):
            P.op('dve', lambda e, i=i: e.bn_stats(out=stats[0:TB, tb, i, :], in_=src[:, i * 512:(i + 1) * 512]), [res], ['stats%d' % tb])
        P.op('dve', lambda e: e.bn_aggr(out=mvar[0:TB, tb, :], in_=stats[0:TB, tb, 0:nstat, :].rearrange("p a b -> p (a b)")), ['stats%d' % tb], ['mvar%d' % tb])
        P.ts('dve', rstd[0:TB, tb, 0:1], mvar[0:TB, tb, 1:2], EPS, None, ALU.add, None, ['mvar%d' % tb], ['rstd%d' % tb])
        P.act(rstd[0:TB, tb, 0:1], rstd[0:TB, tb, 0:1], AF.Sqrt, ['rstd%d' % tb], ['rstd%d' % tb])
        P.op('dve', lambda e: e.reciprocal(out=rstd[0:TB, tb, 0:1], in_=rstd[0:TB, tb, 0:1]), ['rstd%d' % tb], ['rstd%d' % tb])
        P.ts('dve', rstd[0:TB, tb, 1:2], mvar[0:TB, tb, 0:1], rstd[0:TB, tb, 0:1], -1.0, ALU.mult, ALU.mult, ['mvar%d' % tb, 'rstd%d' % tb], ['nmr%d' % tb])
        P.act(src, src, AF.Identity, [res, 'rstd%d' % tb, 'nmr%d' % tb], [res], bias=rstd[0:TB, tb, 1:2], scale=rstd[0:TB, tb, 0:1])

    out_events = []

    store_defer = []

    def store(dst, src, reads, semname, res_w=()):
        def emit():
            out_events.append(P.dma('act', semname, dst, src, reads, list(res_w)))
        store_defer.append(emit)
        while len(store_defer) > 1:
            store_defer.pop(0)()

    def flush_stores():
        while store_defer:
            store_defer.pop(0)()

    def load_ln(gsrc, bsrc, width):
        P.dma('sp', 'S_lnG', lnG[:, 0:width], gsrc[0, :].partition_broadcast(128), [], ['lnG'])
        P.dma('sp', 'S_lnB', lnB[:, 0:width], bsrc[0, :].partition_broadcast(128), [], ['lnB'])

    def mem_prologue():
        for mb in range(2):
            x_, xr = mem_xb[mb]
            transpose_rows_to(x_, xr, 128, 16, A_v, 'A', 0, mb * 128)
        for p in range(8):
            wv, wres = get_piece('wmk', 0, p * 256)

            def evac(mb, bk, p=p):
                s_ = st[sti[0]]; sr = 'st%d' % sti[0]; sti[0] = (sti[0] + 1) % 6
                P.copy('act', s_[:, :], ps[bk][:, 0:256], [PR(bk)], [sr, PR(bk)])
                store(mk_p[mb * 128:(mb + 1) * 128, p * 256:(p + 1) * 256], s_[:, :], [sr], 'S_' + sr)
                k_ = kbs[kbi[0]]; kr = 'kbs%d' % kbi[0]; kbi[0] ^= 1
                P.copy('dve', k_[:, :], ps[bk][:, 0:256], [PR(bk)], [kr])
                b2 = P.bank()
                for lc in range(2):
                    P.tr(psb(b2)[:, lc * 128:(lc + 1) * 128], k_[:, lc * 128:(lc + 1) * 128], ident[:, :], [kr, 'ident'], [PR(b2)])
                P.copy('dve', mkTst[:, 2 * p:2 * p + 2, mb * 128:(mb + 1) * 128],
                       psb(b2)[:, 0:256].rearrange("p (c t) -> p c t", c=2), [PR(b2)], ['gn'])
            tokmajor_piece(wv, wres, A_v, 'A', 2, 128, evac)
        P.dma('sp', 'S_mkT', mkT_s[0, :, :], gn_flat, ['gn'], ['mkTs0'])
        for p in range(8):
            wv, wres = get_piece('wmv', 0, p * 256)

            def evac(mb, bk, p=p):
                s_ = st[sti[0]]; sr = 'st%d' % sti[0]; sti[0] = (sti[0] + 1) % 6
                P.copy('act', s_[:, :], ps[bk][:, 0:256], [PR(bk)], [sr, PR(bk)])
                store(mv_p[mb * 128:(mb + 1) * 128, p * 256:(p + 1) * 256], s_[:, :], [sr], 'S_' + sr)
                k_ = kbs[kbi[0]]; kr = 'kbs%d' % kbi[0]; kbi[0] ^= 1
                P.copy('dve', k_[:, :], ps[bk][:, 0:256], [PR(bk)], [kr])
                P.dma('sp', 'S_' + kr, mv_s[0, mb * 128:(mb + 1) * 128, p * 256:(p + 1) * 256], k_[:, :], [kr], ['mvs0_%d_%d' % (mb, p)])
            tokmajor_piece(wv, wres, A_v, 'A', 2, 128, evac)
        flush_stores()

    def mem_prologue_sample():
        for b in range(2):
            for mb in range(2):
                x_ = xb[xbi[0]]; xr = 'xb%d' % xbi[0]; xbi[0] ^= 1
                P.dma('pool', 'S_' + xr, x_[:, :], cmk[b, mb * 128:(mb + 1) * 128, :], [], [xr])
                transpose_rows_to(x_, xr, 128, 16, mkTst, 'gn', 0, mb * 128)
            P.dma('sp', 'S_mkT', mkT_s[1 + b, :, :], gn_flat, ['gn'], ['mkTs%d' % (1 + b)])

    mvs0_res = ['mvs0_%d_%d' % (mb, p) for mb in range(2) for p in range(8)]

    pending = []
    pending2 = []
    xpre = []

    def run_tile(ti, sample, nxt=None):
        peng = 'dve' if (ti == 0 and not sample) else 'pool'
        NB, TB = (1, 64) if sample else (4, 128)
        NT = NB * TB
        xsrc = x_s if sample else x_p[ti * 512:(ti + 1) * 512, :]
        fk_o = fk_s if sample else fk_p[ti * 512:(ti + 1) * 512, :]
        fv_o = fv_s if sample else fv_p[ti * 512:(ti + 1) * 512, :]
        fl_o = fl_s if sample else fl_p[ti * 512:(ti + 1) * 512, :]
        y_o = y_s if sample else y_p[ti * 512:(ti + 1) * 512, :]
        tag = 's' if sample else 'p%d' % ti

        for tb in range(NB):
            if xpre:
                x_, xr = xpre.pop(0)
            else:
                x_ = xb[xbi[0]]; xr = 'xb%d' % xbi[0]; xbi[0] ^= 1
                P.dma('pool', 'S_' + xr, x_[0:TB, :], xsrc[tb * TB:(tb + 1) * TB, :], [], [xr])
            transpose_rows_to(x_, xr, TB, 16, A_v, 'A', 0, tb * TB)

        while pending:
            pending.pop(0)()
        for p in range(4):
            wv, wres = get_piece('win', 0, p * 256)

            def evac(lc, bk, p=p):
                P.copy('act', qT[:, 2 * p + lc, 0:NT], ps[bk][:, 0:NT], [PR(bk)], ['qT'])
            featmajor_piece(wv, wres, A_v, 'A', NT, evac)
        while pending2:
            pending2.pop(0)()
        kdefer = []
        for p in range(4):
            wv, wres = get_piece('win', 0, 1024 + p * 256)

            def evac(tb, bk, p=p):
                s_ = st[sti[0]]; sr = 'st%d' % sti[0]; sti[0] = (sti[0] + 1) % 6
                P.copy('act', s_[0:TB, :], ps[bk][0:TB, 0:256], [PR(bk)], [sr, PR(bk)])
                store(fk_o[tb * TB:(tb + 1) * TB, p * 256:(p + 1) * 256], s_[0:TB, :], [sr], 'S_' + sr,
                      ['fk_%s_%d_%d' % (tag, tb, p)])
                k_ = kbs[kbi[0]]; kr = 'kbs%d' % kbi[0]; kbi[0] ^= 1
                P.copy(peng, k_[0:TB, :], s_[0:TB, :], [sr], [kr])
                if kdefer:
                    kdefer.pop(0)()

                def trs(k_=k_, kr=kr, p=p, tb=tb):
                    b2 = P.bank()
                    for lc in range(2):
                        P.tr(psb(b2)[:, lc * 128:lc * 128 + TB], k_[0:TB, lc * 128:(lc + 1) * 128], ident[0:TB, 0:TB], [kr, 'ident'], [PR(b2)])
                    P.copy('dve', kTown[:, 2 * p:2 * p + 2, tb * TB:(tb + 1) * TB],
                           psb(b2)[:, 0:256].rearrange("p (c t) -> p c t", c=2)[:, :, 0:TB], [PR(b2)], ['kTown'])
                kdefer.append(trs)
            tokmajor_piece(wv, wres, A_v, 'A', NB, TB, evac)
        while kdefer:
            kdefer.pop(0)()
        flush_stores()
        fbanks = []
        for tb in range(NB):
            bk = P.bank()
            for c in range(16):
                P.mm(ps[bk][0:TB, 0:8], A_v[:, c, tb * TB:(tb + 1) * TB], wf[:, c, :], c == 0, c == 15, ['A', 'wf'], [PR(bk)])
            fbanks.append(bk)
        for tb in range(NB):
            bk = fbanks[tb]
            zr = 'zf4_%d' % tb
            P.tt('dve', zf4[0:TB, tb, :], ps[bk][0:TB, 0:8], bfb[0:TB, :], ALU.add, [PR(bk), 'bfb'], [zr])
            P.act(zf4[0:TB, tb, :], zf4[0:TB, tb, :], AF.Exp, [zr], [zr], scale=-1.0)
            P.ts('dve', zf4[0:TB, tb, :], zf4[0:TB, tb, :], 1.0, None, ALU.add, None, [zr], [zr])
            P.act(zf4[0:TB, tb, :], zf4[0:TB, tb, :], AF.Ln, [zr], [zr])
            P.ts('dve', lfall[0:TB, tb, :], zf4[0:TB, tb, :], -1.0, None, ALU.mult, None, [zr], ['lfall%d' % tb])
            store(fl_o[tb * TB:(tb + 1) * TB, :], lfall[0:TB, tb, :], ['lfall%d' % tb], 'S_lf')
        flush_stores()

        def f_cumsum(tb):
            j = ti * 4 + tb
            if j == 0:
                P.memset('dve', carry[:, :], 0.0, ['carry'])
            bk = P.bank()
            P.mm(ps[bk][:, 0:8], triu[:, :], lfall[:, tb, :], True, True, ['triu', 'lfall%d' % tb], [PR(bk)])
            P.tt('dve', ckall[:, j, :], ps[bk][:, 0:8], carry[:, :], ALU.add, [PR(bk), 'carry'], ['ckall'])
            bk = P.bank()
            P.mm(ps[bk][:, 0:8], sel127[:, :], ckall[:, j, :], True, True, ['sel127', 'ckall'], [PR(bk)])
            P.copy('dve', carry[:, :], ps[bk][:, 0:8], [PR(bk)], ['carry'])
            if tb == 1:
                for i_ in range(4):
                    P.copy('dve', cref[:, i_, :], ps[bk][:, 0:8], [PR(bk)], ['cref'])

        def f_cumsum_sample():
            for b in range(2):
                P.dma('sp', 'S_clsb', clsb[:, :, :], cl[b, :, :].rearrange("(j p) h -> p j h", p=128), [], ['clsb'])
                bk = P.bank()
                P.mm(ps[bk][:, 0:256], triu[:, :], clsb[:, :, :].rearrange("p j h -> p (j h)"), True, True, ['triu', 'clsb'], [PR(bk)])
                P.copy('dve', Lloc[:, :, :].rearrange("p j h -> p (j h)"), ps[bk][:, 0:256], [PR(bk)], ['Lloc'])
                bk = P.bank()
                P.mm(ps[bk][:, 0:256], sel127[:, :], Lloc[:, :, :].rearrange("p j h -> p (j h)"), True, True, ['sel127', 'Lloc'], [PR(bk)])
                P.copy('dve', totb[:, :, :].rearrange("p j h -> p (j h)"), ps[bk][:, 0:256], [PR(bk)], ['totb'])
                for h in range(8):
                    P.op('dve', lambda e, h=h: e.tensor_tensor_scan(out=pref[:, :, h], data0=onesb_f[:, 0:32], data1=totb[:, :, h],
                                                                  initial=0.0, op0=ALU.mult, op1=ALU.add), ['totb', 'onesf'], ['pref'])
                P.copy('dve', cendS[b][:, :], pref[:, 31, :], ['pref'], ['cendS%d' % b])
                P.tt('dve', pref[:, :, :], pref[:, :, :], totb[:, :, :], ALU.subtract, ['pref', 'totb'], ['pref'])
                P.tt('dve', ckall[:, b * 32:(b + 1) * 32, :], Lloc[:, :, :], pref[:, :, :], ALU.add, ['Lloc', 'pref'], ['ckall'])
            P.copy('dve', carryrows[0:32, :], cendS[0][0:32, :], ['cendS0'], ['carryrows'])
            P.copy('dve', carryrows[32:64, :], cendS[1][32:64, :], ['cendS1'], ['carryrows'])
            bk = P.bank()
            P.mm(ps[bk][0:64, 0:8], tri64[:, :], lfall[0:64, 0, :], True, True, ['tri64', 'lfall0'], [PR(bk)])
            P.tt('dve', zf[0:64, :], ps[bk][0:64, 0:8], carryrows[:, :], ALU.add, [PR(bk), 'carryrows'], ['zf'])
            bk = P.bank()
            P.mm(ps[bk][0:64, 0:8], selOwn[:, :], zf[0:64, :], True, True, ['selOwn', 'zf'], [PR(bk)])
            P.tt('dve', biaso[0:64, 0, 0, :], ps[bk][0:64, 0:8], zf[0:64, :], ALU.subtract, [PR(bk), 'zf'], ['biaso'])
            for b in range(2):
                bk = P.bank()
                P.mm(ps[bk][:, 0:8], selS[b][:, :], zf[0:64, :], True, True, ['selS%d' % b, 'zf'], [PR(bk)])
                P.copy('dve', crefS[b][:, :], ps[bk][:, 0:8], [PR(bk)], ['crefS%d' % b])

        for p in range(4):
            wv, wres = get_piece('win', 0, 2048 + p * 256)

            def evac(tb, bk, p=p):
                s_ = st[sti[0]]; sr = 'st%d' % sti[0]; sti[0] = (sti[0] + 1) % 6
                P.copy('act', s_[0:TB, :], ps[bk][0:TB, 0:256], [PR(bk)], [sr, PR(bk)])
                store(fv_o[tb * TB:(tb + 1) * TB, p * 256:(p + 1) * 256], s_[0:TB, :], [sr], 'S_' + sr,
                      ['fv_%s_%d_%d' % (tag, tb, p)])
                P.copy(peng, vown[0:TB, tb, 2 * p:2 * p + 2, 0:128],
                       s_[0:TB, :].rearrange("p (h d) -> p h d", h=2), [sr], ['vown'])
            tokmajor_piece(wv, wres, A_v, 'A', NB, TB, evac)
        flush_stores()
        if not sample and ti < NTILE - 1:
            for half in range(2):
                c_ = 2 * ti + half
                P.dma('act', 'S_kTs', kTs[c_, :, :].rearrange("p (h t) -> p h t", h=8), kTown[:, :, half * 256:(half + 1) * 256],
                      ['kTown'], ['kTs%d' % c_])
                P.dma('act', 'S_vs', vs[c_, :, :], vown[:, 2 * half:2 * half + 2, :, :].rearrange("p j h d -> p (j h d)"),
                      ['vown'], ['vs%d' % c_])
        load_ln(sgu_g, sgu_b, 1024)
        for p in range(4):
            wv, wres = get_piece('win', 0, 4096 + p * 256)

            def evac(tb, bk, p=p):
                P.act(resid[0:TB, tb, 1024 + p * 256:1024 + (p + 1) * 256], ps[bk][0:TB, 0:256], AF.Gelu_apprx_tanh, [PR(bk)], ['resid%d' % tb])
            tokmajor_piece(wv, wres, A_v, 'A', NB, TB, evac)
            if not sample:
                f_cumsum(p)
        if sample:
            f_cumsum_sample()
        for tb in range(NB):
            layer_norm(tb, TB, 1024, resid[0:TB, tb, 1024:2048], 'resid%d' % tb, 2)
        for tb in range(NB):
            src = resid[0:TB, tb, 1024:2048]
            P.tt(peng, src, src, lnG[0:TB, 0:1024], ALU.mult, ['resid%d' % tb, 'lnG'], ['resid%d' % tb])
            if sample:
                P.tt('dve', gst[0:TB, :], src, lnB[0:TB, 0:1024], ALU.add, ['resid%d' % tb, 'lnB'], ['rl0', 'rl1'])
                store(gv_s[:, :], gst[0:TB, :], ['rl0', 'rl1'], 'S_gst')
                flush_stores()
                P.copy('act', gn[0:TB, tb, :], gst[0:TB, :], ['rl0', 'rl1'], ['gn'])
            else:
                P.tt('dve', gn[0:TB, tb, :], src, lnB[0:TB, 0:1024], ALU.add, ['resid%d' % tb, 'lnB'], ['gn'])
        for p in range(4):
            wv, wres = get_piece('win', 0, 3072 + p * 256)

            def evac(tb, bk, p=p):
                P.act(resid[0:TB, tb, p * 256:(p + 1) * 256], ps[bk][0:TB, 0:256], AF.Gelu_apprx_tanh, [PR(bk)], ['resid%d' % tb])
            tokmajor_piece(wv, wres, A_v, 'A', NB, TB, evac)
        wsT_ = wsTS if sample else wsT
        bsc = bscolS if sample else bscol
        wres_ = 'wsTS' if sample else 'wsT'
        sdefer = []

        def sgu_tb(tb):
            t_ = tokb[tki[0]]; tr_ = 'tokb%d' % tki[0]; tki[0] ^= 1
            for g in range(4):
                bk = P.bank()
                P.mm(ps[bk][0:TB, 0:256], wsT_[0:TB, g, 0:TB], gn[0:TB, tb, g * 256:(g + 1) * 256], True, True, [wres_, 'gn'], [PR(bk)])
                P.stt(t_[0:TB, g * 256:(g + 1) * 256], ps[bk][0:TB, 0:256], bsc[0:TB, g:g + 1], resid[0:TB, tb, g * 256:(g + 1) * 256],
                      ALU.add, ALU.mult, [PR(bk), 'bscol', 'resid%d' % tb], [tr_])
            if sdefer:
                sdefer.pop(0)()
            sdefer.append(lambda: transpose_rows_to(t_, tr_, TB, 8, B_v, 'B', 8, tb * TB))
        sgu_q = [(lambda tb=tb: sgu_tb(tb)) for tb in range(NB)]
        chunks = []
        if sample:
            for b in range(2):
                for cj in range(16):
                    chunks.append(('cache', b, cj))
            chunks.append(('ownS', 0, 0))
        else:
            for cj in range(2 * ti):
                chunks.append(('prev', 0, cj))
            chunks.append(('own', 0, 0)); chunks.append(('own', 0, 1))
        units = []
        cslot = [0]
        ptc = [0]

        def prep_stream(kind, b, cj, s):
            if kind == 'prev':
                jb = cj * 2
                P.dma('sp', 'S_kT%d' % s, kT[s][:, :, :], kTs[cj, :, :].rearrange("p (h t) -> p h t", h=8), ['kTs%d' % cj], ['kT%d' % s])
                P.dma('sp', 'S_vg%d' % s, vaug[s][:, :, :, :].rearrange("p j h d -> p (j h d)"), vs[cj, :, :], ['vs%d' % cj],
                      ['vaug%d_0' % s, 'vaug%d_1' % s])
                for jj in range(2):
                    P.tt('dve', biasb[s][:, jj, 0, :], cref[:, 0, :], ckall[:, jb + jj, :], ALU.subtract, ['cref', 'ckall'], ['biasb%d' % s])
                return
            ksrc = ck[b, cj * 256:(cj + 1) * 256, :]; vsrc = cv[b, cj * 256:(cj + 1) * 256, :]
            jb = b * 32 + cj * 2
            P.dma('pool', 'S_kb', kb[:, :, :], ksrc.rearrange("(j p) d -> p j d", p=128), [], ['kb'])
            for jj in range(2):
                P.dma('pool', 'S_vaug%d_%d' % (s, jj), vaug[s][:, jj, :, 0:128], vsrc[jj * 128:(jj + 1) * 128, :].rearrange("p (h d) -> p h d", h=8),
                      [], ['vaug%d_%d' % (s, jj)])
            for jj in range(2):
                bk = P.bank()
                for h in range(8):
                    P.tr(psb(bk)[:, h * 128:(h + 1) * 128], kb[:, jj, h * 128:(h + 1) * 128], ident[:, :], ['kb', 'ident'], [PR(bk)])
                P.copy('dve', kT[s][:, :, jj * 128:(jj + 1) * 128], psb(bk)[:, :].rearrange("p (h t) -> p h t", h=8), [PR(bk)], ['kT%d' % s])
                P.tt('dve', biasb[s][:, jj, 0, :], crefS[b][:, :], ckall[:, jb + jj, :], ALU.subtract, ['crefS%d' % b, 'ckall'], ['biasb%d' % s])

        def mk_stream_unit(kind, b, cj, s, h, first, fc):
            st_ = {}

            def A():
                if first:
                    prep_stream(kind, b, cj, s)
                pi = ptc[0]; ptc[0] ^= 1
                if kind == 'prev':
                    pt_ = PT[pi]; ptr = 'PT%d' % pi
                else:
                    pt_ = PTS[b][pi]; ptr = 'PTS%d_%d' % (b, pi)
                st_['pt'] = (pt_, ptr)
                for jj in range(2):
                    bk = P.bank()
                    P.mm(ps[bk][:, 0:NT], kT[s][:, h, jj * 128:(jj + 1) * 128], qT[:, h, 0:NT], True, True, ['kT%d' % s, 'qT'], [PR(bk)])
                    if kind == 'prev':
                        P.act(pt_[:, jj, 0:512], ps[bk][:, 0:512], AF.Exp, [PR(bk), 'biasb%d' % s], [ptr],
                              bias=biasb[s][:, jj, 0, h:h + 1], scale=FOX_SCALE)
                    else:
                        P.act(pt_[:, jj, b * 32:(b + 1) * 32], ps[bk][:, b * 32:(b + 1) * 32], AF.Exp, [PR(bk), 'biasb%d' % s], [ptr],
                              bias=biasb[s][:, jj, 0, h:h + 1], scale=FOX_SCALE)

            def B():
                pt_, ptr = st_['pt']
                for ip in range(0, NB, 2):
                    n = min(2, NB - ip)
                    bk = P.bank()
                    for il in range(n):
                        i = ip + il
                        for jj in range(2):
                            P.mm(ps[bk][0:TB, il * 129:(il + 1) * 129], pt_[:, jj, i * TB:(i + 1) * TB], vaug[s][:, jj, h, :], jj == 0, jj == 1,
                                 [ptr, 'vaug%d_%d' % (s, jj)], [PR(bk)])
                    if fc:
                        P.copy('dve', oacc[0:TB, ip:ip + n, h, :], ps[bk][0:TB, 0:n * 129].rearrange("p (i d) -> p i d", i=n), [PR(bk)], ['A'])
                    else:
                        P.tt('dve', oacc[0:TB, ip:ip + n, h, :], oacc[0:TB, ip:ip + n, h, :], ps[bk][0:TB, 0:n * 129].rearrange("p (i d) -> p i d", i=n),
                             ALU.add, ['A', PR(bk)], ['A'])
            return A, B

        def mk_own_unit(oc, h, first, fc):
            st_ = {}

            def A():
                if first:
                    for jj in range(2):
                        blk = 2 * oc + jj
                        P.tt('dve', biaso[:, blk, 0, :], cref[:, 0, :], ckall[:, ti * 4 + blk, :], ALU.subtract, ['cref', 'ckall'], ['biaso'])
                pi = ptc[0]; ptc[0] ^= 1
                pt_ = PT[pi]; ptr = 'PT%d' % pi
                st_['pt'] = (pt_, ptr)
                for jj in range(2):
                    blk = 2 * oc + jj
                    nq = 4 - blk
                    bk = P.bank()
                    P.mm(ps[bk][:, 0:nq * 128], kTown[:, h, blk * 128:(blk + 1) * 128], qT[:, h, blk * 128:512], True, True, ['kTown', 'qT'], [PR(bk)])
                    P.act(pt_[:, jj, blk * 128:512], ps[bk][:, 0:nq * 128], AF.Exp, [PR(bk), 'biaso'], [ptr],
                          bias=biaso[:, blk, 0, h:h + 1], scale=FOX_SCALE)
                    P.tt('pool', pt_[:, jj, blk * 128:(blk + 1) * 128], pt_[:, jj, blk * 128:(blk + 1) * 128], maskP[:, :], ALU.mult, [ptr, 'maskP'], [ptr])

            def B():
                pt_, ptr = st_['pt']
                for ip in range(2 * oc, 4, 2):
                    bk = P.bank()
                    for il in range(2):
                        i = ip + il
                        jjs = [jj for jj in range(2) if 2 * oc + jj <= i]
                        for n_, jj in enumerate(jjs):
                            P.mm(ps[bk][:, il * 129:(il + 1) * 129], pt_[:, jj, i * 128:(i + 1) * 128], vown[:, 2 * oc + jj, h, :], n_ == 0, n_ == len(jjs) - 1,
                                 [ptr, 'vown'], [PR(bk)])
                    if fc:
                        P.copy('dve', oacc[:, ip:ip + 2, h, :], ps[bk][:, 0:258].rearrange("p (i d) -> p i d", i=2), [PR(bk)], ['A'])
                    else:
                        P.tt('dve', oacc[:, ip:ip + 2, h, :], oacc[:, ip:ip + 2, h, :], ps[bk][:, 0:258].rearrange("p (i d) -> p i d", i=2),
                             ALU.add, ['A', PR(bk)], ['A'])
            return A, B

        def mk_ownS_unit(h, fc):
            st_ = {}

            def A():
                pi = ptc[0]; ptc[0] ^= 1
                pt_ = PT[pi]; ptr = 'PT%d' % pi
                st_['pt'] = (pt_, ptr)
                bk = P.bank()
                P.mm(ps[bk][0:64, 0:64], kTown[:, h, 0:64], qT[:, h, 0:64], True, True, ['kTown', 'qT'], [PR(bk)])
                P.act(pt_[0:64, 0, 0:64], ps[bk][0:64, 0:64], AF.Exp, [PR(bk), 'biaso'], [ptr], bias=biaso[0:64, 0, 0, h:h + 1], scale=FOX_SCALE)
                P.tt('pool', pt_[0:64, 0, 0:64], pt_[0:64, 0, 0:64], maskS[:, :], ALU.mult, [ptr, 'maskS'], [ptr])

            def B():
                pt_, ptr = st_['pt']
                bk = P.bank()
                P.mm(ps[bk][0:64, 0:129], pt_[0:64, 0, 0:64], vown[0:64, 0, h, :], True, True, [ptr, 'vown'], [PR(bk)])
                if fc:
                    P.copy('dve', oacc[0:64, 0, h, :], ps[bk][0:64, 0:129], [PR(bk)], ['A'])
                else:
                    P.tt('dve', oacc[0:64, 0, h, :], oacc[0:64, 0, h, :], ps[bk][0:64, 0:129], ALU.add, ['A', PR(bk)], ['A'])
            return A, B

        for ci, (kind, b, cj) in enumerate(chunks):
            fc = (ci == 0)
            if kind in ('prev', 'cache'):
                s = cslot[0]; cslot[0] ^= 1
                for h in range(8):
                    units.append(mk_stream_unit(kind, b, cj, s, h, h == 0, fc))
            elif kind == 'own':
                for h in range(8):
                    units.append(mk_own_unit(cj, h, h == 0, fc))
            else:
                for h in range(8):
                    units.append(mk_ownS_unit(h, fc))
        for u in range(len(units)):
            units[u][0]()
            if u > 0:
                units[u - 1][1]()
            if u % 2 == 1 and sgu_q:
                sgu_q.pop(0)()
        units[-1][1]()
        while sgu_q:
            sgu_q.pop(0)()
        while sdefer:
            sdefer.pop(0)()
        P.dma('sp', 'S_resid', resid[0:TB, 0:NB, :], xsrc.rearrange("(b p) d -> p b d", p=TB), [],
              ['resid%d' % tb for tb in range(NB)])

        for i in range(NB):
            P.op('dve', lambda e, i=i: e.reciprocal(out=rlo[0:TB, :], in_=oacc[0:TB, i, :, 128]), ['A'], ['rlo'])
            t_ = tokb[tki[0]]; tr_ = 'tokb%d' % tki[0]; tki[0] ^= 1
            P.tt('dve', t_[0:TB, :].rearrange("p (h d) -> p h d", h=8), oacc[0:TB, i, :, 0:128],
                 rlo[0:TB, :].unsqueeze(2).to_broadcast([TB, 8, 128]), ALU.mult, ['A', 'rlo'], [tr_])
            if sdefer:
                sdefer.pop(0)()
            sdefer.append(lambda t_=t_, tr_=tr_, i=i: transpose_rows_to(t_, tr_, TB, 8, B_v, 'B', 0, i * TB))
        while sdefer:
            sdefer.pop(0)()

        def resid_evac(first):
            def evac(tb, bk, p):
                dst = resid[0:TB, tb, p * 256:(p + 1) * 256]
                if first:
                    P.stt(dst, dst, ALPHA, ps[bk][0:TB, 0:256], ALU.mult, ALU.add, ['resid%d' % tb, PR(bk)], ['resid%d' % tb])
                else:
                    P.tt('dve', dst, dst, ps[bk][0:TB, 0:256], ALU.add, ['resid%d' % tb, PR(bk)], ['resid%d' % tb])
            return evac

        halves = [[0]] if sample else [[0, 1], [2, 3]]
        tokh = [(0, NT)] if sample else [(0, 256), (256, 512)]
        xb_of = {}

        def ln_nonpe(hv):
            for tb in hv:
                layer_norm(tb, TB, 2048, resid[0:TB, tb, :], 'resid%d' % tb, 4)
            for tb in hv:
                src = resid[0:TB, tb, :]
                rr = 'resid%d' % tb
                P.tt(peng, src, src, lnG[0:TB, :], ALU.mult, [rr, 'lnG'], [rr])
                P.tt('dve', src, src, lnB[0:TB, :], ALU.add, [rr, 'lnB'], [rr])
                x_ = xb[xbi[0]]; xr = 'xb%d' % xbi[0]; xbi[0] ^= 1
                P.copy('act', x_[0:TB, :], src, [rr], [xr])
                xb_of[tb] = (x_, xr)

        def ln_tr(hv, dst_v, dst_res):
            for tb in hv:
                x_, xr = xb_of[tb]
                transpose_rows_to(x_, xr, TB, 16, dst_v, dst_res, 0, tb * TB)

        def ln3_compute():
            for tb in range(NB):
                layer_norm(tb, TB, 2048, resid[0:TB, tb, :], 'resid%d' % tb, 4)
            for tb in range(NB):
                src = resid[0:TB, tb, :]
                rr = 'resid%d' % tb
                P.tt('pool', src, src, lnG[0:TB, :], ALU.mult, [rr, 'lnG'], [rr])
                P.tt('dve', src, src, lnB[0:TB, :], ALU.add, [rr, 'lnB'], [rr])

        def ln3_store():
            for tb in range(NB):
                store(y_o[tb * TB:(tb + 1) * TB, :], resid[0:TB, tb, :], ['resid%d' % tb], 'S_y%d' % tb)
            flush_stores()

        def tok_pass(name, r0, src_v, src_res, hv, evacf):
            for p in range(8):
                wv, wres = get_piece(name, r0, p * 256)
                tokmajor_piece(wv, wres, src_v, src_res, hv, TB, lambda tb, bk, p=p: evacf(tb, bk, p))

        def feat_pass(name, c0, src_v, src_res, t0, t1, evacf):
            for p in range(8):
                wv, wres = get_piece(name, 0, c0 + p * 256)
                featmajor_piece(wv, wres, src_v, src_res, t1, lambda lc, bk, p=p: evacf(lc, bk, p, t0, t1), t0=t0)

        def evac_qm(lc, bk, p, t0, t1):
            P.copy('act', B_v[:, 2 * p + lc, t0:t1], ps[bk][:, 0:t1 - t0], [PR(bk)], ['B'])

        def evac_up(lc, bk, p, t0, t1):
            t_ = tmpf[lc]; tr_ = 'rl%d' % lc
            P.act(t_[:, 0:t1 - t0], ps[bk][:, 0:t1 - t0], AF.Relu, [PR(bk)], [tr_])
            P.tt('dve', A_v[:, 2 * p + lc, t0:t1], t_[:, 0:t1 - t0], t_[:, 0:t1 - t0], ALU.mult, [tr_], ['A'])

        ev1 = resid_evac(True)
        load_ln(ln_g[0], ln_b[0], 2048)
        for hi, hv in enumerate(halves):
            tok_pass('wout', 0, B_v, 'B', hv, ev1)
            if hi > 0:
                ln_tr(halves[hi - 1], A_v, 'A')
            ln_nonpe(hv)
            if hi > 0:
                feat_pass('wmq', 0, A_v, 'A', tokh[hi - 1][0], tokh[hi - 1][1], evac_qm)
        ln_tr(halves[-1], A_v, 'A')
        feat_pass('wmq', 0, A_v, 'A', tokh[-1][0], tokh[-1][1], evac_qm)
        load_ln(ln_g[1], ln_b[1], 2048)
        msets = [(1, 0, 32), (2, 32, 64)] if sample else [(0, 0, 512)]
        for (ms, c0, c1) in msets:
            kv_, kres = load_piece(mkT_s[ms, :, :].rearrange("p (c m) -> p c m", c=16), 'w', ['mkTs%d' % ms])
            vv_, vres = load_piece(mv_s[ms, :, :].rearrange("(m p) n -> p m n", p=128), 'mv',
                                   (mvs0_res if ms == 0 else ['mvs%d' % ms]))
            for hd in range(4):
                pt_ = PT[ptc[0]]; ptr = 'PT%d' % ptc[0]; ptc[0] ^= 1
                for mb in range(2):
                    bk = P.bank()
                    for cc in range(4):
                        P.mm(ps[bk][:, 0:c1 - c0], kv_[:, hd * 4 + cc, mb * 128:(mb + 1) * 128], B_v[:, hd * 4 + cc, c0:c1], cc == 0, cc == 3, [kres, 'B'], [PR(bk)])
                    P.act(pt_[:, mb, c0:c1], ps[bk][:, 0:c1 - c0], AF.Exp, [PR(bk)], [ptr], scale=MEM_SCALE)
                bk = P.bank()
                for mb in range(2):
                    P.mm(ps[bk][:, 0:c1 - c0], onesb[:, :], pt_[:, mb, c0:c1], mb == 0, mb == 1, ['onesb', ptr], [PR(bk)])
                r_ = rl[hd % 2]; rr_ = 'rl%d' % (hd % 2)
                P.op('dve', lambda e, r_=r_, bk=bk: e.reciprocal(out=r_[:, 0:c1 - c0], in_=ps[bk][:, 0:c1 - c0]), [PR(bk)], [rr_])
                for cc in range(4):
                    bk = P.bank()
                    ch = hd * 4 + cc
                    for mb in range(2):
                        P.mm(ps[bk][:, 0:c1 - c0], vv_[:, mb, ch * 128:(ch + 1) * 128], pt_[:, mb, c0:c1], mb == 0, mb == 1, [vres, ptr], [PR(bk)])
                    P.tt('dve', A_v[:, ch, c0:c1], ps[bk][:, 0:c1 - c0], r_[:, 0:c1 - c0], ALU.mult, [PR(bk), rr_], ['A'])
        for hi, hv in enumerate(halves):
            tok_pass('wmo', 0, A_v, 'A', hv, ev1)
            if hi > 0:
                ln_tr(halves[hi - 1], B_v, 'B')
            ln_nonpe(hv)
            if hi > 0:
                feat_pass('wup', 0, B_v, 'B', tokh[hi - 1][0], tokh[hi - 1][1], evac_up)
        ln_tr(halves[-1], B_v, 'B')
        feat_pass('wup', 0, B_v, 'B', tokh[-1][0], tokh[-1][1], evac_up)
        if False and nxt is not None:
            nsrc, nTB, nNB = nxt
            for tb in range(min(2, nNB)):
                x_ = xb[xbi[0]]; xr = 'xb%d' % xbi[0]; xbi[0] ^= 1
                P.dma('pool', 'S_' + xr, x_[0:nTB, :], nsrc[tb * nTB:(tb + 1) * nTB, :], [], [xr])
                xpre.append((x_, xr))
        load_ln(ln_g[2], ln_b[2], 2048)
        for g in range(4):
            if g > 0:
                feat_pass('wup', g * 2048, B_v, 'B', 0, NT, evac_up)
            evg = resid_evac(g == 0)
            tok_pass('wdn', g * 2048, A_v, 'A', list(range(NB)), evg)
        pending.append(ln3_compute)
        pending2.append(ln3_store)

    onesf = sb("onesf", [128, 32], F32)
    onesb_f = onesf
    P.memset('pool', onesf[:, :], 1.0, ['onesf'])

    mem_prologue()
    for ti in range(NTILE):
        nxt = (x_p[(ti + 1) * 512:(ti + 2) * 512, :], 128, 4) if ti + 1 < NTILE else None
        run_tile(ti, False, nxt)
    mem_prologue_sample()
    run_tile(0, True)
    while pending:
        pending.pop(0)()
    while pending2:
        pending2.pop(0)()

    flush_stores()
    final = {}
    for (s, v, _) in out_events:
        final[s] = max(final.get(s, 0), v)
    for s in ['S_mkT']:
        final[s] = P.cnt[s]
    fw = [(P.sem(s), v) for s, v in final.items()]

    with nc.Block() as block:
        @block.sync
        def _(e):
            for f in P.q['sp']:
                f(e)
            for s_, v_ in fw:
                e.wait_ge(s_, v_)

        @block.tensor
        def _(e):
            for f in P.q['pe']:
                f(e)

        @block.vector
        def _(e):
            for f in P.q['dve']:
                f(e)

        @block.scalar
        def _(e):
            for f in P.q['act']:
                f(e)

        @block.gpsimd
        def _(e):
            for f in P.q['pool']:
                f(e)
    es.close()
    return nc, P


_CACHE = {}


def kernel(**inp):
    f32 = lambda a: np.ascontiguousarray(np.asarray(a, dtype=np.float32))
    if 'nc' not in _CACHE:
        _CACHE['nc'] = build()[0]
    nc = _CACHE['nc']
    shared = {
        "w_in": f32(inp["w_in"][0]), "b_f": f32(inp["b_f"]), "sgu_g": f32(inp["sgu_ln_g"]), "sgu_b": f32(inp["sgu_ln_b"]),
        "w_s": f32(inp["w_s"][0]), "b_s": f32(inp["b_s"][0]), "w_out": f32(inp["w_out"][0]),
        "ln1_g": f32(inp["ln1_g"]), "ln1_b": f32(inp["ln1_b"]), "ln2_g": f32(inp["ln2_g"]), "ln2_b": f32(inp["ln2_b"]),
        "ln3_g": f32(inp["ln3_g"]), "ln3_b": f32(inp["ln3_b"]),
        "w_mq": f32(inp["w_mq"][0]), "w_mk": f32(inp["w_mk"][0]), "w_mv": f32(inp["w_mv"][0]), "w_mo": f32(inp["w_mo"][0]),
        "w_up": f32(inp["w_up"][0]), "w_down": f32(inp["w_down"][0]),
    }
    in_maps = []
    for c in range(8):
        m = dict(shared)
        m["x_p"] = f32(inp["x_prompt"][c])
        m["x_s"] = f32(inp["x_sample"][2 * c:2 * c + 2]).reshape(64, D)
        m["mem"] = f32(inp["mem_prompt"][c])
        m["ck"] = f32(inp["cache_fox_k"][0, 2 * c:2 * c + 2]).reshape(2, SEQ, 1024)
        m["cv"] = f32(inp["cache_fox_v"][0, 2 * c:2 * c + 2]).reshape(2, SEQ, 1024)
        m["cl"] = f32(inp["cache_fox_logf"][0, 2 * c:2 * c + 2])
        m["cmk"] = f32(inp["cache_mem_k"][0, 2 * c:2 * c + 2]).reshape(2, 256, D)
        m["cmv"] = f32(inp["cache_mem_v"][0, 2 * c:2 * c + 2]).reshape(2, 256, D)
        in_maps.append(m)
    res = run_bass_kernel_spmd(nc, in_maps, core_ids=list(range(8)))
    R = res.results
    cat = lambda k: np.stack([np.asarray(R[c][k]) for c in range(8)], axis=0)
    y_prompt = cat("y_p")
    y_sample = cat("y_s").reshape(16, 32, D)
    fkp = cat("fk_p").reshape(1, 8, SEQ, 8, 128)
    fvp = cat("fv_p").reshape(1, 8, SEQ, 8, 128)
    flp = cat("fl_p").reshape(1, 8, SEQ, 8)
    mkp = cat("mk_p").reshape(1, 8, 256, 4, 512)
    mvp = cat("mv_p").reshape(1, 8, 256, 4, 512)
    fks = cat("fk_s").reshape(1, 16, 32, 8, 128)
    fvs = cat("fv_s").reshape(1, 16, 32, 8, 128)
    fls = cat("fl_s").reshape(1, 16, 32, 8)
    gvs = cat("gv_s").reshape(1, 16, 32, 1024)
    return (y_prompt, y_sample, fkp, fvp, flp, mkp, mvp, fks, fvs, fls, gvs)
```
